# Optimizing a Trainium2 kernel written in Bass

```python
import jax, jax.numpy as jnp
from jax import lax
import numpy as np

D_MODEL = 1024
BATCH = 8
SEQ = 2048
DEPTH = 4
DEC_BATCH = 128
DEC_SEQ = 4
PAST_LEN = 16384
PAGE_SIZE = 128

EPS = 1e-6
N_A = (DEPTH + 1) // 2
N_B = DEPTH // 2
CHUNK = 128
D_V = 2 * D_MODEL
H_A = 8
HD_A = D_V // H_A
POOL_WINDOWS = (2, 4, 8, 16)
N_POOL_GROUPS = len(POOL_WINDOWS)
GD_B = D_MODEL // N_POOL_GROUPS
POOL_BUF = max(POOL_WINDOWS) - 1
D_FF = 2816
CONV_W = 3

kernel_name = "hybrid_gmlp_pool_convffn_step"


def rmsnorm(x, g):
    xf = x.astype(jnp.float32)
    y = xf * lax.rsqrt(jnp.mean(xf * xf, axis=-1, keepdims=True) + EPS)
    return (y * g.astype(jnp.float32)).astype(x.dtype)


def gmlp_mixer(h, w_in, g_v, w_s, b_s, w_out):
    B, T, _ = h.shape
    z = jax.nn.gelu(jnp.einsum('btd,de->bte', h, w_in))
    u, v = jnp.split(z, 2, axis=-1)
    v = rmsnorm(v, g_v)
    n_chunks = -(-T // CHUNK)
    pad = n_chunks * CHUNK - T
    vp = jnp.pad(v, ((0, 0), (0, pad), (0, 0))).reshape(B, n_chunks, CHUNK, H_A, HD_A)
    mask = jnp.tril(jnp.ones((CHUNK, CHUNK), dtype=w_s.dtype))
    ws = w_s * mask[None]
    s = jnp.einsum('hij,bcjhe->bcihe', ws, vp) + jnp.transpose(b_s)[None, None, :, :, None]
    s = s.reshape(B, n_chunks * CHUNK, D_V)[:, :T]
    y = jnp.einsum('bte,ed->btd', u * s, w_out)
    return y, v


def pool_mixer(h, buf, start_pos, w_pool, scale):
    B, T, D = h.shape
    L = POOL_BUF
    hc = jnp.concatenate([buf.astype(h.dtype), h], axis=1).astype(jnp.float32)
    c = jnp.pad(jnp.cumsum(hc, axis=1), ((0, 0), (1, 0), (0, 0)))
    pos = start_pos + jnp.arange(T)
    end = c[:, L + 1:]
    means = []
    for g, w in enumerate(POOL_WINDOWS):
        sl = slice(g * GD_B, (g + 1) * GD_B)
        begin = c[:, L + 1 - w:L + 1 - w + T, sl]
        cnt = jnp.minimum(w, pos + 1).astype(jnp.float32)[None, :, None]
        means.append((end[..., sl] - begin) / cnt)
    mean = jnp.concatenate(means, axis=-1)
    p = (mean - hc[:, L:]).astype(h.dtype).reshape(B, T, N_POOL_GROUPS, GD_B)
    y = jnp.einsum('btgc,gce->btge', p, w_pool).reshape(B, T, D)
    return y * scale, hc[:, -L:].astype(h.dtype)


def conv_ffn(h, buf, w_gate, w_val, conv_w, conv_b, w_down):
    T = h.shape[1]
    a = jnp.einsum('btd,df->btf', h, w_gate)
    val = jnp.einsum('btd,df->btf', h, w_val)
    ac = jnp.concatenate([buf.astype(a.dtype), a], axis=1)
    conv = conv_b + conv_w[0] * ac[:, 0:T]
    for k in range(1, CONV_W):
        conv = conv + conv_w[k] * ac[:, k:k + T]
    y = jnp.einsum('btf,fd->btd', jax.nn.silu(conv) * val, w_down)
    return y, ac[:, -(CONV_W - 1):]


def trunk(x, start_pos, pool_bufs, conv_bufs, norm_mix, norm_ffn, norm_final,
          w_in_a, g_v_a, w_s_a, b_s_a, w_out_a, w_pool_b, scale_b,
          w_gate, w_val, conv_w, conv_b, w_down):
    v_rows, pool_new, conv_new = [], [], []
    for i in range(DEPTH):
        h = rmsnorm(x, norm_mix[i])
        j = i // 2
        if i % 2 == 0:
            y, v = gmlp_mixer(h, w_in_a[j], g_v_a[j], w_s_a[j], b_s_a[j], w_out_a[j])
            v_rows.append(v)
        else:
            y, pb = pool_mixer(h, pool_bufs[j], start_pos, w_pool_b[j], scale_b[j])
            pool_new.append(pb)
        x = x + y
        h = rmsnorm(x, norm_ffn[i])
        y, cb = conv_ffn(h, conv_bufs[i], w_gate[i], w_val[i], conv_w[i], conv_b[i], w_down[i])
        conv_new.append(cb)
        x = x + y
    return rmsnorm(x, norm_final), jnp.stack(v_rows), jnp.stack(pool_new), jnp.stack(conv_new)


def setup_inputs(seed: int = 0) -> dict:
    key = jax.random.key(seed)
    ks = jax.random.split(key, 20)
    f32 = jnp.float32
    nrm = lambda k, shape, s: jax.random.normal(k, shape, f32) * s
    return {
        "x_prompt": nrm(ks[0], (BATCH, SEQ, D_MODEL), 1.0),
        "x_sample": nrm(ks[1], (DEC_BATCH, DEC_SEQ, D_MODEL), 1.0),
        "state_pool": nrm(ks[2], (N_B, DEC_BATCH, POOL_BUF, D_MODEL), 1.0),
        "state_ffn_conv": nrm(ks[3], (DEPTH, DEC_BATCH, CONV_W - 1, D_FF), 0.5),
        "norm_mix": 1.0 + nrm(ks[4], (DEPTH, D_MODEL), 0.05),
        "norm_ffn": 1.0 + nrm(ks[5], (DEPTH, D_MODEL), 0.05),
        "norm_final": 1.0 + nrm(ks[6], (D_MODEL,), 0.05),
        "w_in_a": nrm(ks[7], (N_A, D_MODEL, 2 * D_V), D_MODEL ** -0.5),
        "g_v_a": 1.0 + nrm(ks[8], (N_A, D_V), 0.05),
        "w_s_a": nrm(ks[9], (N_A, H_A, CHUNK, CHUNK), CHUNK ** -0.5),
        "b_s_a": 1.0 + nrm(ks[10], (N_A, H_A, CHUNK), 0.05),
        "w_out_a": nrm(ks[11], (N_A, D_V, D_MODEL), D_V ** -0.5),
        "w_pool_b": nrm(ks[12], (N_B, N_POOL_GROUPS, GD_B, GD_B), GD_B ** -0.5),
        "scale_b": 1.0 + nrm(ks[13], (N_B, D_MODEL), 0.1),
        "w_gate": nrm(ks[14], (DEPTH, D_MODEL, D_FF), D_MODEL ** -0.5),
        "w_val": nrm(ks[15], (DEPTH, D_MODEL, D_FF), D_MODEL ** -0.5),
        "conv_w": nrm(ks[16], (DEPTH, CONV_W, D_FF), CONV_W ** -0.5),
        "conv_b": nrm(ks[17], (DEPTH, D_FF), 0.01),
        "w_down": nrm(ks[18], (DEPTH, D_FF, D_MODEL), D_FF ** -0.5),
    }


def reference(x_prompt, x_sample, state_pool, state_ffn_conv, norm_mix, norm_ffn, norm_final,
              w_in_a, g_v_a, w_s_a, b_s_a, w_out_a, w_pool_b, scale_b,
              w_gate, w_val, conv_w, conv_b, w_down):
    weights = (norm_mix, norm_ffn, norm_final, w_in_a, g_v_a, w_s_a, b_s_a, w_out_a,
               w_pool_b, scale_b, w_gate, w_val, conv_w, conv_b, w_down)
    pool0 = jnp.zeros((N_B, BATCH, POOL_BUF, D_MODEL), x_prompt.dtype)
    conv0 = jnp.zeros((DEPTH, BATCH, CONV_W - 1, D_FF), x_prompt.dtype)
    y_prompt, _, pool_prompt, ffn_conv_prompt = trunk(x_prompt, 0, pool0, conv0, *weights)
    y_sample, gmlp_v_sample, pool_sample, ffn_conv_sample = trunk(
        x_sample, PAST_LEN, state_pool, state_ffn_conv, *weights)
    return (y_prompt, y_sample, gmlp_v_sample, pool_prompt, pool_sample, ffn_conv_prompt, ffn_conv_sample)
```

```python
import numpy as np
from contextlib import ExitStack
import concourse.bass as bass
import concourse.mybir as mybir
from concourse.bass_utils import run_bass_kernel_spmd

F32 = mybir.dt.float32
BF16 = mybir.dt.bfloat16
AF = mybir.ActivationFunctionType
ALU = mybir.AluOpType

N_CORES = 8
D = 1024
DFF = 2816
DV = 2048
DEPTH = 4
SEQ = 2048
NSEQ = 16
NG = 2
NP = 1024
NS = 32
NT = NP + NS
TT = 352
KC = 8
FC = 22
EC = 16
EPS = 1e-6
SLOT = 4096
NSLOT = 4


class Res:
    def __init__(self, reg, space, lo, hi, name=""):
        self.space, self.lo, self.hi, self.name = space, lo, hi, name
        self.last_write = None
        self.reads = []
        lst = reg.setdefault(space, [])
        self.overl = [self]
        for o in lst:
            if o.lo < hi and lo < o.hi:
                self.overl.append(o)
                o.overl.append(self)
        lst.append(self)


class Tick:
    __slots__ = ("sem", "val", "clock")

    def __init__(self, sem, val, clock):
        self.sem, self.val, self.clock = sem, val, clock


class DmaSem:
    def __init__(self, sem):
        self.sem = sem
        self.count = 0


class Eng:
    def __init__(self, name, handle, sem, self_safe=False):
        self.name, self.h, self.sem, self.self_safe = name, handle, sem, self_safe
        self.count = 0
        self.know = {}
        self.nwaits = 0
        self.nops = 0

    def _wait(self, t):
        key = t.sem.name
        if self.know.get(key, 0) >= t.val:
            return
        self.h.wait_ge(t.sem, t.val)
        self.nwaits += 1
        self.know[key] = t.val
        for k, v in t.clock.items():
            if self.know.get(k, 0) < v:
                self.know[k] = v

    def op(self, fn, reads=(), writes=(), dsem=None):
        best = {}

        def add(t, raw):
            k = t.sem.name
            if k not in best:
                best[k] = [t, raw]
            else:
                if best[k][0].val < t.val:
                    best[k][0] = t
                best[k][1] = best[k][1] or raw
        for r in reads:
            for o in r.overl:
                if o.last_write is not None:
                    add(o.last_write, True)
                if r.space == "ps":
                    for t in o.reads:
                        add(t, False)
        for w in writes:
            for o in w.overl:
                if o.last_write is not None:
                    add(o.last_write, False)
                for t in o.reads:
                    add(t, False)
        for t, raw in best.values():
            if t.sem is self.sem and (self.self_safe or not raw):
                continue
            self._wait(t)
        ins = fn(self.h)
        self.nops += 1
        if dsem is not None:
            dsem.count += 16
            ins.then_inc(dsem.sem, 16)
            tick = Tick(dsem.sem, dsem.count, dict(self.know))
        else:
            self.count += 1
            ins.then_inc(self.sem, 1)
            tick = Tick(self.sem, self.count, dict(self.know))
        for r in reads:
            r.reads.append(tick)
            if len(r.reads) > 48:
                b = {}
                for t in r.reads:
                    k = t.sem.name
                    if k not in b or b[k].val < t.val:
                        b[k] = t
                r.reads = list(b.values())
        for w in writes:
            w.last_write = tick
            w.reads = []
        return tick


def tiles_of(c0, c1):
    return list(range(c0 // TT, (c1 - 1) // TT + 1))


def build_program(cfg=None):
    cfg = cfg or {}
    nlayers = cfg.get("nlayers", DEPTH)
    do_mixer = cfg.get("do_mixer", True)
    do_ffn = cfg.get("do_ffn", True)
    ngroups = cfg.get("ngroups", NG)
    layers = cfg.get("layers", list(range(nlayers)))
    pool_dbg = cfg.get("pool_dbg", 99)
    keep_warm = cfg.get("keep_warm", 0)

    nc = bass.Bass("TRN2", target_bir_lowering=False)
    dr = {}

    def din(name, shape):
        dr[name] = nc.dram_tensor(name, list(shape), F32, kind="ExternalInput").ap()
        return dr[name]

    def dout(name, shape):
        dr[name] = nc.dram_tensor(name, list(shape), F32, kind="ExternalOutput").ap()
        return dr[name]

    x_prompt = din("x_prompt", [SEQ, D])
    x_sample = din("x_sample", [NSEQ * 4, D])
    state_pool = din("state_pool", [2, NSEQ * 15, D])
    state_ffn = din("state_ffn_conv", [DEPTH, NSEQ * 2, DFF])
    norm_mix = din("norm_mix", [DEPTH, D])
    norm_ffn = din("norm_ffn", [DEPTH, D])
    norm_final = din("norm_final", [1, D])
    w_in_a = din("w_in_a", [2, D, 2 * DV])
    g_v_a = din("g_v_a", [2, DV])
    w_s_a = din("w_s_a", [2, 8, 128, 128])
    b_s_a = din("b_s_a", [2, 8 * 128])
    w_out_a = din("w_out_a", [2, DV, D])
    w_pool_b = din("w_pool_b", [2, 1024, 256])
    scale_b = din("scale_b", [2, D])
    w_gate = din("w_gate", [DEPTH, D, DFF])
    w_val = din("w_val", [DEPTH, D, DFF])
    conv_w = din("conv_w", [DEPTH * 3, DFF])
    conv_b = din("conv_b", [DEPTH, DFF])
    w_down = din("w_down", [DEPTH, DFF, D])
    c_ident = din("c_ident", [128, 128])
    c_maskT = din("c_maskT", [128, 128])
    c_icnt = din("c_icnt", [128, 64])

    y_prompt = dout("y_prompt", [SEQ, D])
    y_sample = dout("y_sample", [NSEQ * 4, D])
    gmlp_v = dout("gmlp_v_sample", [2, NSEQ * 4, DV])
    pool_prompt = dout("pool_prompt", [2, 15, D])
    pool_sample = dout("pool_sample", [2, NSEQ * 15, D])
    conv_prompt = dout("ffn_conv_prompt", [DEPTH, 2, DFF])
    conv_sample = dout("ffn_conv_sample", [DEPTH, NSEQ * 2, DFF])

    es = ExitStack()
    with es:
        ARENA_BYTES = 212800
        arena = es.enter_context(nc.sbuf_tensor("arena", [128, ARENA_BYTES // 4], F32))
        base = nc.lookup_mloc(arena).addr
        reg = {}
        cursor = [base]

        def al(x):
            return (x + 31) // 32 * 32

        def esz(dt):
            return 4 if dt == F32 else 2

        class Buf:
            pass

        def alloc_at(name, shape, dt, off):
            nb = int(np.prod(shape[1:])) * esz(dt)
            assert off + nb <= base + ARENA_BYTES, (name, off + nb - base, ARENA_BYTES)
            return nc.alloc_sbuf_tensor_at(name, list(shape), dt, offset=off), nb

        def alloc(name, shape, dt, off=None, pieces=None):
            b = Buf()
            if off is None:
                off = cursor[0]
                adv = True
            else:
                adv = False
            b.t, nb = alloc_at(name, shape, dt, off)
            b.off = off
            b.nb = nb
            if adv:
                cursor[0] = off + (nb + 31) // 32 * 32
            b.r = {}
            if pieces is None:
                b.r[None] = Res(reg, "sb", off, off + nb, name)
                b.all = [b.r[None]]
            else:
                for key, lo, hi in pieces:
                    b.r[key] = Res(reg, "sb", off + lo * esz(dt), off + hi * esz(dt), f"{name}{key}")
                b.all = list(b.r.values())
            return b

        def grid_pieces(nrow, rowlen, ncols=NT, coloff=0):
            ps = []
            for k in range(nrow):
                for tt in range(3):
                    ps.append(((k, tt), k * rowlen + coloff + tt * TT, k * rowlen + coloff + (tt + 1) * TT))
            return ps

        xT = alloc("xT", [128, KC, NT], F32, pieces=grid_pieces(KC, NT))
        ring = [alloc(f"ring{i}", [128, SLOT], BF16) for i in range(NSLOT)]
        ident = alloc("ident", [128, 128], F32)
        maskT = alloc("maskT", [128, 128], F32)
        icnt = alloc("icnt", [128, 4, 16], F32)
        onesb = alloc("onesb", [128, 128], BF16)
        epsb = alloc("epsb", [128, 1], F32)
        pvd = alloc("pvd", [128, KC, 12], F32)
        pvf = alloc("pvf", [128, FC, 16], F32)
        wsT = alloc("wsT", [128, 2, 8, 128], BF16)
        msm = alloc("msm", [32, 2, 8, 32], BF16)
        brow = alloc("brow", [1, 2, 8, 32], BF16)
        brow.all = [Res(reg, "virt", i, i + 1, f"brow{i}") for i in range(16)]
        browM = alloc("browM", [1, EC, 128], BF16)
        browM.all = [Res(reg, "virt", 50 + i, 51 + i, f"browM{i}") for i in range(2)]
        msm.all = [Res(reg, "virt", 100 + i, 101 + i, f"msm{i}") for i in range(16)]
        poolhist = alloc("poolhist", [128, 2, KC, 15], F32, pieces=[((j,), j * KC * 15, (j + 1) * KC * 15) for j in range(2)])
        convhist = alloc("convhist", [128, DEPTH, FC, 2], F32,
                         pieces=[((l, f), (l * FC + f) * 2, (l * FC + f + 1) * 2) for l in range(DEPTH) for f in range(FC)])
        rstd = alloc("rstd", [128, NT], F32, pieces=[((tt,), tt * TT, (tt + 1) * TT) for tt in range(3)])
        sq = alloc("sq", [128, 2, NT], BF16, pieces=[((i, tt), i * NT + tt * TT, i * NT + (tt + 1) * TT) for i in range(2) for tt in range(3)])
        stg = alloc("stg", [128, 2, D], F32, pieces=[((i,), i * D, (i + 1) * D) for i in range(2)])
        stg_sub = [Res(reg, "sb", stg.off + ci * 512, stg.off + (ci + 1) * 512, f"stgsub{ci}") for ci in range(3)]
        vstat = alloc("vstat", [128, 4], F32)
        PH = cursor[0]

        hB = alloc("hB", [128, KC, NT], BF16, off=PH, pieces=grid_pieces(KC, NT))
        o1 = al(PH + hB.nb)
        UB = alloc("UB", [128, EC, NT], BF16, off=o1, pieces=grid_pieces(EC, NT))
        o = al(o1 + UB.nb)
        R2 = alloc("R2", [128, 4, KC, 512], BF16, off=o, pieces=[((n,), n * KC * 512, (n + 1) * KC * 512) for n in range(4)])
        o = al(o + R2.nb)
        vsb = alloc("vsb", [128, DV], F32, off=o); o = al(o + vsb.nb)
        vnb = [None, None]
        vnb[0] = alloc("vn0", [128, DV], BF16, off=o); o = al(o + vnb[0].nb)
        vnb[1] = alloc("vn1", [128, DV], BF16, off=o); o = al(o + vnb[1].nb)
        gvb = alloc("gvb", [128, DV], F32, off=o); o = al(o + gvb.nb)
        junk = alloc("junk", [128, DV], BF16, off=o); o = al(o + junk.nb)
        mT = alloc("mT", [128, FC, NT], BF16, off=o1, pieces=grid_pieces(FC, NT))
        o = al(o1 + mT.nb)
        asb = [None, None]
        for i in range(2):
            asb[i] = alloc(f"asb{i}", [128, 2 + NP], F32, off=o,
                           pieces=[((0,), 0, 2 + TT), ((1,), 2 + TT, 2 + 2 * TT), ((2,), 2 + 2 * TT, 2 + NP)]); o = al(o + asb[i].nb)
        c0b, c2b, gb = [None] * 3, [None] * 3, [None] * 3
        for i in range(3):
            c0b[i] = alloc(f"c0b{i}", [128, TT], F32, off=o); o = al(o + c0b[i].nb)
            c2b[i] = alloc(f"c2b{i}", [128, TT], F32, off=o); o = al(o + c2b[i].nb)
            gb[i] = alloc(f"gb{i}", [128, TT], F32, off=o); o = al(o + gb[i].nb)
        ffn_end = o
        o = (base + ARENA_BYTES - (FC * 48 * 4 + FC * 16 * 4 + 64)) // 32 * 32
        assert o >= ffn_end
        cs_lo = o
        ASb = alloc("ASb", [128, FC, 8, 6], F32, off=o, pieces=[((f,), f * 48, (f + 1) * 48) for f in range(FC)]); o = al(o + ASb.nb)
        cst = alloc("cst", [128, FC, 16], F32, off=o); o = al(o + cst.nb)
        HH = alloc("HH", [128, KC, 15 + NT], F32, off=PH,
                   pieces=[((k, 'h'), k * (15 + NT), k * (15 + NT) + 15) for k in range(KC)] + grid_pieces(KC, 15 + NT, coloff=15))
        o = al(PH + HH.nb)
        pA = [alloc(f"pA{i}", [128, 15 + NP], F32, off=o + i * al((15 + NP) * 4)) for i in range(2)]
        o += 2 * al((15 + NP) * 4)
        PW = 15 + NP + 1
        hbfB = alloc("hbfB", [128, KC, PW], BF16, off=o, pieces=[((k,), k * PW, (k + 1) * PW) for k in range(KC)]); o = al(o + hbfB.nb)
        Sb = [None] * 6
        for i in range(6):
            Sb[i] = alloc(f"Sb{i}", [128, PW], BF16, off=o); o = al(o + Sb[i].nb)
        WAb = alloc("WAb", [128, KC, 256], BF16, off=o); o = al(o + WAb.nb)
        WBb = alloc("WBb", [128, KC, 256], BF16, off=o); o = al(o + WBb.nb)
        pS = alloc("pS", [128, KC, NS], BF16, off=o, pieces=[((k,), k * NS, (k + 1) * NS) for k in range(KC)]); o = al(o + pS.nb)
        pfix = alloc("pfix", [128, KC, 16], BF16, off=o, pieces=[((k,), k * 16, (k + 1) * 16) for k in range(KC)]); o = al(o + pfix.nb)
        fx = [alloc(f"fx{i}", [128, KC, 32], F32, off=o + i * KC * 128) for i in range(2)]; o += 2 * KC * 128
        SAs = [alloc(f"SAs{i}", [128, KC, 8, 19], F32, off=o + i * al(KC * 152 * 4)) for i in range(2)]; o += 2 * al(KC * 152 * 4)
        ptmp8 = alloc("ptmp8", [128, KC, 16], F32, off=o); o = al(o + ptmp8.nb)
        HS = alloc("HS", [128, KC, 8, 19], F32, off=o, pieces=[((k,), k * 152, (k + 1) * 152) for k in range(KC)]); o = al(o + HS.nb)
        sA = [alloc(f"sA{i}", [128, 8, 19], F32, off=o + i * al(152 * 4)) for i in range(2)]
        o += 2 * al(152 * 4)
        ptmp = alloc("ptmp", [128, 16], F32, off=o); o += 64
        HSo = alloc("HSo", [128, KC, 120], F32, off=o); o = al(o + HSo.nb)
        assert o <= cs_lo, (o - PH, cs_lo - PH)

        xstg = alloc("xstg", [128, 2, D], F32, off=al(PH + 36 * 1024), pieces=[((i,), i * D, (i + 1) * D) for i in range(2)])

        psum = es.enter_context(nc.psum_tensor("psum", [128, 4096], F32))
        bank_r = [Res(reg, "ps", b * 2048, (b + 1) * 2048, f"bank{b}") for b in range(8)]

        def bank(b, c0=0, c1=512):
            return psum[:, b * 512 + c0: b * 512 + c1]

        def mksem(n):
            return es.enter_context(nc.semaphore(n))
        es.enter_context(nc.Block())
        PE = Eng("pe", nc.tensor, mksem("s_pe"), self_safe=True)
        ACT = Eng("act", nc.scalar, mksem("s_act"))
        DVE = Eng("dve", nc.vector, mksem("s_dve"))
        POOL = Eng("pool", nc.gpsimd, mksem("s_pool"))
        SP = Eng("sp", nc.sync, mksem("s_sp"))
        dsems = []

        def newdsem(n):
            d = DmaSem(mksem(n))
            dsems.append(d)
            return d
        ring_sem = [newdsem(f"d_ring{i}") for i in range(NSLOT)]
        r2_sem = [newdsem(f"d_r2{n}") for n in range(4)]
        stg_ld = [newdsem(f"d_stgl{i}") for i in range(2)]
        stg_st = [newdsem(f"d_stgs{i}") for i in range(2)]
        misc_ld = newdsem("d_misc")
        xstg_ld = [newdsem(f"d_xstg{i}") for i in range(2)]
        gvb_sem = newdsem("d_gvb")
        small_sem = newdsem("d_small")
        msm_sem = newdsem("d_msm")
        browM_sem = newdsem("d_browM")

        SP.op(lambda e: e.dma_start(out=ident.t[:], in_=c_ident), writes=ident.all, dsem=misc_ld)
        SP.op(lambda e: e.dma_start(out=maskT.t[:], in_=c_maskT), writes=maskT.all, dsem=newdsem("d_misc2"))
        SP.op(lambda e: e.dma_start(out=icnt.t[:], in_=c_icnt.rearrange("p (a b) -> p a b", b=16)), writes=icnt.all, dsem=newdsem("d_misc3"))
        POOL.op(lambda e: e.memset(onesb.t[:], 1.0), writes=onesb.all)
        POOL.op(lambda e: e.memset(epsb.t[:], EPS), writes=epsb.all)
        POOL.op(lambda e: e.memset(msm.t[:], 0.0), writes=msm.all)

        evac_flip = [0]

        def evac_copy(out_ap, in_ap, reads, writes, eng=None):
            evac_flip[0] ^= 1
            if eng is not None:
                evac_flip[0] = eng
            if evac_flip[0]:
                return ACT.op(lambda e: e.activation(out=out_ap, in_=in_ap, func=AF.Copy), reads=reads, writes=writes)
            return DVE.op(lambda e: e.tensor_copy(out=out_ap, in_=in_ap), reads=reads, writes=writes)

        s0 = stg.r[(0,)]
        SP.op(lambda e: e.dma_start(out=stg.t[0:4, 0, :], in_=norm_mix), writes=[s0], dsem=stg_ld[0])
        SP.op(lambda e: e.dma_start(out=stg.t[4:8, 0, :], in_=norm_ffn), writes=[s0], dsem=stg_ld[0])
        SP.op(lambda e: e.dma_start(out=stg.t[8:9, 0, :], in_=norm_final), writes=[s0], dsem=stg_ld[0])
        SP.op(lambda e: e.dma_start(out=stg.t[9:11, 0, :], in_=scale_b), writes=[s0], dsem=stg_ld[0])

        def tr_pvd(e):
            for kc in range(KC):
                i = e.transpose(bank(7, kc * 16, kc * 16 + 11), stg.t[0:11, 0, kc * 128:(kc + 1) * 128], ident.t[0:11, 0:11])
            return i
        PE.op(tr_pvd, reads=[s0] + ident.all, writes=[bank_r[7]])
        DVE.op(lambda e: e.tensor_copy(out=pvd.t[:, :, 0:11], in_=bank(7, 0, 128).rearrange("p (k c) -> p k c", c=16)[:, :, 0:11]),
               reads=[bank_r[7]], writes=pvd.all)
        stream = []

        def unit_cols(w2d, c0, cw, nk):
            return w2d.rearrange("(k p) f -> p k f", p=128)[:, :, c0:c0 + cw]

        def plan_units(g):
            for l in layers:
                j = l // 2
                if do_mixer:
                    if l % 2 == 0:
                        for u in range(4):
                            stream.append(dict(kind="ring", key=("Uu", g, l, u), src=unit_cols(w_in_a[j], u * 512, 512, KC), shp=(KC, 512)))
                        for q in range(4):
                            stream.append(dict(kind="ring", key=("Wo", g, l, q), src=unit_cols(w_out_a[j], q * 256, 256, EC), shp=(EC, 256)))
                            if q == 0:
                                for u in range(4):
                                    stream.append(dict(kind="r2", key=("R2", g, l, u), src=unit_cols(w_in_a[j], DV + u * 512, 512, KC), u=u))
                    else:
                        stream.append(dict(kind="ring", key=("Wp", g, l, 0), src=unit_cols(w_pool_b[j], 0, 256, KC), shp=(KC, 256)))
                if do_ffn:
                    for u in range(6):
                        cw = 512 if u < 5 else 256
                        stream.append(dict(kind="ring", key=("Wg", g, l, u), src=unit_cols(w_gate[l], u * 512, cw, KC), shp=(KC, cw)))
                        stream.append(dict(kind="ring", key=("Wv", g, l, u), src=unit_cols(w_val[l], u * 512, cw, KC), shp=(KC, cw)))
                    for dc in range(KC):
                        stream.append(dict(kind="ring", key=("Wd", g, l, dc), src=unit_cols(w_down[l], dc * 128, 128, FC), shp=(FC, 128)))
        for g in range(ngroups):
            plan_units(g)
        ring_order = [u for u in stream if u["kind"] == "ring"]
        for i, u in enumerate(ring_order):
            u["ridx"] = i
        unit_by_key = {u["key"]: u for u in stream}
        spos = [0]

        def pump():
            while spos[0] < len(stream):
                nx = stream[spos[0]]
                if nx["kind"] == "ring":
                    k = nx["ridx"]
                    if k >= NSLOT and not ring_order[k - NSLOT].get("released"):
                        break
                    sl = k % NSLOT
                    a, b = nx["shp"]
                    dst = ring[sl].t[:, 0:a * b].rearrange("p (a b) -> p a b", b=b)
                    POOL.op(lambda e: e.dma_start(out=dst, in_=nx["src"]), writes=ring[sl].all, dsem=ring_sem[sl])
                    nx["view"] = dst
                    nx["res"] = ring[sl].all
                else:
                    n = nx["u"]
                    POOL.op(lambda e: e.dma_start(out=R2.t[:, n, :, :], in_=nx["src"]), writes=[R2.r[(n,)]], dsem=r2_sem[n])
                nx["issued"] = True
                spos[0] += 1

        def ensure(key):
            pump()
            u = unit_by_key[key]
            assert u.get("issued"), key
            return u

        def release(key):
            unit_by_key[key]["released"] = True
            pump()

        pump()
        def setup_part2():
            for ci, c0 in enumerate(range(0, DFF, 1024)):
                cw = min(1024, DFF - c0)
                si = (ci + 1) % 2
                sr = stg.r[(si,)]
                SP.op(lambda e: e.dma_start(out=stg.t[0:12, si, 0:cw], in_=conv_w[:, c0:c0 + cw]), writes=[sr], dsem=stg_ld[si])
                SP.op(lambda e: e.dma_start(out=stg.t[12:16, si, 0:cw], in_=conv_b[:, c0:c0 + cw]), writes=[sr], dsem=stg_ld[si])
                nf = cw // 128
                bk = 5 + (ci % 2)

                def tr_pvf(e):
                    for j in range(nf):
                        i = e.transpose(bank(bk, j * 16, j * 16 + 16), stg.t[0:16, si, j * 128:(j + 1) * 128], ident.t[0:16, 0:16])
                    return i
                PE.op(tr_pvf, reads=[sr] + ident.all, writes=[bank_r[bk]])
                f0 = c0 // 128
                DVE.op(lambda e: e.tensor_copy(out=pvf.t[:, f0:f0 + nf, :], in_=bank(bk, 0, nf * 16).rearrange("p (k c) -> p k c", c=16)),
                       reads=[bank_r[bk]], writes=pvf.all)
            for l in range(2):
                si = l % 2
                sr = stg.r[(si,)]
                SP.op(lambda e: e.dma_start(out=stg.t[:, si, :].rearrange("p (h j) -> p h j", j=128),
                                            in_=w_s_a[l].rearrange("h i j -> i h j")), writes=[sr], dsem=stg_ld[si])
                for hb in range(2):
                    bk = 5 + hb

                    def tr_ws(e):
                        for j in range(4):
                            hh = hb * 4 + j
                            i = e.transpose(bank(bk, j * 128, (j + 1) * 128), stg.t[:, si, hh * 128:(hh + 1) * 128], ident.t[:])
                        return i
                    PE.op(tr_ws, reads=[sr] + ident.all, writes=[bank_r[bk]])
                    for j in range(4):
                        hh = hb * 4 + j
                        DVE.op(lambda e: e.tensor_tensor(out=wsT.t[:, l, hh, :], in0=bank(bk, j * 128, (j + 1) * 128), in1=maskT.t[:], op=ALU.mult),
                               reads=[bank_r[bk]] + maskT.all, writes=wsT.all)
                for s in range(8):
                    SP.op(lambda e: e.dma_start(out=msm.t[4 * s:4 * s + 4, l, :, 4 * s:4 * s + 4], in_=wsT.t[0:4, l, :, 0:4]),
                          reads=wsT.all, writes=[msm.all[l * 8 + s]], dsem=msm_sem)

            for l in range(2):
                for s in range(8):
                    POOL.op(lambda e: e.dma_start(out=brow.t[0:1, l, :, 4 * s:4 * s + 4],
                                                  in_=b_s_a[l:l + 1, :].rearrange("o (h i) -> o h i", i=128)[:, :, 0:4]),
                            writes=[brow.all[l * 8 + s]], dsem=small_sem)

        setup2_done = [False]

        def ensure_setup2():
            if not setup2_done[0]:
                setup2_done[0] = True
                setup_part2()

        SSQ_BANKS = [0, 1, 2]

        def norm_square(kc, tt=None):
            i = kc % 2
            if tt is None:
                ACT.op(lambda e: e.activation(out=sq.t[:, i, :], in_=xT.t[:, kc, :], func=AF.Square),
                       reads=[xT.r[(kc, t)] for t in range(3)], writes=[sq.r[(i, t)] for t in range(3)])
            else:
                cs = slice(tt * TT, (tt + 1) * TT)
                ACT.op(lambda e: e.activation(out=sq.t[:, i, cs], in_=xT.t[:, kc, cs], func=AF.Square),
                       reads=[xT.r[(kc, tt)]], writes=[sq.r[(i, tt)]])

        def norm_mm(kc, tt=None):
            i = kc % 2
            tts = range(3) if tt is None else [tt]

            def f(e):
                for t in tts:
                    ins = e.matmul(bank(SSQ_BANKS[t], 0, TT), lhsT=onesb.t[:], rhs=sq.t[:, i, t * TT:(t + 1) * TT],
                                   start=(kc == 0), stop=(kc == KC - 1))
                return ins
            PE.op(f, reads=[sq.r[(i, t)] for t in tts] + onesb.all, writes=[bank_r[SSQ_BANKS[t]] for t in tts])

        rstd_ready = [False]

        def norm_finish(gidx, dst, dst_res, dst_f32_off=None, kc_major=False):
            off = 0 if dst_f32_off is None else dst_f32_off
            if not rstd_ready[0]:
                for tt in range(3):
                    norm_rstd(tt)
            rstd_ready[0] = False
            order = [(tt, kc) for tt in range(3) for kc in range(KC)]
            if kc_major:
                order = [(tt, kc) for kc in range(KC) for tt in range(3)]
            for tt, kc in order:
                cs = slice(tt * TT, (tt + 1) * TT)
                ds = slice(off + tt * TT, off + (tt + 1) * TT)
                DVE.op(lambda e: e.scalar_tensor_tensor(out=dst[:, kc, ds], in0=xT.t[:, kc, cs], scalar=pvd.t[:, kc, gidx:gidx + 1],
                                                        in1=rstd.t[:, cs], op0=ALU.mult, op1=ALU.mult),
                       reads=[xT.r[(kc, tt)], rstd.r[(tt,)]] + pvd.all, writes=[dst_res[(kc, tt)]])

        def full_norm_stats():
            for kc in range(KC):
                norm_square(kc)
                norm_mm(kc)

        def norm_rstd(tt):
            cs = slice(tt * TT, (tt + 1) * TT)
            ACT.op(lambda e: e.activation(out=rstd.t[:, cs], in_=bank(SSQ_BANKS[tt], 0, TT), func=AF.Ln, bias=epsb.t[:, 0:1], scale=1.0 / D),
                   reads=[bank_r[SSQ_BANKS[tt]]] + epsb.all, writes=[rstd.r[(tt,)]])
            ACT.op(lambda e: e.activation(out=rstd.t[:, cs], in_=rstd.t[:, cs], func=AF.Exp, scale=-0.5),
                   reads=[rstd.r[(tt,)]], writes=[rstd.r[(tt,)]])

        class FinalPhase:
            def __init__(self):
                self.pend = None
                self.rstd_done = False

            def chunk(self, dc, mm_emit, evac_emit, hooks=None):
                last = (dc == KC - 1)
                for tt in range(3):
                    if hooks and tt in hooks:
                        hooks[tt]()
                    bk = ACC_BANKS[acc_flip[0]]
                    acc_flip[0] ^= 1
                    mm_emit(dc, tt, bk)
                    if self.pend is not None and tt == 0:
                        norm_mm(self.pend)
                        self.pend = None
                    evac_emit(dc, tt, bk)
                    if last:
                        norm_square(dc, tt)
                        if tt >= 1:
                            norm_mm(dc, tt - 1)
                            norm_rstd(tt - 1)
                if last:
                    norm_mm(dc, 2)
                    norm_rstd(2)
                    rstd_ready[0] = True
                    if keep_warm:
                        def warm(e):
                            for _ in range(keep_warm):
                                i = e.matmul(bank(ACC_BANKS[0], 0, TT), lhsT=onesb.t[:], rhs=sq.t[:, 0, 0:TT], start=True, stop=True)
                            return i
                        PE.op(warm, reads=[sq.r[(0, 0)]] + onesb.all, writes=[bank_r[ACC_BANKS[0]]])
                else:
                    norm_square(dc)
                    self.pend = dc

        ACC_BANKS = [3, 4]
        acc_flip = [0]

        def load_x_dma(g, c):
            p0, sr0 = g * NP, g * NS
            si = c % 2
            sres = xstg.r[(si,)]
            if c < 8:
                SP.op(lambda e: e.dma_start(out=xstg.t[:, si, :], in_=x_prompt[p0 + c * 128: p0 + (c + 1) * 128, :]), writes=[sres], dsem=xstg_ld[si])
            else:
                SP.op(lambda e: e.dma_start(out=xstg.t[0:NS, si, :], in_=x_sample[sr0:sr0 + NS, :]), writes=[sres], dsem=xstg_ld[si])

        def load_x_tr(g, c):
            si = c % 2
            sres = xstg.r[(si,)]
            ntok = 128 if c < 8 else NS
            col0 = c * 128
            for hb in range(2):
                bk = [7, 4][hb]

                def trx(e):
                    for j in range(4):
                        kc = hb * 4 + j
                        i = e.transpose(bank(bk, j * 128, j * 128 + ntok), xstg.t[0:ntok, si, kc * 128:(kc + 1) * 128], ident.t[0:ntok, 0:ntok])
                    return i
                PE.op(trx, reads=[sres] + ident.all, writes=[bank_r[bk]])
                wr = [xT.r[(hb * 4 + j, tt)] for j in range(4) for tt in tiles_of(col0, col0 + ntok)]
                evac_copy(xT.t[:, hb * 4:hb * 4 + 4, col0:col0 + ntok],
                          bank(bk).rearrange("p (k c) -> p k c", c=128)[:, :, 0:ntok], [bank_r[bk]], wr)

        def store_y_chunk(g, c):
            p0, sr0 = g * NP, g * NS
            si = c % 2
            sres = stg.r[(si,)]
            ntok = 128 if c < 8 else NS
            col0 = c * 128
            for hb in range(2):
                bk = 5 + hb

                def try_(e):
                    for jj in range(4):
                        kc = hb * 4 + jj
                        i = e.transpose(bank(bk, jj * 128, (jj + 1) * 128)[0:ntok, :], HH.t[:, kc, 15 + col0:15 + col0 + ntok], ident.t[:])
                    return i
                PE.op(try_, reads=[HH.r[(hb * 4 + jj, tt)] for jj in range(4) for tt in tiles_of(col0, col0 + ntok)] + ident.all, writes=[bank_r[bk]])
                evac_copy(stg.t[0:ntok, si, hb * 512:(hb + 1) * 512], bank(bk)[0:ntok, :], [bank_r[bk]], [sres])

        def store_y_dma(g, c):
            p0, sr0 = g * NP, g * NS
            si = c % 2
            sres = stg.r[(si,)]
            if c < 8:
                SP.op(lambda e: e.dma_start(out=y_prompt[p0 + c * 128:p0 + (c + 1) * 128, :], in_=stg.t[:, si, :]), reads=[sres], dsem=stg_st[si])
            else:
                SP.op(lambda e: e.dma_start(out=y_sample[sr0:sr0 + NS, :], in_=stg.t[0:NS, si, :]), reads=[sres], dsem=stg_st[si])

        for g in range(ngroups):
            p0 = g * NP
            sr0 = g * NS
            sq0 = g * 8

            if g == 0:
                load_x_dma(0, 0)
                for c in range(9):
                    if c + 1 < 9:
                        load_x_dma(0, c + 1)
                    load_x_tr(0, c)
            full_norm_stats()

            pool_state_loaded = [False]

            def load_pool_state_dma(j):
                SP.op(lambda e: e.dma_start(out=stg.t[0:120, 0, :], in_=state_pool[j, sq0 * 15:(sq0 + 8) * 15, :]), writes=[stg.r[(0,)]], dsem=stg_ld[0])
                pool_state_loaded[0] = True

            def load_conv_state(l, eng=None):
                for ci, c0 in enumerate(range(0, DFF, 1024)):
                    cw = min(1024, DFF - c0)
                    nf = cw // 128
                    si = ci % 2
                    SP.op(lambda e: e.dma_start(out=stg.t[0:16, si, 0:cw], in_=state_ffn[l, sq0 * 2:(sq0 + 8) * 2, c0:c0 + cw]),
                          writes=[stg.r[(si,)]], dsem=stg_ld[si])
                    bk = 5 + si

                    def trst(e):
                        for jj in range(nf):
                            i = e.transpose(bank(bk, jj * 16, jj * 16 + 16), stg.t[0:16, si, jj * 128:(jj + 1) * 128], ident.t[0:16, 0:16])
                        return i
                    PE.op(trst, reads=[stg.r[(si,)]] + ident.all, writes=[bank_r[bk]])
                    f0 = c0 // 128
                    evac_copy(ASb.t[:, f0:f0 + nf, :, 0:2], bank(bk, 0, nf * 16).rearrange("p (f s r) -> p f s r", s=8, r=2),
                              [bank_r[bk]], [ASb.r[(f,)] for f in range(f0, f0 + nf)], eng=eng)

            for l in layers:
                j = l // 2
                if do_mixer and l % 2 == 0:
                    gmlp_layer = True
                else:
                    gmlp_layer = False
                if do_mixer and gmlp_layer:
                    norm_finish(l, hB.t, hB.r)
                    SP.op(lambda e: e.dma_start(out=gvb.t[:], in_=g_v_a[j].partition_broadcast(128)), writes=gvb.all, dsem=gvb_sem)
                    for r in range(2):
                        POOL.op(lambda e: e.dma_start(out=browM.t[0:1, :, :].rearrange("o (h r) i -> o h r i", r=2)[:, :, r, :],
                                                      in_=b_s_a[j:j + 1, :].rearrange("o (h i) -> o h i", i=128)),
                                writes=[browM.all[r]], dsem=browM_sem)
                    ubanks = [5, 6, 7, 3]
                    step = 0
                    for ec in range(EC):
                        if ec == 4:
                            ensure_setup2()
                        u = ensure(("Uu", g, l, ec // 4))
                        wv = u["view"]
                        for tt in range(3):
                            bk = ubanks[step % 4]
                            step += 1
                            cs = slice(tt * TT, (tt + 1) * TT)

                            def mmu(e):
                                for kc in range(KC):
                                    i = e.matmul(bank(bk, 0, TT), lhsT=wv[:, kc, (ec % 4) * 128:(ec % 4 + 1) * 128], rhs=hB.t[:, kc, cs],
                                                 start=(kc == 0), stop=(kc == KC - 1))
                                return i
                            PE.op(mmu, reads=u["res"] + [hB.r[(kc, tt)] for kc in range(KC)], writes=[bank_r[bk]])
                            ACT.op(lambda e: e.activation(out=UB.t[:, ec, cs], in_=bank(bk, 0, TT), func=AF.Gelu_apprx_tanh),
                                   reads=[bank_r[bk]], writes=[UB.r[(ec, tt)]])
                        if ec % 4 == 3:
                            release(("Uu", g, l, ec // 4))
                    def v_mm(c):
                        ntok = 128 if c < 8 else NS
                        col0 = c * 128
                        for half in range(2):
                            def f(e):
                                for n in (2 * half, 2 * half + 1):
                                    for kc in range(KC):
                                        i = e.matmul(psum[0:ntok, n * 512:(n + 1) * 512], lhsT=hB.t[:, kc, col0:col0 + ntok],
                                                     rhs=R2.t[:, n, kc, :], start=(kc == 0), stop=(kc == KC - 1))
                                return i
                            PE.op(f, reads=[R2.r[(2 * half,)], R2.r[(2 * half + 1,)]] + [hB.r[(kc, tt)] for kc in range(KC) for tt in tiles_of(col0, col0 + ntok)],
                                  writes=bank_r[2 * half:2 * half + 2])
                            hs_ = slice(half * 1024, (half + 1) * 1024)
                            ACT.op(lambda e: e.activation(out=vsb.t[0:ntok, hs_], in_=psum[0:ntok, hs_], func=AF.Gelu_apprx_tanh),
                                   reads=bank_r[2 * half:2 * half + 2], writes=vsb.all)

                    def v_elem(c):
                        ntok = 128 if c < 8 else NS
                        vb = vnb[c % 2]
                        ACT.op(lambda e: e.activation(out=junk.t[0:ntok, :], in_=vsb.t[0:ntok, :], func=AF.Square, accum_out=vstat.t[0:ntok, 0:1]),
                               reads=vsb.all, writes=junk.all + vstat.all)
                        ACT.op(lambda e: e.activation(out=vstat.t[0:ntok, 1:2], in_=vstat.t[0:ntok, 0:1], func=AF.Sqrt, bias=epsb.t[0:ntok, 0:1], scale=1.0 / DV),
                               reads=vstat.all + epsb.all, writes=vstat.all)
                        DVE.op(lambda e: e.reciprocal(out=vstat.t[0:ntok, 2:3], in_=vstat.t[0:ntok, 1:2]), reads=vstat.all, writes=vstat.all)
                        DVE.op(lambda e: e.scalar_tensor_tensor(out=vb.t[0:ntok, :], in0=vsb.t[0:ntok, :], scalar=vstat.t[0:ntok, 2:3], in1=gvb.t[0:ntok, :],
                                                                op0=ALU.mult, op1=ALU.mult),
                               reads=vsb.all + vstat.all + gvb.all, writes=vb.all)
                        if c == 8:
                            so = stg.t[0:NS, :, :].rearrange("p a b -> p (a b)")
                            DVE.op(lambda e: e.scalar_tensor_tensor(out=so, in0=vsb.t[0:NS, :], scalar=vstat.t[0:NS, 2:3], in1=gvb.t[0:NS, :],
                                                                    op0=ALU.mult, op1=ALU.mult),
                                   reads=vsb.all + vstat.all + gvb.all, writes=stg.all)
                            SP.op(lambda e: e.dma_start(out=gmlp_v[j, sr0:sr0 + NS, :], in_=so), reads=stg.all, dsem=stg_st[0])

                    def s_mm(c):
                        ntok = 128 if c < 8 else NS
                        vb = vnb[c % 2]

                        def f(e):
                            if c < 8:
                                for n in range(4):
                                    e.matmul(psum[:, DV + n * 512: DV + (n + 1) * 512], lhsT=onesb.t[0:1, 0:128],
                                             rhs=browM.t[0:1, 4 * n:4 * n + 4, :].rearrange("o a b -> o (a b)"), start=True, stop=False, skip_group_check=True)
                                for ec in range(EC):
                                    i = e.matmul(psum[:, DV + ec * 128: DV + (ec + 1) * 128], lhsT=vb.t[:, ec * 128:(ec + 1) * 128],
                                                 rhs=wsT.t[:, j, ec // 2, :], start=False, stop=True, skip_group_check=True)
                                return i
                            for ec in range(EC):
                                hh = ec // 2
                                o_ap = bank(7, ec * NS, (ec + 1) * NS)
                                e.matmul(o_ap, lhsT=vb.t[0:ntok, ec * 128:(ec + 1) * 128], rhs=msm.t[0:NS, j, hh, :], start=True, stop=False)
                                i = e.matmul(o_ap, lhsT=onesb.t[0:1, 0:128], rhs=brow.t[0:1, j, hh, :], start=False, stop=True)
                            return i
                        PE.op(f, reads=vb.all + wsT.all + msm.all + brow.all + browM.all + onesb.all, writes=(bank_r[4:8] if c < 8 else [bank_r[7]]))

                    def s_elem(c):
                        ntok = 128 if c < 8 else NS
                        col0 = c * 128
                        urs = [UB.r[(ec, tt)] for ec in range(EC) for tt in tiles_of(col0, col0 + ntok)]
                        if c < 8:
                            s_in, s_rd = psum[:, DV:2 * DV].rearrange("p (a b) -> p a b", b=128), bank_r[4:8]
                        else:
                            s_in, s_rd = bank(7).rearrange("p (a b) -> p a b", b=NS), [bank_r[7]]
                        DVE.op(lambda e: e.tensor_tensor(out=UB.t[:, :, col0:col0 + ntok], in0=s_in, in1=UB.t[:, :, col0:col0 + ntok], op=ALU.mult),
                               reads=s_rd + urs, writes=urs)
                    v_mm(0)
                    v_elem(0)
                    for c in range(1, 9):
                        v_mm(c)
                        s_mm(c - 1)
                        s_elem(c - 1)
                        v_elem(c)
                    fp = FinalPhase()
                    for dc in range(KC):
                        u = ensure(("Wo", g, l, dc // 2))
                        wv = u["view"]

                        def mm_emit(dc, tt, bk):
                            cs = slice(tt * TT, (tt + 1) * TT)

                            def mmo(e):
                                for ec in range(EC):
                                    i = e.matmul(bank(bk, 0, TT), lhsT=wv[:, ec, (dc % 2) * 128:(dc % 2 + 1) * 128], rhs=UB.t[:, ec, cs],
                                                 start=(ec == 0), stop=(ec == EC - 1))
                                return i
                            PE.op(mmo, reads=u["res"] + [UB.r[(ec, tt)] for ec in range(EC)], writes=[bank_r[bk]])

                        def evac_emit(dc, tt, bk):
                            cs = slice(tt * TT, (tt + 1) * TT)
                            DVE.op(lambda e: e.tensor_tensor(out=xT.t[:, dc, cs], in0=bank(bk, 0, TT), in1=xT.t[:, dc, cs], op=ALU.add),
                                   reads=[bank_r[bk], xT.r[(dc, tt)]], writes=[xT.r[(dc, tt)]])
                        hk = None
                        if dc == 0:
                            hk = {2: lambda: (s_mm(8), s_elem(8))}
                        elif dc == 2 and do_ffn:
                            hk = {0: lambda: load_conv_state(l)}
                        fp.chunk(dc, mm_emit, evac_emit, hooks=hk)
                        if dc % 2 == 1:
                            release(("Wo", g, l, dc // 2))
                elif do_mixer:
                    ensure_setup2()
                    if g == 0:
                        POOL.op(lambda e: e.memset(HH.t[:, :, 0:15], 0.0), writes=[HH.r[(k, 'h')] for k in range(KC)])
                    else:
                        DVE.op(lambda e: e.tensor_copy(out=HH.t[:, :, 0:15], in_=poolhist.t[:, j, :, :]),
                               reads=[poolhist.r[(j,)]], writes=[HH.r[(k, 'h')] for k in range(KC)])
                    norm_finish(l, HH.t, HH.r, dst_f32_off=15, kc_major=True)
                    wp = ensure(("Wp", g, l, 0))
                    wpv = wp["view"]
                    coefA = [-0.5, 0.25, 0.125, 0.0625]
                    coefB = [0.5, -1.0, -1.0, -1.0]
                    for gi in range(4):
                        ACT.op(lambda e: e.activation(out=WAb.t[:, 2 * gi:2 * gi + 2, :], in_=wpv[:, 2 * gi:2 * gi + 2, :], func=AF.Copy, scale=coefA[gi]),
                               reads=wp["res"], writes=WAb.all)
                        ACT.op(lambda e: e.activation(out=WBb.t[:, 2 * gi:2 * gi + 2, :], in_=wpv[:, 2 * gi:2 * gi + 2, :], func=AF.Copy, scale=coefB[gi]),
                               reads=wp["res"], writes=WBb.all)

                    def cast_h(kc):
                        ACT.op(lambda e: e.activation(out=hbfB.t[:, kc, 0:15 + NP], in_=HH.t[:, kc, 0:15 + NP], func=AF.Copy),
                               reads=[HH.r[(kc, 'h')]] + [HH.r[(kc, tt)] for tt in range(3)], writes=[hbfB.r[(kc,)]])
                    cast_h(0)
                    cast_h(1)
                    if not pool_state_loaded[0]:
                        load_pool_state_dma(j)
                    pool_state_loaded[0] = False
                    for hb in range(2):
                        bk = 5 + hb

                        def trs(e):
                            for jj in range(4):
                                kc = hb * 4 + jj
                                i = e.transpose(bank(bk, jj * 128, jj * 128 + 120), stg.t[0:120, 0, kc * 128:(kc + 1) * 128], ident.t[0:120, 0:120])
                            return i
                        PE.op(trs, reads=[stg.r[(0,)]] + ident.all, writes=[bank_r[bk]])
                        for jj in range(4):
                            kc = hb * 4 + jj
                            evac_copy(HS.t[:, kc, :, 0:15], bank(bk, jj * 128, jj * 128 + 120).rearrange("p (s r) -> p s r", r=15),
                                      [bank_r[bk]], [HS.r[(kc,)]], eng=1)
                    for kc in range(2, KC):
                        cast_h(kc)
                    pfp = FinalPhase()
                    fix0 = 16 if g == 0 else 0

                    def pool_terms(gi, k):
                        hsrc = (hbfB.t[:, k, :], [hbfB.r[(k,)]])
                        if gi == 0:
                            return [(hsrc, 0, WAb), (hsrc, 1, WBb)]
                        sb = Sb[k - 2]
                        ssrc = (sb.t[:, :], sb.all)
                        half = 2 ** gi
                        return [(ssrc, 0, WAb), (ssrc, half, WAb), (hsrc, 0, WBb)]

                    def pool_mm(ec):
                        gi = ec // 2
                        eo = (ec % 2) * 128

                        def mm_emit(ec, tt, bk):
                            c0 = tt * TT + (fix0 if tt == 0 else 0)
                            c1 = min((tt + 1) * TT, NP)
                            rds = list(wp["res"]) + WAb.all + WBb.all
                            plan = []
                            grp = []
                            for cc in range(2):
                                k = gi * 2 + cc
                                for (src, srcres), sh, W in pool_terms(gi, k):
                                    grp.append((W.t[:, k, eo:eo + 128], src[:, 15 + c0 - sh:15 + c1 - sh]))
                                    rds += srcres
                            plan.append((bank(bk, c0 - tt * TT, c1 - tt * TT), grp))
                            if tt == 0 and fix0:
                                plan.append((bank(bk, 0, fix0), [(wpv[:, gi * 2 + cc, eo:eo + 128], pfix.t[:, gi * 2 + cc, :]) for cc in range(2)]))
                                rds += [pfix.r[(gi * 2 + cc,)] for cc in range(2)]
                            if tt == 2:
                                plan.append((bank(bk, NP - 2 * TT, TT), [(wpv[:, gi * 2 + cc, eo:eo + 128], pS.t[:, gi * 2 + cc, :]) for cc in range(2)]))
                                rds += [pS.r[(gi * 2 + cc,)] for cc in range(2)]

                            def mmp(e):
                                for o_ap, lst in plan:
                                    for n, (lt, rh) in enumerate(lst):
                                        i = e.matmul(o_ap, lhsT=lt, rhs=rh, start=(n == 0), stop=(n == len(lst) - 1))
                                return i
                            PE.op(mmp, reads=rds, writes=[bank_r[bk]])

                        def evac_emit(ec, tt, bk):
                            cs = slice(tt * TT, (tt + 1) * TT)
                            DVE.op(lambda e: e.scalar_tensor_tensor(out=xT.t[:, ec, cs], in0=bank(bk, 0, TT), scalar=pvd.t[:, ec, 9 + j:10 + j], in1=xT.t[:, ec, cs],
                                                                    op0=ALU.mult, op1=ALU.add),
                                   reads=[bank_r[bk], xT.r[(ec, tt)]] + pvd.all, writes=[xT.r[(ec, tt)]])
                        pfp.chunk(ec, mm_emit, evac_emit)
                    allHH = [HH.r[(k, 'h')] for k in range(KC)] + [HH.r[(k, tt)] for k in range(KC) for tt in range(3)]
                    DVE.op(lambda e: e.tensor_copy(out=HS.t[:, :, :, 15:19], in_=HH.t[:, :, 15 + NP:15 + NT].rearrange("p k (s t) -> p k s t", t=4)),
                           reads=[HH.r[(k, 2)] for k in range(KC)], writes=HS.all)
                    DVE.op(lambda e: e.tensor_copy(out=poolhist.t[:, j, :, :], in_=HH.t[:, :, NP:NP + 15]),
                           reads=[HH.r[(k, 2)] for k in range(KC)], writes=[poolhist.r[(j,)]])
                    for st in range(4):
                        k0 = 2 * st
                        sh = 2 ** st
                        lo = 2 ** (st + 1) - 1
                        w = 2 ** (st + 1)
                        if st == 0:
                            sa, sar = HS.t, HS.all
                            fa, far = HH.t, allHH
                        else:
                            sa, sar = SAs[(st - 1) % 2].t, SAs[(st - 1) % 2].all
                            fa, far = fx[(st - 1) % 2].t, fx[(st - 1) % 2].all
                        sd = SAs[st % 2]
                        DVE.op(lambda e: e.tensor_tensor(out=sd.t[:, k0:KC, :, lo:19], in0=sa[:, k0:KC, :, lo:19], in1=sa[:, k0:KC, :, lo - sh:19 - sh], op=ALU.add),
                               reads=sar, writes=sd.all)
                        DVE.op(lambda e: e.scalar_tensor_tensor(out=pS.t[:, k0:k0 + 2, :].rearrange("p k (s t) -> p k s t", t=4), in0=sd.t[:, k0:k0 + 2, :, 15:19],
                                                                scalar=1.0 / w, in1=HS.t[:, k0:k0 + 2, :, 15:19], op0=ALU.mult, op1=ALU.subtract),
                               reads=sd.all + HS.all, writes=[pS.r[(k0,)], pS.r[(k0 + 1,)]])
                        if fix0:
                            fd = fx[st % 2]
                            DVE.op(lambda e: e.tensor_tensor(out=fd.t[:, k0:KC, lo:31], in0=fa[:, k0:KC, lo:31], in1=fa[:, k0:KC, lo - sh:31 - sh], op=ALU.add),
                                   reads=far, writes=fd.all)
                            for kc in (k0, k0 + 1):
                                DVE.op(lambda e: e.tensor_tensor(out=ptmp8.t[:, kc, :], in0=fd.t[:, kc, 15:31], in1=icnt.t[:, st, :], op=ALU.mult),
                                       reads=fd.all + icnt.all, writes=ptmp8.all)
                                DVE.op(lambda e: e.tensor_tensor(out=pfix.t[:, kc, :], in0=ptmp8.t[:, kc, :], in1=HH.t[:, kc, 15:31], op=ALU.subtract),
                                       reads=ptmp8.all + [HH.r[(kc, 0)]], writes=[pfix.r[(kc,)]])
                    def s_adds(kc):
                        gi = kc // 2
                        src = HH.t[:, kc, 0:15 + NP]
                        srcr = [HH.r[(kc, 'h')]] + [HH.r[(kc, tt)] for tt in range(3)]
                        sb = Sb[kc - 2]
                        cur, curr = src, srcr
                        for st in range(gi):
                            sh = 2 ** st
                            lo = 2 ** (st + 1) - 1
                            dstb = sb if st == gi - 1 else pA[st % 2]
                            a, b = cur, dstb.t
                            DVE.op(lambda e: e.tensor_tensor(out=b[:, lo:15 + NP], in0=a[:, lo:15 + NP], in1=a[:, lo - sh:15 + NP - sh], op=ALU.add),
                                   reads=curr, writes=dstb.all)
                            cur, curr = dstb.t, dstb.all
                    if pool_dbg >= 2:
                        s_adds(2)
                        s_adds(3)
                        pool_mm(0)
                        pool_mm(1)
                        s_adds(4)
                        s_adds(5)
                        pool_mm(2)
                        pool_mm(3)
                        s_adds(6)
                        s_adds(7)
                        pool_mm(4)
                        pool_mm(5)
                        if do_ffn:
                            load_conv_state(l, eng=1)
                        pool_mm(6)
                        pool_mm(7)
                    release(("Wp", g, l, 0))
                    DVE.op(lambda e: e.tensor_copy(out=HSo.t[:, :, :].rearrange("p k (s r) -> p k s r", r=15), in_=HS.t[:, :, :, 4:19]),
                           reads=HS.all, writes=HSo.all)
                    for hb in range(2 if pool_dbg >= 3 else 0):
                        bk = 5 + hb

                        def trps(e):
                            for jj in range(4):
                                kc = hb * 4 + jj
                                i = e.transpose(bank(bk, jj * 128, (jj + 1) * 128)[0:120, :], HSo.t[:, kc, :], ident.t[:])
                            return i
                        PE.op(trps, reads=HSo.all + ident.all, writes=[bank_r[bk]])
                        evac_copy(stg.t[0:120, 1, hb * 512:(hb + 1) * 512], bank(bk)[0:120, :], [bank_r[bk]], [stg.r[(1,)]])
                    if pool_dbg >= 3:
                        SP.op(lambda e: e.dma_start(out=pool_sample[j, sq0 * 15:(sq0 + 8) * 15, :], in_=stg.t[0:120, 1, :]), reads=[stg.r[(1,)]], dsem=stg_st[1])
                    if g == ngroups - 1 and pool_dbg >= 4:
                        for hb in range(2):
                            bk = 5 + hb

                            def trpp(e):
                                for jj in range(4):
                                    kc = hb * 4 + jj
                                    i = e.transpose(bank(bk, jj * 128, (jj + 1) * 128)[0:15, :], poolhist.t[:, j, kc, :], ident.t[:])
                                return i
                            PE.op(trpp, reads=[poolhist.r[(j,)]] + ident.all, writes=[bank_r[bk]])
                            evac_copy(stg.t[0:15, 0, hb * 512:(hb + 1) * 512], bank(bk)[0:15, :], [bank_r[bk]], [stg.r[(0,)]])
                        SP.op(lambda e: e.dma_start(out=pool_prompt[j, :, :], in_=stg.t[0:15, 0, :]), reads=[stg.r[(0,)]], dsem=stg_st[0])
                if do_ffn:
                    ensure_setup2()
                    if not (do_mixer):
                        load_conv_state(l)
                    norm_finish(4 + l, hB.t, hB.r)
                    gbanks = [0, 1, 2]
                    vbanks = [5, 6, 7]
                    steps = [(fc, tt) for fc in range(FC) for tt in range(3)]

                    def stage1(i):
                        fc, tt = steps[i]
                        ug = ensure(("Wg", g, l, fc // 4))
                        uv = ensure(("Wv", g, l, fc // 4))
                        gbk, vbk = gbanks[i % 3], vbanks[i % 3]
                        cs = slice(tt * TT, (tt + 1) * TT)
                        fo = (fc % 4) * 128
                        hr = [hB.r[(kc, tt)] for kc in range(KC)]

                        def mmg(e):
                            for kc in range(KC):
                                ins = e.matmul(bank(gbk, 0, TT), lhsT=ug["view"][:, kc, fo:fo + 128], rhs=hB.t[:, kc, cs], start=(kc == 0), stop=(kc == KC - 1))
                            return ins
                        PE.op(mmg, reads=ug["res"] + hr, writes=[bank_r[gbk]])

                        def mmv(e):
                            for kc in range(KC):
                                ins = e.matmul(bank(vbk, 0, TT), lhsT=uv["view"][:, kc, fo:fo + 128], rhs=hB.t[:, kc, cs], start=(kc == 0), stop=(kc == KC - 1))
                            return ins
                        PE.op(mmv, reads=uv["res"] + hr, writes=[bank_r[vbk]])
                        ab = asb[fc % 2]
                        abw = [ab.r[(tt,)]]
                        abr = [ab.r[(tt,)]] + ([ab.r[(tt - 1,)]] if tt > 0 else [])
                        npr = TT if tt < 2 else NP - 2 * TT
                        pc0 = tt * TT
                        if tt == 0:
                            if g == 0:
                                DVE.op(lambda e: e.memset(ab.t[:, 0:2], 0.0), writes=abw)
                            else:
                                ACT.op(lambda e: e.activation(out=ab.t[:, 0:2], in_=convhist.t[:, l, fc, :], func=AF.Copy),
                                       reads=[convhist.r[(l, fc)]], writes=abw)
                        ACT.op(lambda e: e.activation(out=ab.t[:, 2 + pc0:2 + pc0 + npr], in_=bank(gbk, 0, npr), func=AF.Copy),
                               reads=[bank_r[gbk]], writes=abw)
                        if tt == 2:
                            ACT.op(lambda e: e.activation(out=ASb.t[:, fc, :, 2:6], in_=bank(gbk, npr, TT).rearrange("p (s t) -> p s t", t=4), func=AF.Copy),
                                   reads=[bank_r[gbk]], writes=[ASb.r[(fc,)]])
                        cb = c0b[i % 3]
                        ACT.op(lambda e: e.activation(out=cb.t[:, :], in_=bank(gbk, 0, TT), func=AF.Identity,
                                                      bias=pvf.t[:, fc, 12 + l:13 + l], scale=pvf.t[:, fc, 3 * l + 2:3 * l + 3]),
                               reads=[bank_r[gbk]] + pvf.all, writes=cb.all)
                        c2 = c2b[i % 3]
                        DVE.op(lambda e: e.scalar_tensor_tensor(out=c2.t[:, 0:npr], in0=ab.t[:, 1 + pc0:1 + pc0 + npr], scalar=pvf.t[:, fc, 3 * l + 1:3 * l + 2],
                                                                in1=cb.t[:, 0:npr], op0=ALU.mult, op1=ALU.add),
                               reads=abr + cb.all + pvf.all, writes=c2.all)
                        DVE.op(lambda e: e.scalar_tensor_tensor(out=c2.t[:, 0:npr], in0=ab.t[:, pc0:pc0 + npr], scalar=pvf.t[:, fc, 3 * l:3 * l + 1],
                                                                in1=c2.t[:, 0:npr], op0=ALU.mult, op1=ALU.add),
                               reads=abr + c2.all + pvf.all, writes=c2.all)
                        if tt == 2:
                            v3 = lambda ap: ap.rearrange("p (s t) -> p s t", t=4)
                            DVE.op(lambda e: e.scalar_tensor_tensor(out=v3(c2.t[:, npr:TT]), in0=ASb.t[:, fc, :, 1:5], scalar=pvf.t[:, fc, 3 * l + 1:3 * l + 2],
                                                                    in1=v3(cb.t[:, npr:TT]), op0=ALU.mult, op1=ALU.add),
                                   reads=[ASb.r[(fc,)]] + cb.all + pvf.all, writes=c2.all)
                            DVE.op(lambda e: e.scalar_tensor_tensor(out=v3(c2.t[:, npr:TT]), in0=ASb.t[:, fc, :, 0:4], scalar=pvf.t[:, fc, 3 * l:3 * l + 1],
                                                                    in1=v3(c2.t[:, npr:TT]), op0=ALU.mult, op1=ALU.add),
                                   reads=[ASb.r[(fc,)]] + c2.all + pvf.all, writes=c2.all)
                            DVE.op(lambda e: e.tensor_copy(out=convhist.t[:, l, fc, :], in_=ab.t[:, NP:NP + 2]),
                                   reads=abw, writes=[convhist.r[(l, fc)]])

                        if tt == 2 and (fc % 4 == 3 or fc == FC - 1):
                            release(("Wg", g, l, fc // 4))
                            release(("Wv", g, l, fc // 4))

                    def stage2(i):
                        fc, tt = steps[i]
                        vbk = vbanks[i % 3]
                        cs = slice(tt * TT, (tt + 1) * TT)
                        c2 = c2b[i % 3]
                        gg = gb[i % 3]
                        ACT.op(lambda e: e.activation(out=gg.t[:, :], in_=c2.t[:, :], func=AF.Silu), reads=c2.all, writes=gg.all)
                        DVE.op(lambda e: e.tensor_tensor(out=mT.t[:, fc, cs], in0=gg.t[:, :], in1=bank(vbk, 0, TT), op=ALU.mult),
                               reads=gg.all + [bank_r[vbk]], writes=[mT.r[(fc, tt)]])
                    for i in range(len(steps)):
                        stage1(i)
                        if i > 0:
                            stage2(i - 1)
                    stage2(len(steps) - 1)
                    DVE.op(lambda e: e.tensor_copy(out=cst.t[:, :, :].rearrange("p f (s r) -> p f s r", r=2), in_=ASb.t[:, :, :, 4:6]),
                           reads=ASb.all, writes=cst.all)
                    for ci, f0 in enumerate(range(0, FC, 8)):
                        nf = min(8, FC - f0)
                        bk = 5 + (ci % 2)
                        PE.op(lambda e: e.transpose(bank(bk, 0, 128)[0:nf * 16, :], cst.t[:, f0:f0 + nf, :].rearrange("p f c -> p (f c)"), ident.t[:]),
                              reads=cst.all + ident.all, writes=[bank_r[bk]])
                        evac_copy(stg.t[0:nf * 16, 0, ci * 128:(ci + 1) * 128], bank(bk, 0, 128)[0:nf * 16, :], [bank_r[bk]], [stg_sub[ci]])
                        for fl in range(nf):
                            fc = f0 + fl
                            SP.op(lambda e: e.dma_start(out=conv_sample[l, sq0 * 2:(sq0 + 8) * 2, fc * 128:(fc + 1) * 128],
                                                        in_=stg.t[fl * 16:(fl + 1) * 16, 0, ci * 128:(ci + 1) * 128]),
                                  reads=[stg_sub[ci]], dsem=stg_st[0])
                    if g == ngroups - 1:
                        PE.op(lambda e: e.transpose(bank(7, 0, 128)[0:2 * FC, :], convhist.t[:, l, :, :].rearrange("p f c -> p (f c)"), ident.t[:]),
                              reads=[convhist.r[(l, f)] for f in range(FC)] + ident.all, writes=[bank_r[7]])
                        evac_copy(stg.t[0:2 * FC, 1, 0:128], bank(7, 0, 128)[0:2 * FC, :], [bank_r[7]], [stg.r[(1,)]])
                        for fc in range(FC):
                            SP.op(lambda e: e.dma_start(out=conv_prompt[l, :, fc * 128:(fc + 1) * 128], in_=stg.t[2 * fc:2 * fc + 2, 1, 0:128]),
                                  reads=[stg.r[(1,)]], dsem=stg_st[1])
                    if do_mixer and (l + 1) in layers and (l + 1) % 2 == 1:
                        load_pool_state_dma((l + 1) // 2)
                    fp = FinalPhase()
                    for dc in range(KC):
                        u = ensure(("Wd", g, l, dc))
                        wv = u["view"]

                        def mm_emit(dc, tt, bk):
                            cs = slice(tt * TT, (tt + 1) * TT)

                            def mmd(e):
                                for fc in range(FC):
                                    i = e.matmul(bank(bk, 0, TT), lhsT=wv[:, fc, :], rhs=mT.t[:, fc, cs], start=(fc == 0), stop=(fc == FC - 1))
                                return i
                            PE.op(mmd, reads=u["res"] + [mT.r[(fc, tt)] for fc in range(FC)], writes=[bank_r[bk]])

                        def evac_emit(dc, tt, bk):
                            cs = slice(tt * TT, (tt + 1) * TT)
                            DVE.op(lambda e: e.tensor_tensor(out=xT.t[:, dc, cs], in0=bank(bk, 0, TT), in1=xT.t[:, dc, cs], op=ALU.add),
                                   reads=[bank_r[bk], xT.r[(dc, tt)]], writes=[xT.r[(dc, tt)]])
                        fp.chunk(dc, mm_emit, evac_emit)
                        release(("Wd", g, l, dc))

            norm_finish(8, HH.t, HH.r, dst_f32_off=15)
            nxt = g + 1 < ngroups
            if nxt:
                load_x_dma(g + 1, 0)
                load_x_dma(g + 1, 1)
            for c in range(9):
                store_y_chunk(g, c)
                if nxt:
                    load_x_tr(g + 1, c)
                store_y_dma(g, c)
                if nxt and c + 2 < 9:
                    load_x_dma(g + 1, c + 2)

        for d in dsems:
            if d.count > 0:
                SP.h.wait_ge(d.sem, d.count)
        nc._stats = {e.name: (e.nops, e.nwaits) for e in (PE, ACT, DVE, POOL, SP)}
    return nc


def _consts():
    ident = np.eye(128, dtype=np.float32)
    maskT = np.triu(np.ones((128, 128), dtype=np.float32))
    icnt = np.zeros((4, 16), dtype=np.float32)
    for gi, w in enumerate((2, 4, 8, 16)):
        for p in range(16):
            icnt[gi, p] = 1.0 / min(w, p + 1)
    icnt = np.broadcast_to(icnt.reshape(1, 64), (128, 64)).copy()
    return ident, maskT, icnt


def make_in_maps(inputs, n_cores=N_CORES):
    f = lambda a: np.ascontiguousarray(np.asarray(a, dtype=np.float32))
    ident, maskT, icnt = _consts()
    shared = {
        "norm_mix": f(inputs["norm_mix"]), "norm_ffn": f(inputs["norm_ffn"]),
        "norm_final": f(inputs["norm_final"]).reshape(1, D),
        "w_in_a": f(inputs["w_in_a"]), "g_v_a": f(inputs["g_v_a"]), "w_s_a": f(inputs["w_s_a"]),
        "b_s_a": f(inputs["b_s_a"]).reshape(2, 8 * 128), "w_out_a": f(inputs["w_out_a"]),
        "w_pool_b": f(inputs["w_pool_b"]).reshape(2, 1024, 256), "scale_b": f(inputs["scale_b"]),
        "w_gate": f(inputs["w_gate"]), "w_val": f(inputs["w_val"]),
        "conv_w": f(inputs["conv_w"]).reshape(DEPTH * 3, DFF), "conv_b": f(inputs["conv_b"]),
        "w_down": f(inputs["w_down"]),
        "c_ident": ident, "c_maskT": maskT, "c_icnt": icnt,
    }
    xp = f(inputs["x_prompt"]); xs = f(inputs["x_sample"])
    sp = f(inputs["state_pool"]); sf = f(inputs["state_ffn_conv"])
    maps = []
    for c in range(n_cores):
        m = dict(shared)
        m["x_prompt"] = xp[c]
        m["x_sample"] = np.ascontiguousarray(xs[c * NSEQ:(c + 1) * NSEQ].reshape(NSEQ * 4, D))
        m["state_pool"] = np.ascontiguousarray(sp[:, c * NSEQ:(c + 1) * NSEQ].reshape(2, NSEQ * 15, D))
        m["state_ffn_conv"] = np.ascontiguousarray(sf[:, c * NSEQ:(c + 1) * NSEQ].reshape(DEPTH, NSEQ * 2, DFF))
        maps.append(m)
    return maps


def assemble(results, n_cores=N_CORES):
    g = lambda k: [np.asarray(r[k], dtype=np.float32) for r in results]
    y_prompt = np.stack(g("y_prompt"), axis=0)
    y_sample = np.concatenate([a.reshape(NSEQ, 4, D) for a in g("y_sample")], axis=0)
    gv = np.concatenate([a.reshape(2, NSEQ, 4, DV) for a in g("gmlp_v_sample")], axis=1)
    pp = np.stack(g("pool_prompt"), axis=1)
    ps = np.concatenate([a.reshape(2, NSEQ, 15, D) for a in g("pool_sample")], axis=1)
    cp = np.stack(g("ffn_conv_prompt"), axis=1)
    cs = np.concatenate([a.reshape(DEPTH, NSEQ, 2, DFF) for a in g("ffn_conv_sample")], axis=1)
    return (y_prompt, y_sample, gv, pp, ps, cp, cs)


def kernel(**inputs):
    nc = build_program()
    maps = make_in_maps(inputs)
    res = run_bass_kernel_spmd(nc, maps, core_ids=list(range(N_CORES)))
    return assemble(res.results)
```

```python
import numpy as np
from contextlib import ExitStack
import concourse.bass as bass
import concourse.mybir as mybir
from concourse.bass_utils import run_bass_kernel_spmd

F32 = mybir.dt.float32
BF16 = mybir.dt.bfloat16
AF = mybir.ActivationFunctionType
ALU = mybir.AluOpType

N_CORES = 8
D = 1024
DFF = 2816
DV = 2048
DEPTH = 4
SEQ = 2048
NSEQ = 16
NG = 2
NP = 1024
NS = 32
NT = NP + NS
TT = 352
KC = 8
FC = 22
EC = 16
EPS = 1e-6
SLOT = 4096
NSLOT = 4


class Res:
    def __init__(self, reg, space, lo, hi, name=""):
        self.space, self.lo, self.hi, self.name = space, lo, hi, name
        self.last_write = None
        self.reads = []
        lst = reg.setdefault(space, [])
        self.overl = [self]
        for o in lst:
            if o.lo < hi and lo < o.hi:
                self.overl.append(o)
                o.overl.append(self)
        lst.append(self)


class Tick:
    __slots__ = ("sem", "val", "clock")

    def __init__(self, sem, val, clock):
        self.sem, self.val, self.clock = sem, val, clock


class DmaSem:
    def __init__(self, sem):
        self.sem = sem
        self.count = 0


class Eng:
    def __init__(self, name, handle, sem, self_safe=False):
        self.name, self.h, self.sem, self.self_safe = name, handle, sem, self_safe
        self.count = 0
        self.know = {}
        self.nwaits = 0
        self.nops = 0

    def _wait(self, t):
        key = t.sem.name
        if self.know.get(key, 0) >= t.val:
            return
        self.h.wait_ge(t.sem, t.val)
        self.nwaits += 1
        self.know[key] = t.val
        for k, v in t.clock.items():
            if self.know.get(k, 0) < v:
                self.know[k] = v

    def op(self, fn, reads=(), writes=(), dsem=None):
        best = {}

        def add(t, raw):
            k = t.sem.name
            if k not in best:
                best[k] = [t, raw]
            else:
                if best[k][0].val < t.val:
                    best[k][0] = t
                best[k][1] = best[k][1] or raw
        for r in reads:
            for o in r.overl:
                if o.last_write is not None:
                    add(o.last_write, True)
                if r.space == "ps":
                    for t in o.reads:
                        add(t, False)
        for w in writes:
            for o in w.overl:
                if o.last_write is not None:
                    add(o.last_write, False)
                for t in o.reads:
                    add(t, False)
        for t, raw in best.values():
            if t.sem is self.sem and (self.self_safe or not raw):
                continue
            self._wait(t)
        ins = fn(self.h)
        self.nops += 1
        if dsem is not None:
            dsem.count += 16
            ins.then_inc(dsem.sem, 16)
            tick = Tick(dsem.sem, dsem.count, dict(self.know))
        else:
            self.count += 1
            ins.then_inc(self.sem, 1)
            tick = Tick(self.sem, self.count, dict(self.know))
        for r in reads:
            r.reads.append(tick)
            if len(r.reads) > 48:
                b = {}
                for t in r.reads:
                    k = t.sem.name
                    if k not in b or b[k].val < t.val:
                        b[k] = t
                r.reads = list(b.values())
        for w in writes:
            w.last_write = tick
            w.reads = []
        return tick


def tiles_of(c0, c1):
    return list(range(c0 // TT, (c1 - 1) // TT + 1))


def build_program(cfg=None):
    cfg = cfg or {}
    nlayers = cfg.get("nlayers", DEPTH)
    do_mixer = cfg.get("do_mixer", True)
    do_ffn = cfg.get("do_ffn", True)
    ngroups = cfg.get("ngroups", NG)
    layers = cfg.get("layers", list(range(nlayers)))
    pool_dbg = cfg.get("pool_dbg", 99)
    keep_warm = cfg.get("keep_warm", 0)

    nc = bass.Bass("TRN2", target_bir_lowering=False)
    dr = {}

    def din(name, shape):
        dr[name] = nc.dram_tensor(name, list(shape), F32, kind="ExternalInput").ap()
        return dr[name]

    def dout(name, shape):
        dr[name] = nc.dram_tensor(name, list(shape), F32, kind="ExternalOutput").ap()
        return dr[name]

    x_prompt = din("x_prompt", [SEQ, D])
    x_sample = din("x_sample", [NSEQ * 4, D])
    state_pool = din("state_pool", [2, NSEQ * 15, D])
    state_ffn = din("state_ffn_conv", [DEPTH, NSEQ * 2, DFF])
    norm_mix = din("norm_mix", [DEPTH, D])
    norm_ffn = din("norm_ffn", [DEPTH, D])
    norm_final = din("norm_final", [1, D])
    w_in_a = din("w_in_a", [2, D, 2 * DV])
    g_v_a = din("g_v_a", [2, DV])
    w_s_a = din("w_s_a", [2, 8, 128, 128])
    b_s_a = din("b_s_a", [2, 8 * 128])
    w_out_a = din("w_out_a", [2, DV, D])
    w_pool_b = din("w_pool_b", [2, 1024, 256])
    scale_b = din("scale_b", [2, D])
    w_gate = din("w_gate", [DEPTH, D, DFF])
    w_val = din("w_val", [DEPTH, D, DFF])
    conv_w = din("conv_w", [DEPTH * 3, DFF])
    conv_b = din("conv_b", [DEPTH, DFF])
    w_down = din("w_down", [DEPTH, DFF, D])
    c_ident = din("c_ident", [128, 128])
    c_maskT = din("c_maskT", [128, 128])
    c_icnt = din("c_icnt", [128, 64])

    y_prompt = dout("y_prompt", [SEQ, D])
    y_sample = dout("y_sample", [NSEQ * 4, D])
    gmlp_v = dout("gmlp_v_sample", [2, NSEQ * 4, DV])
    pool_prompt = dout("pool_prompt", [2, 15, D])
    pool_sample = dout("pool_sample", [2, NSEQ * 15, D])
    conv_prompt = dout("ffn_conv_prompt", [DEPTH, 2, DFF])
    conv_sample = dout("ffn_conv_sample", [DEPTH, NSEQ * 2, DFF])

    es = ExitStack()
    with es:
        ARENA_BYTES = 212800
        arena = es.enter_context(nc.sbuf_tensor("arena", [128, ARENA_BYTES // 4], F32))
        base = nc.lookup_mloc(arena).addr
        reg = {}
        cursor = [base]

        def al(x):
            return (x + 31) // 32 * 32

        def esz(dt):
            return 4 if dt == F32 else 2

        class Buf:
            pass

        def alloc_at(name, shape, dt, off):
            nb = int(np.prod(shape[1:])) * esz(dt)
            assert off + nb <= base + ARENA_BYTES, (name, off + nb - base, ARENA_BYTES)
            return nc.alloc_sbuf_tensor_at(name, list(shape), dt, offset=off), nb

        def alloc(name, shape, dt, off=None, pieces=None):
            b = Buf()
            if off is None:
                off = cursor[0]
                adv = True
            else:
                adv = False
            b.t, nb = alloc_at(name, shape, dt, off)
            b.off = off
            b.nb = nb
            if adv:
                cursor[0] = off + (nb + 31) // 32 * 32
            b.r = {}
            if pieces is None:
                b.r[None] = Res(reg, "sb", off, off + nb, name)
                b.all = [b.r[None]]
            else:
                for key, lo, hi in pieces:
                    b.r[key] = Res(reg, "sb", off + lo * esz(dt), off + hi * esz(dt), f"{name}{key}")
                b.all = list(b.r.values())
            return b

        def grid_pieces(nrow, rowlen, ncols=NT, coloff=0):
            ps = []
            for k in range(nrow):
                for tt in range(3):
                    ps.append(((k, tt), k * rowlen + coloff + tt * TT, k * rowlen + coloff + (tt + 1) * TT))
            return ps

        xT = alloc("xT", [128, KC, NT], F32, pieces=grid_pieces(KC, NT))
        ring = [alloc(f"ring{i}", [128, SLOT], BF16) for i in range(NSLOT)]
        ident = alloc("ident", [128, 128], F32)
        maskT = alloc("maskT", [128, 128], F32)
        icnt = alloc("icnt", [128, 4, 16], F32)
        onesb = alloc("onesb", [128, 128], BF16)
        epsb = alloc("epsb", [128, 1], F32)
        pvd = alloc("pvd", [128, KC, 12], F32)
        pvf = alloc("pvf", [128, FC, 16], F32)
        wsT = alloc("wsT", [128, 2, 8, 128], BF16)
        msm = alloc("msm", [32, 2, 8, 32], BF16)
        brow = alloc("brow", [1, 2, 8, 32], BF16)
        brow.all = [Res(reg, "virt", i, i + 1, f"brow{i}") for i in range(16)]
        browM = alloc("browM", [1, EC, 128], BF16)
        browM.all = [Res(reg, "virt", 50 + i, 51 + i, f"browM{i}") for i in range(2)]
        msm.all = [Res(reg, "virt", 100 + i, 101 + i, f"msm{i}") for i in range(16)]
        poolhist = alloc("poolhist", [128, 2, KC, 15], F32, pieces=[((j,), j * KC * 15, (j + 1) * KC * 15) for j in range(2)])
        convhist = alloc("convhist", [128, DEPTH, FC, 2], F32,
                         pieces=[((l, f), (l * FC + f) * 2, (l * FC + f + 1) * 2) for l in range(DEPTH) for f in range(FC)])
        rstd = alloc("rstd", [128, NT], F32, pieces=[((tt,), tt * TT, (tt + 1) * TT) for tt in range(3)])
        sq = alloc("sq", [128, 2, NT], BF16, pieces=[((i, tt), i * NT + tt * TT, i * NT + (tt + 1) * TT) for i in range(2) for tt in range(3)])
        stg = alloc("stg", [128, 2, D], F32, pieces=[((i,), i * D, (i + 1) * D) for i in range(2)])
        stg_sub = [Res(reg, "sb", stg.off + ci * 512, stg.off + (ci + 1) * 512, f"stgsub{ci}") for ci in range(3)]
        vstat = alloc("vstat", [128, 4], F32)
        PH = cursor[0]

        hB = alloc("hB", [128, KC, NT], BF16, off=PH, pieces=grid_pieces(KC, NT))
        o1 = al(PH + hB.nb)
        UB = alloc("UB", [128, EC, NT], BF16, off=o1, pieces=grid_pieces(EC, NT))
        o = al(o1 + UB.nb)
        R2 = alloc("R2", [128, 4, KC, 512], BF16, off=o, pieces=[((n,), n * KC * 512, (n + 1) * KC * 512) for n in range(4)])
        o = al(o + R2.nb)
        vsb = alloc("vsb", [128, DV], F32, off=o); o = al(o + vsb.nb)
        vnb = [None, None]
        vnb[0] = alloc("vn0", [128, DV], BF16, off=o); o = al(o + vnb[0].nb)
        vnb[1] = alloc("vn1", [128, DV], BF16, off=o); o = al(o + vnb[1].nb)
        gvb = alloc("gvb", [128, DV], F32, off=o); o = al(o + gvb.nb)
        junk = alloc("junk", [128, DV], BF16, off=o); o = al(o + junk.nb)
        mT = alloc("mT", [128, FC, NT], BF16, off=o1, pieces=grid_pieces(FC, NT))
        o = al(o1 + mT.nb)
        asb = [None, None]
        for i in range(2):
            asb[i] = alloc(f"asb{i}", [128, 2 + NP], F32, off=o,
                           pieces=[((0,), 0, 2 + TT), ((1,), 2 + TT, 2 + 2 * TT), ((2,), 2 + 2 * TT, 2 + NP)]); o = al(o + asb[i].nb)
        c0b, c2b, gb = [None] * 3, [None] * 3, [None] * 3
        for i in range(3):
            c0b[i] = alloc(f"c0b{i}", [128, TT], F32, off=o); o = al(o + c0b[i].nb)
            c2b[i] = alloc(f"c2b{i}", [128, TT], F32, off=o); o = al(o + c2b[i].nb)
            gb[i] = alloc(f"gb{i}", [128, TT], F32, off=o); o = al(o + gb[i].nb)
        ffn_end = o
        o = (base + ARENA_BYTES - (FC * 48 * 4 + FC * 16 * 4 + 64)) // 32 * 32
        assert o >= ffn_end
        cs_lo = o
        ASb = alloc("ASb", [128, FC, 8, 6], F32, off=o, pieces=[((f,), f * 48, (f + 1) * 48) for f in range(FC)]); o = al(o + ASb.nb)
        cst = alloc("cst", [128, FC, 16], F32, off=o); o = al(o + cst.nb)
        HH = alloc("HH", [128, KC, 15 + NT], F32, off=PH,
                   pieces=[((k, 'h'), k * (15 + NT), k * (15 + NT) + 15) for k in range(KC)] + grid_pieces(KC, 15 + NT, coloff=15))
        o = al(PH + HH.nb)
        pA = [alloc(f"pA{i}", [128, 15 + NP], F32, off=o + i * al((15 + NP) * 4)) for i in range(2)]
        o += 2 * al((15 + NP) * 4)
        PW = 15 + NP + 1
        hbfB = alloc("hbfB", [128, KC, PW], BF16, off=o, pieces=[((k,), k * PW, (k + 1) * PW) for k in range(KC)]); o = al(o + hbfB.nb)
        Sb = [None] * 6
        for i in range(6):
            Sb[i] = alloc(f"Sb{i}", [128, PW], BF16, off=o); o = al(o + Sb[i].nb)
        WAb = alloc("WAb", [128, KC, 256], BF16, off=o); o = al(o + WAb.nb)
        WBb = alloc("WBb", [128, KC, 256], BF16, off=o); o = al(o + WBb.nb)
        pS = alloc("pS", [128, KC, NS], BF16, off=o, pieces=[((k,), k * NS, (k + 1) * NS) for k in range(KC)]); o = al(o + pS.nb)
        pfix = alloc("pfix", [128, KC, 16], BF16, off=o, pieces=[((k,), k * 16, (k + 1) * 16) for k in range(KC)]); o = al(o + pfix.nb)
        fx = [alloc(f"fx{i}", [128, KC, 32], F32, off=o + i * KC * 128) for i in range(2)]; o += 2 * KC * 128
        SAs = [alloc(f"SAs{i}", [128, KC, 8, 19], F32, off=o + i * al(KC * 152 * 4)) for i in range(2)]; o += 2 * al(KC * 152 * 4)
        ptmp8 = alloc("ptmp8", [128, KC, 16], F32, off=o); o = al(o + ptmp8.nb)
        HS = alloc("HS", [128, KC, 8, 19], F32, off=o, pieces=[((k,), k * 152, (k + 1) * 152) for k in range(KC)]); o = al(o + HS.nb)
        sA = [alloc(f"sA{i}", [128, 8, 19], F32, off=o + i * al(152 * 4)) for i in range(2)]
        o += 2 * al(152 * 4)
        ptmp = alloc("ptmp", [128, 16], F32, off=o); o += 64
        HSo = alloc("HSo", [128, KC, 120], F32, off=o); o = al(o + HSo.nb)
        assert o <= cs_lo, (o - PH, cs_lo - PH)

        xstg = alloc("xstg", [128, 2, D], F32, off=al(PH + 36 * 1024), pieces=[((i,), i * D, (i + 1) * D) for i in range(2)])

        psum = es.enter_context(nc.psum_tensor("psum", [128, 4096], F32))
        bank_r = [Res(reg, "ps", b * 2048, (b + 1) * 2048, f"bank{b}") for b in range(8)]

        def bank(b, c0=0, c1=512):
            return psum[:, b * 512 + c0: b * 512 + c1]

        def mksem(n):
            return es.enter_context(nc.semaphore(n))
        es.enter_context(nc.Block())
        PE = Eng("pe", nc.tensor, mksem("s_pe"), self_safe=True)
        ACT = Eng("act", nc.scalar, mksem("s_act"))
        DVE = Eng("dve", nc.vector, mksem("s_dve"))
        POOL = Eng("pool", nc.gpsimd, mksem("s_pool"))
        SP = Eng("sp", nc.sync, mksem("s_sp"))
        dsems = []

        def newdsem(n):
            d = DmaSem(mksem(n))
            dsems.append(d)
            return d
        ring_sem = [newdsem(f"d_ring{i}") for i in range(NSLOT)]
        r2_sem = [newdsem(f"d_r2{n}") for n in range(4)]
        stg_ld = [newdsem(f"d_stgl{i}") for i in range(2)]
        stg_st = [newdsem(f"d_stgs{i}") for i in range(2)]
        misc_ld = newdsem("d_misc")
        xstg_ld = [newdsem(f"d_xstg{i}") for i in range(2)]
        gvb_sem = newdsem("d_gvb")
        small_sem = newdsem("d_small")
        msm_sem = newdsem("d_msm")
        browM_sem = newdsem("d_browM")

        SP.op(lambda e: e.dma_start(out=ident.t[:], in_=c_ident), writes=ident.all, dsem=misc_ld)
        SP.op(lambda e: e.dma_start(out=maskT.t[:], in_=c_maskT), writes=maskT.all, dsem=newdsem("d_misc2"))
        SP.op(lambda e: e.dma_start(out=icnt.t[:], in_=c_icnt.rearrange("p (a b) -> p a b", b=16)), writes=icnt.all, dsem=newdsem("d_misc3"))
        POOL.op(lambda e: e.memset(onesb.t[:], 1.0), writes=onesb.all)
        POOL.op(lambda e: e.memset(epsb.t[:], EPS), writes=epsb.all)
        POOL.op(lambda e: e.memset(msm.t[:], 0.0), writes=msm.all)

        evac_flip = [0]

        def evac_copy(out_ap, in_ap, reads, writes, eng=None):
            evac_flip[0] ^= 1
            if eng is not None:
                evac_flip[0] = eng
            if evac_flip[0]:
                return ACT.op(lambda e: e.activation(out=out_ap, in_=in_ap, func=AF.Copy), reads=reads, writes=writes)
            return DVE.op(lambda e: e.tensor_copy(out=out_ap, in_=in_ap), reads=reads, writes=writes)

        s0 = stg.r[(0,)]
        SP.op(lambda e: e.dma_start(out=stg.t[0:4, 0, :], in_=norm_mix), writes=[s0], dsem=stg_ld[0])
        SP.op(lambda e: e.dma_start(out=stg.t[4:8, 0, :], in_=norm_ffn), writes=[s0], dsem=stg_ld[0])
        SP.op(lambda e: e.dma_start(out=stg.t[8:9, 0, :], in_=norm_final), writes=[s0], dsem=stg_ld[0])
        SP.op(lambda e: e.dma_start(out=stg.t[9:11, 0, :], in_=scale_b), writes=[s0], dsem=stg_ld[0])

        def tr_pvd(e):
            for kc in range(KC):
                i = e.transpose(bank(7, kc * 16, kc * 16 + 11), stg.t[0:11, 0, kc * 128:(kc + 1) * 128], ident.t[0:11, 0:11])
            return i
        PE.op(tr_pvd, reads=[s0] + ident.all, writes=[bank_r[7]])
        DVE.op(lambda e: e.tensor_copy(out=pvd.t[:, :, 0:11], in_=bank(7, 0, 128).rearrange("p (k c) -> p k c", c=16)[:, :, 0:11]),
               reads=[bank_r[7]], writes=pvd.all)
        stream = []

        def unit_cols(w2d, c0, cw, nk):
            return w2d.rearrange("(k p) f -> p k f", p=128)[:, :, c0:c0 + cw]

        def plan_units(g):
            for l in layers:
                j = l // 2
                if do_mixer:
                    if l % 2 == 0:
                        for u in range(4):
                            stream.append(dict(kind="ring", key=("Uu", g, l, u), src=unit_cols(w_in_a[j], u * 512, 512, KC), shp=(KC, 512)))
                        for q in range(4):
                            stream.append(dict(kind="ring", key=("Wo", g, l, q), src=unit_cols(w_out_a[j], q * 256, 256, EC), shp=(EC, 256)))
                            if q == 0:
                                for u in range(4):
                                    stream.append(dict(kind="r2", key=("R2", g, l, u), src=unit_cols(w_in_a[j], DV + u * 512, 512, KC), u=u))
                    else:
                        stream.append(dict(kind="ring", key=("Wp", g, l, 0), src=unit_cols(w_pool_b[j], 0, 256, KC), shp=(KC, 256)))
                if do_ffn:
                    for u in range(6):
                        cw = 512 if u < 5 else 256
                        stream.append(dict(kind="ring", key=("Wg", g, l, u), src=unit_cols(w_gate[l], u * 512, cw, KC), shp=(KC, cw)))
                        stream.append(dict(kind="ring", key=("Wv", g, l, u), src=unit_cols(w_val[l], u * 512, cw, KC), shp=(KC, cw)))
                    for dc in range(KC):
                        stream.append(dict(kind="ring", key=("Wd", g, l, dc), src=unit_cols(w_down[l], dc * 128, 128, FC), shp=(FC, 128)))
        for g in range(ngroups):
            plan_units(g)
        ring_order = [u for u in stream if u["kind"] == "ring"]
        for i, u in enumerate(ring_order):
            u["ridx"] = i
        unit_by_key = {u["key"]: u for u in stream}
        spos = [0]

        def pump():
            while spos[0] < len(stream):
                nx = stream[spos[0]]
                if nx["kind"] == "ring":
                    k = nx["ridx"]
                    if k >= NSLOT and not ring_order[k - NSLOT].get("released"):
                        break
                    sl = k % NSLOT
                    a, b = nx["shp"]
                    dst = ring[sl].t[:, 0:a * b].rearrange("p (a b) -> p a b", b=b)
                    POOL.op(lambda e: e.dma_start(out=dst, in_=nx["src"]), writes=ring[sl].all, dsem=ring_sem[sl])
                    nx["view"] = dst
                    nx["res"] = ring[sl].all
                else:
                    n = nx["u"]
                    POOL.op(lambda e: e.dma_start(out=R2.t[:, n, :, :], in_=nx["src"]), writes=[R2.r[(n,)]], dsem=r2_sem[n])
                nx["issued"] = True
                spos[0] += 1

        def ensure(key):
            pump()
            u = unit_by_key[key]
            assert u.get("issued"), key
            return u

        def release(key):
            unit_by_key[key]["released"] = True
            pump()

        pump()
        def setup_part2():
            for ci, c0 in enumerate(range(0, DFF, 1024)):
                cw = min(1024, DFF - c0)
                si = (ci + 1) % 2
                sr = stg.r[(si,)]
                SP.op(lambda e: e.dma_start(out=stg.t[0:12, si, 0:cw], in_=conv_w[:, c0:c0 + cw]), writes=[sr], dsem=stg_ld[si])
                SP.op(lambda e: e.dma_start(out=stg.t[12:16, si, 0:cw], in_=conv_b[:, c0:c0 + cw]), writes=[sr], dsem=stg_ld[si])
                nf = cw // 128
                bk = 5 + (ci % 2)

                def tr_pvf(e):
                    for j in range(nf):
                        i = e.transpose(bank(bk, j * 16, j * 16 + 16), stg.t[0:16, si, j * 128:(j + 1) * 128], ident.t[0:16, 0:16])
                    return i
                PE.op(tr_pvf, reads=[sr] + ident.all, writes=[bank_r[bk]])
                f0 = c0 // 128
                DVE.op(lambda e: e.tensor_copy(out=pvf.t[:, f0:f0 + nf, :], in_=bank(bk, 0, nf * 16).rearrange("p (k c) -> p k c", c=16)),
                       reads=[bank_r[bk]], writes=pvf.all)
            for l in range(2):
                si = l % 2
                sr = stg.r[(si,)]
                SP.op(lambda e: e.dma_start(out=stg.t[:, si, :].rearrange("p (h j) -> p h j", j=128),
                                            in_=w_s_a[l].rearrange("h i j -> i h j")), writes=[sr], dsem=stg_ld[si])
                for hb in range(2):
                    bk = 5 + hb

                    def tr_ws(e):
                        for j in range(4):
                            hh = hb * 4 + j
                            i = e.transpose(bank(bk, j * 128, (j + 1) * 128), stg.t[:, si, hh * 128:(hh + 1) * 128], ident.t[:])
                        return i
                    PE.op(tr_ws, reads=[sr] + ident.all, writes=[bank_r[bk]])
                    for j in range(4):
                        hh = hb * 4 + j
                        DVE.op(lambda e: e.tensor_tensor(out=wsT.t[:, l, hh, :], in0=bank(bk, j * 128, (j + 1) * 128), in1=maskT.t[:], op=ALU.mult),
                               reads=[bank_r[bk]] + maskT.all, writes=wsT.all)
                for s in range(8):
                    SP.op(lambda e: e.dma_start(out=msm.t[4 * s:4 * s + 4, l, :, 4 * s:4 * s + 4], in_=wsT.t[0:4, l, :, 0:4]),
                          reads=wsT.all, writes=[msm.all[l * 8 + s]], dsem=msm_sem)

            for l in range(2):
                for s in range(8):
                    POOL.op(lambda e: e.dma_start(out=brow.t[0:1, l, :, 4 * s:4 * s + 4],
                                                  in_=b_s_a[l:l + 1, :].rearrange("o (h i) -> o h i", i=128)[:, :, 0:4]),
                            writes=[brow.all[l * 8 + s]], dsem=small_sem)

        SSQ_BANKS = [0, 1, 2]

        def norm_square(kc, tt=None):
            i = kc % 2
            if tt is None:
                ACT.op(lambda e: e.activation(out=sq.t[:, i, :], in_=xT.t[:, kc, :], func=AF.Square),
                       reads=[xT.r[(kc, t)] for t in range(3)], writes=[sq.r[(i, t)] for t in range(3)])
            else:
                cs = slice(tt * TT, (tt + 1) * TT)
                ACT.op(lambda e: e.activation(out=sq.t[:, i, cs], in_=xT.t[:, kc, cs], func=AF.Square),
                       reads=[xT.r[(kc, tt)]], writes=[sq.r[(i, tt)]])

        def norm_mm(kc, tt=None):
            i = kc % 2
            tts = range(3) if tt is None else [tt]

            def f(e):
                for t in tts:
                    ins = e.matmul(bank(SSQ_BANKS[t], 0, TT), lhsT=onesb.t[:], rhs=sq.t[:, i, t * TT:(t + 1) * TT],
                                   start=(kc == 0), stop=(kc == KC - 1))
                return ins
            PE.op(f, reads=[sq.r[(i, t)] for t in tts] + onesb.all, writes=[bank_r[SSQ_BANKS[t]] for t in tts])

        rstd_ready = [False]

        def norm_finish(gidx, dst, dst_res, dst_f32_off=None, kc_major=False):
            off = 0 if dst_f32_off is None else dst_f32_off
            if not rstd_ready[0]:
                for tt in range(3):
                    norm_rstd(tt)
            rstd_ready[0] = False
            order = [(tt, kc) for tt in range(3) for kc in range(KC)]
            if kc_major:
                order = [(tt, kc) for kc in range(KC) for tt in range(3)]
            for tt, kc in order:
                cs = slice(tt * TT, (tt + 1) * TT)
                ds = slice(off + tt * TT, off + (tt + 1) * TT)
                DVE.op(lambda e: e.scalar_tensor_tensor(out=dst[:, kc, ds], in0=xT.t[:, kc, cs], scalar=pvd.t[:, kc, gidx:gidx + 1],
                                                        in1=rstd.t[:, cs], op0=ALU.mult, op1=ALU.mult),
                       reads=[xT.r[(kc, tt)], rstd.r[(tt,)]] + pvd.all, writes=[dst_res[(kc, tt)]])

        def full_norm_stats():
            for kc in range(KC):
                norm_square(kc)
                norm_mm(kc)

        def norm_rstd(tt):
            cs = slice(tt * TT, (tt + 1) * TT)
            ACT.op(lambda e: e.activation(out=rstd.t[:, cs], in_=bank(SSQ_BANKS[tt], 0, TT), func=AF.Ln, bias=epsb.t[:, 0:1], scale=1.0 / D),
                   reads=[bank_r[SSQ_BANKS[tt]]] + epsb.all, writes=[rstd.r[(tt,)]])
            ACT.op(lambda e: e.activation(out=rstd.t[:, cs], in_=rstd.t[:, cs], func=AF.Exp, scale=-0.5),
                   reads=[rstd.r[(tt,)]], writes=[rstd.r[(tt,)]])

        class FinalPhase:
            def __init__(self, banks=None):
                self.pend = None
                self.rstd_done = False
                self.banks = banks
                self.cnt = 0

            def chunk(self, dc, mm_emit, evac_emit, hooks=None):
                last = (dc == KC - 1)
                for tt in range(3):
                    if hooks and tt in hooks:
                        hooks[tt]()
                    if self.banks is None:
                        bk = ACC_BANKS[acc_flip[0]]
                        acc_flip[0] ^= 1
                    else:
                        bk = self.banks[self.cnt % len(self.banks)]
                        self.cnt += 1
                    mm_emit(dc, tt, bk)
                    if self.pend is not None and tt == 0:
                        norm_mm(self.pend)
                        self.pend = None
                    evac_emit(dc, tt, bk)
                    if last:
                        norm_square(dc, tt)
                        if tt >= 1:
                            norm_mm(dc, tt - 1)
                            norm_rstd(tt - 1)
                if last:
                    norm_mm(dc, 2)
                    norm_rstd(2)
                    rstd_ready[0] = True
                    if keep_warm:
                        def warm(e):
                            for _ in range(keep_warm):
                                i = e.matmul(bank(ACC_BANKS[0], 0, TT), lhsT=onesb.t[:], rhs=sq.t[:, 0, 0:TT], start=True, stop=True)
                            return i
                        PE.op(warm, reads=[sq.r[(0, 0)]] + onesb.all, writes=[bank_r[ACC_BANKS[0]]])
                else:
                    norm_square(dc)
                    self.pend = dc

        ACC_BANKS = [3, 4]
        acc_flip = [0]

        def load_x_dma(g, c):
            p0, sr0 = g * NP, g * NS
            si = c % 2
            sres = xstg.r[(si,)]
            if c < 8:
                SP.op(lambda e: e.dma_start(out=xstg.t[:, si, :], in_=x_prompt[p0 + c * 128: p0 + (c + 1) * 128, :]), writes=[sres], dsem=xstg_ld[si])
            else:
                SP.op(lambda e: e.dma_start(out=xstg.t[0:NS, si, :], in_=x_sample[sr0:sr0 + NS, :]), writes=[sres], dsem=xstg_ld[si])

        def load_x_tr(g, c):
            si = c % 2
            sres = xstg.r[(si,)]
            ntok = 128 if c < 8 else NS
            col0 = c * 128
            for hb in range(2):
                bk = [7, 4][hb]

                def trx(e):
                    for j in range(4):
                        kc = hb * 4 + j
                        i = e.transpose(bank(bk, j * 128, j * 128 + ntok), xstg.t[0:ntok, si, kc * 128:(kc + 1) * 128], ident.t[0:ntok, 0:ntok])
                    return i
                PE.op(trx, reads=[sres] + ident.all, writes=[bank_r[bk]])
                wr = [xT.r[(hb * 4 + j, tt)] for j in range(4) for tt in tiles_of(col0, col0 + ntok)]
                evac_copy(xT.t[:, hb * 4:hb * 4 + 4, col0:col0 + ntok],
                          bank(bk).rearrange("p (k c) -> p k c", c=128)[:, :, 0:ntok], [bank_r[bk]], wr)

        def store_y_chunk(g, c):
            p0, sr0 = g * NP, g * NS
            si = c % 2
            sres = stg.r[(si,)]
            ntok = 128 if c < 8 else NS
            col0 = c * 128
            for hb in range(2):
                bk = 5 + hb

                def try_(e):
                    for jj in range(4):
                        kc = hb * 4 + jj
                        i = e.transpose(bank(bk, jj * 128, (jj + 1) * 128)[0:ntok, :], HH.t[:, kc, 15 + col0:15 + col0 + ntok], ident.t[:])
                    return i
                PE.op(try_, reads=[HH.r[(hb * 4 + jj, tt)] for jj in range(4) for tt in tiles_of(col0, col0 + ntok)] + ident.all, writes=[bank_r[bk]])
                evac_copy(stg.t[0:ntok, si, hb * 512:(hb + 1) * 512], bank(bk)[0:ntok, :], [bank_r[bk]], [sres])

        def store_y_dma(g, c):
            p0, sr0 = g * NP, g * NS
            si = c % 2
            sres = stg.r[(si,)]
            if c < 8:
                SP.op(lambda e: e.dma_start(out=y_prompt[p0 + c * 128:p0 + (c + 1) * 128, :], in_=stg.t[:, si, :]), reads=[sres], dsem=stg_st[si])
            else:
                SP.op(lambda e: e.dma_start(out=y_sample[sr0:sr0 + NS, :], in_=stg.t[0:NS, si, :]), reads=[sres], dsem=stg_st[si])

        for g in range(ngroups):
            p0 = g * NP
            sr0 = g * NS
            sq0 = g * 8

            if g == 0:
                load_x_dma(0, 0)
                for c in range(9):
                    if c + 1 < 9:
                        load_x_dma(0, c + 1)
                    load_x_tr(0, c)
            full_norm_stats()
            if g == 0:
                setup_part2()

            pool_state_loaded = [False]

            def load_pool_state_dma(j):
                SP.op(lambda e: e.dma_start(out=stg.t[0:120, 0, :], in_=state_pool[j, sq0 * 15:(sq0 + 8) * 15, :]), writes=[stg.r[(0,)]], dsem=stg_ld[0])
                pool_state_loaded[0] = True

            def load_conv_state(l, eng=None):
                for ci, c0 in enumerate(range(0, DFF, 1024)):
                    cw = min(1024, DFF - c0)
                    nf = cw // 128
                    si = ci % 2
                    SP.op(lambda e: e.dma_start(out=stg.t[0:16, si, 0:cw], in_=state_ffn[l, sq0 * 2:(sq0 + 8) * 2, c0:c0 + cw]),
                          writes=[stg.r[(si,)]], dsem=stg_ld[si])
                    bk = 5 + si

                    def trst(e):
                        for jj in range(nf):
                            i = e.transpose(bank(bk, jj * 16, jj * 16 + 16), stg.t[0:16, si, jj * 128:(jj + 1) * 128], ident.t[0:16, 0:16])
                        return i
                    PE.op(trst, reads=[stg.r[(si,)]] + ident.all, writes=[bank_r[bk]])
                    f0 = c0 // 128
                    evac_copy(ASb.t[:, f0:f0 + nf, :, 0:2], bank(bk, 0, nf * 16).rearrange("p (f s r) -> p f s r", s=8, r=2),
                              [bank_r[bk]], [ASb.r[(f,)] for f in range(f0, f0 + nf)], eng=eng)

            for l in layers:
                j = l // 2
                if do_mixer and l % 2 == 0:
                    gmlp_layer = True
                else:
                    gmlp_layer = False
                if do_mixer and gmlp_layer:
                    norm_finish(l, hB.t, hB.r)
                    SP.op(lambda e: e.dma_start(out=gvb.t[:], in_=g_v_a[j].partition_broadcast(128)), writes=gvb.all, dsem=gvb_sem)
                    for r in range(2):
                        POOL.op(lambda e: e.dma_start(out=browM.t[0:1, :, :].rearrange("o (h r) i -> o h r i", r=2)[:, :, r, :],
                                                      in_=b_s_a[j:j + 1, :].rearrange("o (h i) -> o h i", i=128)),
                                writes=[browM.all[r]], dsem=browM_sem)
                    ubanks = [5, 6, 7, 3]
                    step = 0
                    for ec in range(EC):
                        u = ensure(("Uu", g, l, ec // 4))
                        wv = u["view"]
                        for tt in range(3):
                            bk = ubanks[step % 4]
                            step += 1
                            cs = slice(tt * TT, (tt + 1) * TT)

                            def mmu(e):
                                for kc in range(KC):
                                    i = e.matmul(bank(bk, 0, TT), lhsT=wv[:, kc, (ec % 4) * 128:(ec % 4 + 1) * 128], rhs=hB.t[:, kc, cs],
                                                 start=(kc == 0), stop=(kc == KC - 1))
                                return i
                            PE.op(mmu, reads=u["res"] + [hB.r[(kc, tt)] for kc in range(KC)], writes=[bank_r[bk]])
                            ACT.op(lambda e: e.activation(out=UB.t[:, ec, cs], in_=bank(bk, 0, TT), func=AF.Gelu_apprx_tanh),
                                   reads=[bank_r[bk]], writes=[UB.r[(ec, tt)]])
                        if ec % 4 == 3:
                            release(("Uu", g, l, ec // 4))
                    def v_mm(c):
                        ntok = 128 if c < 8 else NS
                        col0 = c * 128
                        for half in range(2):
                            def f(e):
                                for n in (2 * half, 2 * half + 1):
                                    for kc in range(KC):
                                        i = e.matmul(psum[0:ntok, n * 512:(n + 1) * 512], lhsT=hB.t[:, kc, col0:col0 + ntok],
                                                     rhs=R2.t[:, n, kc, :], start=(kc == 0), stop=(kc == KC - 1))
                                return i
                            PE.op(f, reads=[R2.r[(2 * half,)], R2.r[(2 * half + 1,)]] + [hB.r[(kc, tt)] for kc in range(KC) for tt in tiles_of(col0, col0 + ntok)],
                                  writes=bank_r[2 * half:2 * half + 2])
                            hs_ = slice(half * 1024, (half + 1) * 1024)
                            ACT.op(lambda e: e.activation(out=vsb.t[0:ntok, hs_], in_=psum[0:ntok, hs_], func=AF.Gelu_apprx_tanh),
                                   reads=bank_r[2 * half:2 * half + 2], writes=vsb.all)

                    def v_elem(c):
                        ntok = 128 if c < 8 else NS
                        vb = vnb[c % 2]
                        ACT.op(lambda e: e.activation(out=junk.t[0:ntok, :], in_=vsb.t[0:ntok, :], func=AF.Square, accum_out=vstat.t[0:ntok, 0:1]),
                               reads=vsb.all, writes=junk.all + vstat.all)
                        ACT.op(lambda e: e.activation(out=vstat.t[0:ntok, 1:2], in_=vstat.t[0:ntok, 0:1], func=AF.Sqrt, bias=epsb.t[0:ntok, 0:1], scale=1.0 / DV),
                               reads=vstat.all + epsb.all, writes=vstat.all)
                        DVE.op(lambda e: e.reciprocal(out=vstat.t[0:ntok, 2:3], in_=vstat.t[0:ntok, 1:2]), reads=vstat.all, writes=vstat.all)
                        DVE.op(lambda e: e.scalar_tensor_tensor(out=vb.t[0:ntok, :], in0=vsb.t[0:ntok, :], scalar=vstat.t[0:ntok, 2:3], in1=gvb.t[0:ntok, :],
                                                                op0=ALU.mult, op1=ALU.mult),
                               reads=vsb.all + vstat.all + gvb.all, writes=vb.all)
                        if c == 8:
                            so = stg.t[0:NS, :, :].rearrange("p a b -> p (a b)")
                            DVE.op(lambda e: e.scalar_tensor_tensor(out=so, in0=vsb.t[0:NS, :], scalar=vstat.t[0:NS, 2:3], in1=gvb.t[0:NS, :],
                                                                    op0=ALU.mult, op1=ALU.mult),
                                   reads=vsb.all + vstat.all + gvb.all, writes=stg.all)
                            SP.op(lambda e: e.dma_start(out=gmlp_v[j, sr0:sr0 + NS, :], in_=so), reads=stg.all, dsem=stg_st[0])

                    def s_mm(c):
                        ntok = 128 if c < 8 else NS
                        vb = vnb[c % 2]

                        def f(e):
                            if c < 8:
                                for n in range(4):
                                    e.matmul(psum[:, DV + n * 512: DV + (n + 1) * 512], lhsT=onesb.t[0:1, 0:128],
                                             rhs=browM.t[0:1, 4 * n:4 * n + 4, :].rearrange("o a b -> o (a b)"), start=True, stop=False, skip_group_check=True)
                                for ec in range(EC):
                                    i = e.matmul(psum[:, DV + ec * 128: DV + (ec + 1) * 128], lhsT=vb.t[:, ec * 128:(ec + 1) * 128],
                                                 rhs=wsT.t[:, j, ec // 2, :], start=False, stop=True, skip_group_check=True)
                                return i
                            for ec in range(EC):
                                hh = ec // 2
                                o_ap = bank(7, ec * NS, (ec + 1) * NS)
                                e.matmul(o_ap, lhsT=vb.t[0:ntok, ec * 128:(ec + 1) * 128], rhs=msm.t[0:NS, j, hh, :], start=True, stop=False)
                                i = e.matmul(o_ap, lhsT=onesb.t[0:1, 0:128], rhs=brow.t[0:1, j, hh, :], start=False, stop=True)
                            return i
                        PE.op(f, reads=vb.all + wsT.all + msm.all + brow.all + browM.all + onesb.all, writes=(bank_r[4:8] if c < 8 else [bank_r[7]]))

                    def s_elem(c):
                        ntok = 128 if c < 8 else NS
                        col0 = c * 128
                        urs = [UB.r[(ec, tt)] for ec in range(EC) for tt in tiles_of(col0, col0 + ntok)]
                        if c < 8:
                            s_in, s_rd = psum[:, DV:2 * DV].rearrange("p (a b) -> p a b", b=128), bank_r[4:8]
                        else:
                            s_in, s_rd = bank(7).rearrange("p (a b) -> p a b", b=NS), [bank_r[7]]
                        DVE.op(lambda e: e.tensor_tensor(out=UB.t[:, :, col0:col0 + ntok], in0=s_in, in1=UB.t[:, :, col0:col0 + ntok], op=ALU.mult),
                               reads=s_rd + urs, writes=urs)
                    v_mm(0)
                    v_elem(0)
                    for c in range(1, 9):
                        v_mm(c)
                        s_mm(c - 1)
                        s_elem(c - 1)
                        v_elem(c)
                    fp = FinalPhase()
                    for dc in range(KC):
                        u = ensure(("Wo", g, l, dc // 2))
                        wv = u["view"]

                        def mm_emit(dc, tt, bk):
                            cs = slice(tt * TT, (tt + 1) * TT)

                            def mmo(e):
                                for ec in range(EC):
                                    i = e.matmul(bank(bk, 0, TT), lhsT=wv[:, ec, (dc % 2) * 128:(dc % 2 + 1) * 128], rhs=UB.t[:, ec, cs],
                                                 start=(ec == 0), stop=(ec == EC - 1))
                                return i
                            PE.op(mmo, reads=u["res"] + [UB.r[(ec, tt)] for ec in range(EC)], writes=[bank_r[bk]])

                        def evac_emit(dc, tt, bk):
                            cs = slice(tt * TT, (tt + 1) * TT)
                            DVE.op(lambda e: e.tensor_tensor(out=xT.t[:, dc, cs], in0=bank(bk, 0, TT), in1=xT.t[:, dc, cs], op=ALU.add),
                                   reads=[bank_r[bk], xT.r[(dc, tt)]], writes=[xT.r[(dc, tt)]])
                        hk = None
                        if dc == 0:
                            hk = {2: lambda: (s_mm(8), s_elem(8))}
                        elif dc == 2 and do_ffn:
                            hk = {0: lambda: load_conv_state(l)}
                        fp.chunk(dc, mm_emit, evac_emit, hooks=hk)
                        if dc % 2 == 1:
                            release(("Wo", g, l, dc // 2))
                elif do_mixer:
                    if g == 0:
                        POOL.op(lambda e: e.memset(HH.t[:, :, 0:15], 0.0), writes=[HH.r[(k, 'h')] for k in range(KC)])
                    else:
                        DVE.op(lambda e: e.tensor_copy(out=HH.t[:, :, 0:15], in_=poolhist.t[:, j, :, :]),
                               reads=[poolhist.r[(j,)]], writes=[HH.r[(k, 'h')] for k in range(KC)])
                    norm_finish(l, HH.t, HH.r, dst_f32_off=15, kc_major=True)
                    wp = ensure(("Wp", g, l, 0))
                    wpv = wp["view"]
                    coefA = [-0.5, 0.25, 0.125, 0.0625]
                    coefB = [0.5, -1.0, -1.0, -1.0]
                    for gi in range(4):
                        ACT.op(lambda e: e.activation(out=WAb.t[:, 2 * gi:2 * gi + 2, :], in_=wpv[:, 2 * gi:2 * gi + 2, :], func=AF.Copy, scale=coefA[gi]),
                               reads=wp["res"], writes=WAb.all)
                        ACT.op(lambda e: e.activation(out=WBb.t[:, 2 * gi:2 * gi + 2, :], in_=wpv[:, 2 * gi:2 * gi + 2, :], func=AF.Copy, scale=coefB[gi]),
                               reads=wp["res"], writes=WBb.all)

                    def cast_h(kc):
                        ACT.op(lambda e: e.activation(out=hbfB.t[:, kc, 0:15 + NP], in_=HH.t[:, kc, 0:15 + NP], func=AF.Copy),
                               reads=[HH.r[(kc, 'h')]] + [HH.r[(kc, tt)] for tt in range(3)], writes=[hbfB.r[(kc,)]])
                    cast_h(0)
                    cast_h(1)
                    if not pool_state_loaded[0]:
                        load_pool_state_dma(j)
                    pool_state_loaded[0] = False
                    for hb in range(2):
                        bk = 5 + hb

                        def trs(e):
                            for jj in range(4):
                                kc = hb * 4 + jj
                                i = e.transpose(bank(bk, jj * 128, jj * 128 + 120), stg.t[0:120, 0, kc * 128:(kc + 1) * 128], ident.t[0:120, 0:120])
                            return i
                        PE.op(trs, reads=[stg.r[(0,)]] + ident.all, writes=[bank_r[bk]])
                        for jj in range(4):
                            kc = hb * 4 + jj
                            evac_copy(HS.t[:, kc, :, 0:15], bank(bk, jj * 128, jj * 128 + 120).rearrange("p (s r) -> p s r", r=15),
                                      [bank_r[bk]], [HS.r[(kc,)]], eng=1)
                    for kc in range(2, KC):
                        cast_h(kc)
                    pfp = FinalPhase()
                    fix0 = 16 if g == 0 else 0

                    def pool_terms(gi, k):
                        hsrc = (hbfB.t[:, k, :], [hbfB.r[(k,)]])
                        if gi == 0:
                            return [(hsrc, 0, WAb), (hsrc, 1, WBb)]
                        sb = Sb[k - 2]
                        ssrc = (sb.t[:, :], sb.all)
                        half = 2 ** gi
                        return [(ssrc, 0, WAb), (ssrc, half, WAb), (hsrc, 0, WBb)]

                    def pool_mm(ec):
                        gi = ec // 2
                        eo = (ec % 2) * 128

                        def mm_emit(ec, tt, bk):
                            c0 = tt * TT + (fix0 if tt == 0 else 0)
                            c1 = min((tt + 1) * TT, NP)
                            rds = list(wp["res"]) + WAb.all + WBb.all
                            plan = []
                            grp = []
                            for cc in range(2):
                                k = gi * 2 + cc
                                for (src, srcres), sh, W in pool_terms(gi, k):
                                    grp.append((W.t[:, k, eo:eo + 128], src[:, 15 + c0 - sh:15 + c1 - sh]))
                                    rds += srcres
                            plan.append((bank(bk, c0 - tt * TT, c1 - tt * TT), grp))
                            if tt == 0 and fix0:
                                plan.append((bank(bk, 0, fix0), [(wpv[:, gi * 2 + cc, eo:eo + 128], pfix.t[:, gi * 2 + cc, :]) for cc in range(2)]))
                                rds += [pfix.r[(gi * 2 + cc,)] for cc in range(2)]
                            if tt == 2:
                                plan.append((bank(bk, NP - 2 * TT, TT), [(wpv[:, gi * 2 + cc, eo:eo + 128], pS.t[:, gi * 2 + cc, :]) for cc in range(2)]))
                                rds += [pS.r[(gi * 2 + cc,)] for cc in range(2)]

                            def mmp(e):
                                for o_ap, lst in plan:
                                    for n, (lt, rh) in enumerate(lst):
                                        i = e.matmul(o_ap, lhsT=lt, rhs=rh, start=(n == 0), stop=(n == len(lst) - 1))
                                return i
                            PE.op(mmp, reads=rds, writes=[bank_r[bk]])

                        def evac_emit(ec, tt, bk):
                            cs = slice(tt * TT, (tt + 1) * TT)
                            DVE.op(lambda e: e.scalar_tensor_tensor(out=xT.t[:, ec, cs], in0=bank(bk, 0, TT), scalar=pvd.t[:, ec, 9 + j:10 + j], in1=xT.t[:, ec, cs],
                                                                    op0=ALU.mult, op1=ALU.add),
                                   reads=[bank_r[bk], xT.r[(ec, tt)]] + pvd.all, writes=[xT.r[(ec, tt)]])
                        pfp.chunk(ec, mm_emit, evac_emit)
                    allHH = [HH.r[(k, 'h')] for k in range(KC)] + [HH.r[(k, tt)] for k in range(KC) for tt in range(3)]
                    DVE.op(lambda e: e.tensor_copy(out=HS.t[:, :, :, 15:19], in_=HH.t[:, :, 15 + NP:15 + NT].rearrange("p k (s t) -> p k s t", t=4)),
                           reads=[HH.r[(k, 2)] for k in range(KC)], writes=HS.all)
                    DVE.op(lambda e: e.tensor_copy(out=poolhist.t[:, j, :, :], in_=HH.t[:, :, NP:NP + 15]),
                           reads=[HH.r[(k, 2)] for k in range(KC)], writes=[poolhist.r[(j,)]])
                    for st in range(4):
                        k0 = 2 * st
                        sh = 2 ** st
                        lo = 2 ** (st + 1) - 1
                        w = 2 ** (st + 1)
                        if st == 0:
                            sa, sar = HS.t, HS.all
                            fa, far = HH.t, allHH
                        else:
                            sa, sar = SAs[(st - 1) % 2].t, SAs[(st - 1) % 2].all
                            fa, far = fx[(st - 1) % 2].t, fx[(st - 1) % 2].all
                        sd = SAs[st % 2]
                        DVE.op(lambda e: e.tensor_tensor(out=sd.t[:, k0:KC, :, lo:19], in0=sa[:, k0:KC, :, lo:19], in1=sa[:, k0:KC, :, lo - sh:19 - sh], op=ALU.add),
                               reads=sar, writes=sd.all)
                        DVE.op(lambda e: e.scalar_tensor_tensor(out=pS.t[:, k0:k0 + 2, :].rearrange("p k (s t) -> p k s t", t=4), in0=sd.t[:, k0:k0 + 2, :, 15:19],
                                                                scalar=1.0 / w, in1=HS.t[:, k0:k0 + 2, :, 15:19], op0=ALU.mult, op1=ALU.subtract),
                               reads=sd.all + HS.all, writes=[pS.r[(k0,)], pS.r[(k0 + 1,)]])
                        if fix0:
                            fd = fx[st % 2]
                            DVE.op(lambda e: e.tensor_tensor(out=fd.t[:, k0:KC, lo:31], in0=fa[:, k0:KC, lo:31], in1=fa[:, k0:KC, lo - sh:31 - sh], op=ALU.add),
                                   reads=far, writes=fd.all)
                            for kc in (k0, k0 + 1):
                                DVE.op(lambda e: e.tensor_tensor(out=ptmp8.t[:, kc, :], in0=fd.t[:, kc, 15:31], in1=icnt.t[:, st, :], op=ALU.mult),
                                       reads=fd.all + icnt.all, writes=ptmp8.all)
                                DVE.op(lambda e: e.tensor_tensor(out=pfix.t[:, kc, :], in0=ptmp8.t[:, kc, :], in1=HH.t[:, kc, 15:31], op=ALU.subtract),
                                       reads=ptmp8.all + [HH.r[(kc, 0)]], writes=[pfix.r[(kc,)]])
                    def s_adds(kc):
                        gi = kc // 2
                        src = HH.t[:, kc, 0:15 + NP]
                        srcr = [HH.r[(kc, 'h')]] + [HH.r[(kc, tt)] for tt in range(3)]
                        sb = Sb[kc - 2]
                        cur, curr = src, srcr
                        for st in range(gi):
                            sh = 2 ** st
                            lo = 2 ** (st + 1) - 1
                            dstb = sb if st == gi - 1 else pA[st % 2]
                            a, b = cur, dstb.t
                            DVE.op(lambda e: e.tensor_tensor(out=b[:, lo:15 + NP], in0=a[:, lo:15 + NP], in1=a[:, lo - sh:15 + NP - sh], op=ALU.add),
                                   reads=curr, writes=dstb.all)
                            cur, curr = dstb.t, dstb.all
                    if pool_dbg >= 2:
                        s_adds(2)
                        s_adds(3)
                        pool_mm(0)
                        pool_mm(1)
                        s_adds(4)
                        s_adds(5)
                        pool_mm(2)
                        pool_mm(3)
                        s_adds(6)
                        s_adds(7)
                        pool_mm(4)
                        pool_mm(5)
                        if do_ffn:
                            load_conv_state(l, eng=1)
                        pool_mm(6)
                        pool_mm(7)
                    release(("Wp", g, l, 0))
                    DVE.op(lambda e: e.tensor_copy(out=HSo.t[:, :, :].rearrange("p k (s r) -> p k s r", r=15), in_=HS.t[:, :, :, 4:19]),
                           reads=HS.all, writes=HSo.all)
                    for hb in range(2 if pool_dbg >= 3 else 0):
                        bk = 5 + hb

                        def trps(e):
                            for jj in range(4):
                                kc = hb * 4 + jj
                                i = e.transpose(bank(bk, jj * 128, (jj + 1) * 128)[0:120, :], HSo.t[:, kc, :], ident.t[:])
                            return i
                        PE.op(trps, reads=HSo.all + ident.all, writes=[bank_r[bk]])
                        evac_copy(stg.t[0:120, 1, hb * 512:(hb + 1) * 512], bank(bk)[0:120, :], [bank_r[bk]], [stg.r[(1,)]])
                    if pool_dbg >= 3:
                        SP.op(lambda e: e.dma_start(out=pool_sample[j, sq0 * 15:(sq0 + 8) * 15, :], in_=stg.t[0:120, 1, :]), reads=[stg.r[(1,)]], dsem=stg_st[1])
                    if g == ngroups - 1 and pool_dbg >= 4:
                        for hb in range(2):
                            bk = 5 + hb

                            def trpp(e):
                                for jj in range(4):
                                    kc = hb * 4 + jj
                                    i = e.transpose(bank(bk, jj * 128, (jj + 1) * 128)[0:15, :], poolhist.t[:, j, kc, :], ident.t[:])
                                return i
                            PE.op(trpp, reads=[poolhist.r[(j,)]] + ident.all, writes=[bank_r[bk]])
                            evac_copy(stg.t[0:15, 0, hb * 512:(hb + 1) * 512], bank(bk)[0:15, :], [bank_r[bk]], [stg.r[(0,)]])
                        SP.op(lambda e: e.dma_start(out=pool_prompt[j, :, :], in_=stg.t[0:15, 0, :]), reads=[stg.r[(0,)]], dsem=stg_st[0])
                if do_ffn:
                    if not (do_mixer):
                        load_conv_state(l)
                    norm_finish(4 + l, hB.t, hB.r)
                    gbanks = [0, 1, 2]
                    vbanks = [5, 6, 7]
                    steps = [(fc, tt) for fc in range(FC) for tt in range(3)]

                    def stage1(i):
                        fc, tt = steps[i]
                        ug = ensure(("Wg", g, l, fc // 4))
                        uv = ensure(("Wv", g, l, fc // 4))
                        gbk, vbk = gbanks[i % 3], vbanks[i % 3]
                        cs = slice(tt * TT, (tt + 1) * TT)
                        fo = (fc % 4) * 128
                        hr = [hB.r[(kc, tt)] for kc in range(KC)]

                        def mmg(e):
                            for kc in range(KC):
                                ins = e.matmul(bank(gbk, 0, TT), lhsT=ug["view"][:, kc, fo:fo + 128], rhs=hB.t[:, kc, cs], start=(kc == 0), stop=(kc == KC - 1))
                            return ins
                        PE.op(mmg, reads=ug["res"] + hr, writes=[bank_r[gbk]])

                        def mmv(e):
                            for kc in range(KC):
                                ins = e.matmul(bank(vbk, 0, TT), lhsT=uv["view"][:, kc, fo:fo + 128], rhs=hB.t[:, kc, cs], start=(kc == 0), stop=(kc == KC - 1))
                            return ins
                        PE.op(mmv, reads=uv["res"] + hr, writes=[bank_r[vbk]])
                        ab = asb[fc % 2]
                        abw = [ab.r[(tt,)]]
                        abr = [ab.r[(tt,)]] + ([ab.r[(tt - 1,)]] if tt > 0 else [])
                        npr = TT if tt < 2 else NP - 2 * TT
                        pc0 = tt * TT
                        if tt == 0:
                            if g == 0:
                                DVE.op(lambda e: e.memset(ab.t[:, 0:2], 0.0), writes=abw)
                            else:
                                ACT.op(lambda e: e.activation(out=ab.t[:, 0:2], in_=convhist.t[:, l, fc, :], func=AF.Copy),
                                       reads=[convhist.r[(l, fc)]], writes=abw)
                        ACT.op(lambda e: e.activation(out=ab.t[:, 2 + pc0:2 + pc0 + npr], in_=bank(gbk, 0, npr), func=AF.Copy),
                               reads=[bank_r[gbk]], writes=abw)
                        if tt == 2:
                            ACT.op(lambda e: e.activation(out=ASb.t[:, fc, :, 2:6], in_=bank(gbk, npr, TT).rearrange("p (s t) -> p s t", t=4), func=AF.Copy),
                                   reads=[bank_r[gbk]], writes=[ASb.r[(fc,)]])
                        cb = c0b[i % 3]
                        ACT.op(lambda e: e.activation(out=cb.t[:, :], in_=bank(gbk, 0, TT), func=AF.Identity,
                                                      bias=pvf.t[:, fc, 12 + l:13 + l], scale=pvf.t[:, fc, 3 * l + 2:3 * l + 3]),
                               reads=[bank_r[gbk]] + pvf.all, writes=cb.all)
                        c2 = c2b[i % 3]
                        DVE.op(lambda e: e.scalar_tensor_tensor(out=c2.t[:, 0:npr], in0=ab.t[:, 1 + pc0:1 + pc0 + npr], scalar=pvf.t[:, fc, 3 * l + 1:3 * l + 2],
                                                                in1=cb.t[:, 0:npr], op0=ALU.mult, op1=ALU.add),
                               reads=abr + cb.all + pvf.all, writes=c2.all)
                        DVE.op(lambda e: e.scalar_tensor_tensor(out=c2.t[:, 0:npr], in0=ab.t[:, pc0:pc0 + npr], scalar=pvf.t[:, fc, 3 * l:3 * l + 1],
                                                                in1=c2.t[:, 0:npr], op0=ALU.mult, op1=ALU.add),
                               reads=abr + c2.all + pvf.all, writes=c2.all)
                        if tt == 2:
                            v3 = lambda ap: ap.rearrange("p (s t) -> p s t", t=4)
                            DVE.op(lambda e: e.scalar_tensor_tensor(out=v3(c2.t[:, npr:TT]), in0=ASb.t[:, fc, :, 1:5], scalar=pvf.t[:, fc, 3 * l + 1:3 * l + 2],
                                                                    in1=v3(cb.t[:, npr:TT]), op0=ALU.mult, op1=ALU.add),
                                   reads=[ASb.r[(fc,)]] + cb.all + pvf.all, writes=c2.all)
                            DVE.op(lambda e: e.scalar_tensor_tensor(out=v3(c2.t[:, npr:TT]), in0=ASb.t[:, fc, :, 0:4], scalar=pvf.t[:, fc, 3 * l:3 * l + 1],
                                                                    in1=v3(c2.t[:, npr:TT]), op0=ALU.mult, op1=ALU.add),
                                   reads=[ASb.r[(fc,)]] + c2.all + pvf.all, writes=c2.all)
                            DVE.op(lambda e: e.tensor_copy(out=convhist.t[:, l, fc, :], in_=ab.t[:, NP:NP + 2]),
                                   reads=abw, writes=[convhist.r[(l, fc)]])

                        if tt == 2 and (fc % 4 == 3 or fc == FC - 1):
                            release(("Wg", g, l, fc // 4))
                            release(("Wv", g, l, fc // 4))

                    def stage2(i):
                        fc, tt = steps[i]
                        vbk = vbanks[i % 3]
                        cs = slice(tt * TT, (tt + 1) * TT)
                        c2 = c2b[i % 3]
                        gg = gb[i % 3]
                        ACT.op(lambda e: e.activation(out=gg.t[:, :], in_=c2.t[:, :], func=AF.Silu), reads=c2.all, writes=gg.all)
                        DVE.op(lambda e: e.tensor_tensor(out=mT.t[:, fc, cs], in0=gg.t[:, :], in1=bank(vbk, 0, TT), op=ALU.mult),
                               reads=gg.all + [bank_r[vbk]], writes=[mT.r[(fc, tt)]])
                    for i in range(len(steps)):
                        stage1(i)
                        if i > 0:
                            stage2(i - 1)
                    stage2(len(steps) - 1)
                    DVE.op(lambda e: e.tensor_copy(out=cst.t[:, :, :].rearrange("p f (s r) -> p f s r", r=2), in_=ASb.t[:, :, :, 4:6]),
                           reads=ASb.all, writes=cst.all)
                    for ci, f0 in enumerate(range(0, FC, 8)):
                        nf = min(8, FC - f0)
                        bk = 5 + (ci % 2)
                        PE.op(lambda e: e.transpose(bank(bk, 0, 128)[0:nf * 16, :], cst.t[:, f0:f0 + nf, :].rearrange("p f c -> p (f c)"), ident.t[:]),
                              reads=cst.all + ident.all, writes=[bank_r[bk]])
                        evac_copy(stg.t[0:nf * 16, 0, ci * 128:(ci + 1) * 128], bank(bk, 0, 128)[0:nf * 16, :], [bank_r[bk]], [stg_sub[ci]])
                        for fl in range(nf):
                            fc = f0 + fl
                            SP.op(lambda e: e.dma_start(out=conv_sample[l, sq0 * 2:(sq0 + 8) * 2, fc * 128:(fc + 1) * 128],
                                                        in_=stg.t[fl * 16:(fl + 1) * 16, 0, ci * 128:(ci + 1) * 128]),
                                  reads=[stg_sub[ci]], dsem=stg_st[0])
                    if g == ngroups - 1:
                        PE.op(lambda e: e.transpose(bank(7, 0, 128)[0:2 * FC, :], convhist.t[:, l, :, :].rearrange("p f c -> p (f c)"), ident.t[:]),
                              reads=[convhist.r[(l, f)] for f in range(FC)] + ident.all, writes=[bank_r[7]])
                        evac_copy(stg.t[0:2 * FC, 1, 0:128], bank(7, 0, 128)[0:2 * FC, :], [bank_r[7]], [stg.r[(1,)]])
                        for fc in range(FC):
                            SP.op(lambda e: e.dma_start(out=conv_prompt[l, :, fc * 128:(fc + 1) * 128], in_=stg.t[2 * fc:2 * fc + 2, 1, 0:128]),
                                  reads=[stg.r[(1,)]], dsem=stg_st[1])
                    if do_mixer and (l + 1) in layers and (l + 1) % 2 == 1:
                        load_pool_state_dma((l + 1) // 2)
                    fp = FinalPhase(banks=[3, 4, 7])
                    for dc in range(KC):
                        u = ensure(("Wd", g, l, dc))
                        wv = u["view"]

                        def mm_emit(dc, tt, bk):
                            cs = slice(tt * TT, (tt + 1) * TT)

                            def mmd(e):
                                for fc in range(FC):
                                    i = e.matmul(bank(bk, 0, TT), lhsT=wv[:, fc, :], rhs=mT.t[:, fc, cs], start=(fc == 0), stop=(fc == FC - 1))
                                return i
                            PE.op(mmd, reads=u["res"] + [mT.r[(fc, tt)] for fc in range(FC)], writes=[bank_r[bk]])

                        def evac_emit(dc, tt, bk):
                            cs = slice(tt * TT, (tt + 1) * TT)
                            DVE.op(lambda e: e.tensor_tensor(out=xT.t[:, dc, cs], in0=bank(bk, 0, TT), in1=xT.t[:, dc, cs], op=ALU.add),
                                   reads=[bank_r[bk], xT.r[(dc, tt)]], writes=[xT.r[(dc, tt)]])
                        fp.chunk(dc, mm_emit, evac_emit)
                        release(("Wd", g, l, dc))

            norm_finish(8, HH.t, HH.r, dst_f32_off=15)
            nxt = g + 1 < ngroups
            if nxt:
                load_x_dma(g + 1, 0)
                load_x_dma(g + 1, 1)
            for c in range(9):
                store_y_chunk(g, c)
                if nxt:
                    load_x_tr(g + 1, c)
                store_y_dma(g, c)
                if nxt and c + 2 < 9:
                    load_x_dma(g + 1, c + 2)

        for d in dsems:
            if d.count > 0:
                SP.h.wait_ge(d.sem, d.count)
        nc._stats = {e.name: (e.nops, e.nwaits) for e in (PE, ACT, DVE, POOL, SP)}
    return nc


def _consts():
    ident = np.eye(128, dtype=np.float32)
    maskT = np.triu(np.ones((128, 128), dtype=np.float32))
    icnt = np.zeros((4, 16), dtype=np.float32)
    for gi, w in enumerate((2, 4, 8, 16)):
        for p in range(16):
            icnt[gi, p] = 1.0 / min(w, p + 1)
    icnt = np.broadcast_to(icnt.reshape(1, 64), (128, 64)).copy()
    return ident, maskT, icnt


def make_in_maps(inputs, n_cores=N_CORES):
    f = lambda a: np.ascontiguousarray(np.asarray(a, dtype=np.float32))
    ident, maskT, icnt = _consts()
    shared = {
        "norm_mix": f(inputs["norm_mix"]), "norm_ffn": f(inputs["norm_ffn"]),
        "norm_final": f(inputs["norm_final"]).reshape(1, D),
        "w_in_a": f(inputs["w_in_a"]), "g_v_a": f(inputs["g_v_a"]), "w_s_a": f(inputs["w_s_a"]),
        "b_s_a": f(inputs["b_s_a"]).reshape(2, 8 * 128), "w_out_a": f(inputs["w_out_a"]),
        "w_pool_b": f(inputs["w_pool_b"]).reshape(2, 1024, 256), "scale_b": f(inputs["scale_b"]),
        "w_gate": f(inputs["w_gate"]), "w_val": f(inputs["w_val"]),
        "conv_w": f(inputs["conv_w"]).reshape(DEPTH * 3, DFF), "conv_b": f(inputs["conv_b"]),
        "w_down": f(inputs["w_down"]),
        "c_ident": ident, "c_maskT": maskT, "c_icnt": icnt,
    }
    xp = f(inputs["x_prompt"]); xs = f(inputs["x_sample"])
    sp = f(inputs["state_pool"]); sf = f(inputs["state_ffn_conv"])
    maps = []
    for c in range(n_cores):
        m = dict(shared)
        m["x_prompt"] = xp[c]
        m["x_sample"] = np.ascontiguousarray(xs[c * NSEQ:(c + 1) * NSEQ].reshape(NSEQ * 4, D))
        m["state_pool"] = np.ascontiguousarray(sp[:, c * NSEQ:(c + 1) * NSEQ].reshape(2, NSEQ * 15, D))
        m["state_ffn_conv"] = np.ascontiguousarray(sf[:, c * NSEQ:(c + 1) * NSEQ].reshape(DEPTH, NSEQ * 2, DFF))
        maps.append(m)
    return maps


def assemble(results, n_cores=N_CORES):
    g = lambda k: [np.asarray(r[k], dtype=np.float32) for r in results]
    y_prompt = np.stack(g("y_prompt"), axis=0)
    y_sample = np.concatenate([a.reshape(NSEQ, 4, D) for a in g("y_sample")], axis=0)
    gv = np.concatenate([a.reshape(2, NSEQ, 4, DV) for a in g("gmlp_v_sample")], axis=1)
    pp = np.stack(g("pool_prompt"), axis=1)
    ps = np.concatenate([a.reshape(2, NSEQ, 15, D) for a in g("pool_sample")], axis=1)
    cp = np.stack(g("ffn_conv_prompt"), axis=1)
    cs = np.concatenate([a.reshape(DEPTH, NSEQ, 2, DFF) for a in g("ffn_conv_sample")], axis=1)
    return (y_prompt, y_sample, gv, pp, ps, cp, cs)


def kernel(**inputs):
    nc = build_program()
    maps = make_in_maps(inputs)
    res = run_bass_kernel_spmd(nc, maps, core_ids=list(range(N_CORES)))
    return assemble(res.results)
```

```python
import numpy as np
from contextlib import ExitStack
import concourse.bass as bass
import concourse.mybir as mybir
from concourse.bass_utils import run_bass_kernel_spmd

F32 = mybir.dt.float32
BF16 = mybir.dt.bfloat16
AF = mybir.ActivationFunctionType
ALU = mybir.AluOpType

N_CORES = 8
D = 1024
DFF = 2816
DV = 2048
DEPTH = 4
SEQ = 2048
NSEQ = 16
NG = 2
NP = 1024
NS = 32
NT = NP + NS
TT = 352
KC = 8
FC = 22
EC = 16
EPS = 1e-6
SLOT = 4096
NSLOT = 4


class Res:
    def __init__(self, reg, space, lo, hi, name=""):
        self.space, self.lo, self.hi, self.name = space, lo, hi, name
        self.last_write = None
        self.reads = []
        lst = reg.setdefault(space, [])
        self.overl = [self]
        for o in lst:
            if o.lo < hi and lo < o.hi:
                self.overl.append(o)
                o.overl.append(self)
        lst.append(self)


class Tick:
    __slots__ = ("sem", "val", "clock")

    def __init__(self, sem, val, clock):
        self.sem, self.val, self.clock = sem, val, clock


class DmaSem:
    def __init__(self, sem):
        self.sem = sem
        self.count = 0


class Eng:
    def __init__(self, name, handle, sem, self_safe=False):
        self.name, self.h, self.sem, self.self_safe = name, handle, sem, self_safe
        self.count = 0
        self.know = {}
        self.nwaits = 0
        self.nops = 0

    def _wait(self, t):
        key = t.sem.name
        if self.know.get(key, 0) >= t.val:
            return
        self.h.wait_ge(t.sem, t.val)
        self.nwaits += 1
        self.know[key] = t.val
        for k, v in t.clock.items():
            if self.know.get(k, 0) < v:
                self.know[k] = v

    def op(self, fn, reads=(), writes=(), dsem=None):
        best = {}

        def add(t, raw):
            k = t.sem.name
            if k not in best:
                best[k] = [t, raw]
            else:
                if best[k][0].val < t.val:
                    best[k][0] = t
                best[k][1] = best[k][1] or raw
        for r in reads:
            for o in r.overl:
                if o.last_write is not None:
                    add(o.last_write, True)
                if r.space == "ps":
                    for t in o.reads:
                        add(t, False)
        for w in writes:
            for o in w.overl:
                if o.last_write is not None:
                    add(o.last_write, False)
                for t in o.reads:
                    add(t, False)
        for t, raw in best.values():
            if t.sem is self.sem and (self.self_safe or not raw):
                continue
            self._wait(t)
        ins = fn(self.h)
        self.nops += 1
        if dsem is not None:
            dsem.count += 16
            ins.then_inc(dsem.sem, 16)
            tick = Tick(dsem.sem, dsem.count, dict(self.know))
        else:
            self.count += 1
            ins.then_inc(self.sem, 1)
            tick = Tick(self.sem, self.count, dict(self.know))
        for r in reads:
            r.reads.append(tick)
            if len(r.reads) > 48:
                b = {}
                for t in r.reads:
                    k = t.sem.name
                    if k not in b or b[k].val < t.val:
                        b[k] = t
                r.reads = list(b.values())
        for w in writes:
            w.last_write = tick
            w.reads = []
        return tick


def tiles_of(c0, c1):
    return list(range(c0 // TT, (c1 - 1) // TT + 1))


def build_program(cfg=None):
    cfg = cfg or {}
    nlayers = cfg.get("nlayers", DEPTH)
    do_mixer = cfg.get("do_mixer", True)
    do_ffn = cfg.get("do_ffn", True)
    ngroups = cfg.get("ngroups", NG)
    layers = cfg.get("layers", list(range(nlayers)))
    pool_dbg = cfg.get("pool_dbg", 99)
    keep_warm = cfg.get("keep_warm", 0)

    nc = bass.Bass("TRN2", target_bir_lowering=False)
    dr = {}

    def din(name, shape):
        dr[name] = nc.dram_tensor(name, list(shape), F32, kind="ExternalInput").ap()
        return dr[name]

    def dout(name, shape):
        dr[name] = nc.dram_tensor(name, list(shape), F32, kind="ExternalOutput").ap()
        return dr[name]

    x_prompt = din("x_prompt", [SEQ, D])
    x_sample = din("x_sample", [NSEQ * 4, D])
    state_pool = din("state_pool", [2, NSEQ * 15, D])
    state_ffn = din("state_ffn_conv", [DEPTH, NSEQ * 2, DFF])
    norm_mix = din("norm_mix", [DEPTH, D])
    norm_ffn = din("norm_ffn", [DEPTH, D])
    norm_final = din("norm_final", [1, D])
    w_in_a = din("w_in_a", [2, D, 2 * DV])
    g_v_a = din("g_v_a", [2, DV])
    w_s_a = din("w_s_a", [2, 8, 128, 128])
    b_s_a = din("b_s_a", [2, 8 * 128])
    w_out_a = din("w_out_a", [2, DV, D])
    w_pool_b = din("w_pool_b", [2, 1024, 256])
    scale_b = din("scale_b", [2, D])
    w_gate = din("w_gate", [DEPTH, D, DFF])
    w_val = din("w_val", [DEPTH, D, DFF])
    conv_w = din("conv_w", [DEPTH * 3, DFF])
    conv_b = din("conv_b", [DEPTH, DFF])
    w_down = din("w_down", [DEPTH, DFF, D])
    c_ident = din("c_ident", [128, 128])
    c_maskT = din("c_maskT", [128, 128])
    c_icnt = din("c_icnt", [128, 64])

    y_prompt = dout("y_prompt", [SEQ, D])
    y_sample = dout("y_sample", [NSEQ * 4, D])
    gmlp_v = dout("gmlp_v_sample", [2, NSEQ * 4, DV])
    pool_prompt = dout("pool_prompt", [2, 15, D])
    pool_sample = dout("pool_sample", [2, NSEQ * 15, D])
    conv_prompt = dout("ffn_conv_prompt", [DEPTH, 2, DFF])
    conv_sample = dout("ffn_conv_sample", [DEPTH, NSEQ * 2, DFF])

    es = ExitStack()
    with es:
        ARENA_BYTES = 212800
        arena = es.enter_context(nc.sbuf_tensor("arena", [128, ARENA_BYTES // 4], F32))
        base = nc.lookup_mloc(arena).addr
        reg = {}
        cursor = [base]

        def al(x):
            return (x + 31) // 32 * 32

        def esz(dt):
            return 4 if dt == F32 else 2

        class Buf:
            pass

        def alloc_at(name, shape, dt, off):
            nb = int(np.prod(shape[1:])) * esz(dt)
            assert off + nb <= base + ARENA_BYTES, (name, off + nb - base, ARENA_BYTES)
            return nc.alloc_sbuf_tensor_at(name, list(shape), dt, offset=off), nb

        def alloc(name, shape, dt, off=None, pieces=None):
            b = Buf()
            if off is None:
                off = cursor[0]
                adv = True
            else:
                adv = False
            b.t, nb = alloc_at(name, shape, dt, off)
            b.off = off
            b.nb = nb
            if adv:
                cursor[0] = off + (nb + 31) // 32 * 32
            b.r = {}
            if pieces is None:
                b.r[None] = Res(reg, "sb", off, off + nb, name)
                b.all = [b.r[None]]
            else:
                for key, lo, hi in pieces:
                    b.r[key] = Res(reg, "sb", off + lo * esz(dt), off + hi * esz(dt), f"{name}{key}")
                b.all = list(b.r.values())
            return b

        def grid_pieces(nrow, rowlen, ncols=NT, coloff=0):
            ps = []
            for k in range(nrow):
                for tt in range(3):
                    ps.append(((k, tt), k * rowlen + coloff + tt * TT, k * rowlen + coloff + (tt + 1) * TT))
            return ps

        xT = alloc("xT", [128, KC, NT], F32, pieces=grid_pieces(KC, NT))
        ring = [alloc(f"ring{i}", [128, SLOT], BF16) for i in range(NSLOT)]
        ident = alloc("ident", [128, 128], F32)
        maskT = alloc("maskT", [128, 128], F32)
        icnt = alloc("icnt", [128, 4, 16], F32)
        onesb = alloc("onesb", [128, 128], BF16)
        epsb = alloc("epsb", [128, 1], F32)
        pvd = alloc("pvd", [128, KC, 12], F32)
        pvf = alloc("pvf", [128, FC, 16], F32)
        wsT = alloc("wsT", [128, 2, 8, 128], BF16)
        msm = alloc("msm", [32, 2, 8, 32], BF16)
        brow = alloc("brow", [1, 2, 8, 32], BF16)
        brow.all = [Res(reg, "virt", i, i + 1, f"brow{i}") for i in range(16)]
        browM = alloc("browM", [1, EC, 128], BF16)
        browM.all = [Res(reg, "virt", 50 + i, 51 + i, f"browM{i}") for i in range(2)]
        msm.all = [Res(reg, "virt", 100 + i, 101 + i, f"msm{i}") for i in range(16)]
        poolhist = alloc("poolhist", [128, 2, KC, 15], F32, pieces=[((j,), j * KC * 15, (j + 1) * KC * 15) for j in range(2)])
        convhist = alloc("convhist", [128, DEPTH, FC, 2], F32,
                         pieces=[((l, f), (l * FC + f) * 2, (l * FC + f + 1) * 2) for l in range(DEPTH) for f in range(FC)])
        rstd = alloc("rstd", [128, NT], F32, pieces=[((tt,), tt * TT, (tt + 1) * TT) for tt in range(3)])
        sq = alloc("sq", [128, 2, NT], BF16, pieces=[((i, tt), i * NT + tt * TT, i * NT + (tt + 1) * TT) for i in range(2) for tt in range(3)])
        stg = alloc("stg", [128, 2, D], F32, pieces=[((i,), i * D, (i + 1) * D) for i in range(2)])
        stg_sub = [Res(reg, "sb", stg.off + ci * 512, stg.off + (ci + 1) * 512, f"stgsub{ci}") for ci in range(3)]
        vstat = alloc("vstat", [128, 4], F32)
        PH = cursor[0]

        hB = alloc("hB", [128, KC, NT], BF16, off=PH, pieces=grid_pieces(KC, NT))
        o1 = al(PH + hB.nb)
        UB = alloc("UB", [128, EC, NT], BF16, off=o1, pieces=grid_pieces(EC, NT))
        o = al(o1 + UB.nb)
        R2 = alloc("R2", [128, 4, KC, 512], BF16, off=o, pieces=[((n,), n * KC * 512, (n + 1) * KC * 512) for n in range(4)])
        o = al(o + R2.nb)
        vsb = alloc("vsb", [128, DV], F32, off=o); o = al(o + vsb.nb)
        vnb = [None, None]
        vnb[0] = alloc("vn0", [128, DV], BF16, off=o); o = al(o + vnb[0].nb)
        vnb[1] = alloc("vn1", [128, DV], BF16, off=o); o = al(o + vnb[1].nb)
        gvb = alloc("gvb", [128, DV], F32, off=o); o = al(o + gvb.nb)
        junk = alloc("junk", [128, DV], BF16, off=o); o = al(o + junk.nb)
        mT = alloc("mT", [128, FC, NT], BF16, off=o1, pieces=grid_pieces(FC, NT))
        o = al(o1 + mT.nb)
        asb = [None, None]
        for i in range(2):
            asb[i] = alloc(f"asb{i}", [128, 2 + NP], F32, off=o,
                           pieces=[((0,), 0, 2 + TT), ((1,), 2 + TT, 2 + 2 * TT), ((2,), 2 + 2 * TT, 2 + NP)]); o = al(o + asb[i].nb)
        c0b, c2b, gb = [None] * 3, [None] * 3, [None] * 3
        for i in range(3):
            c0b[i] = alloc(f"c0b{i}", [128, TT], F32, off=o); o = al(o + c0b[i].nb)
            c2b[i] = alloc(f"c2b{i}", [128, TT], F32, off=o); o = al(o + c2b[i].nb)
            gb[i] = alloc(f"gb{i}", [128, TT], F32, off=o); o = al(o + gb[i].nb)
        ffn_end = o
        o = (base + ARENA_BYTES - (FC * 48 * 4 + FC * 16 * 4 + 64)) // 32 * 32
        assert o >= ffn_end
        cs_lo = o
        ASb = alloc("ASb", [128, FC, 8, 6], F32, off=o, pieces=[((f,), f * 48, (f + 1) * 48) for f in range(FC)]); o = al(o + ASb.nb)
        cst = alloc("cst", [128, FC, 16], F32, off=o); o = al(o + cst.nb)
        HH = alloc("HH", [128, KC, 15 + NT], F32, off=PH,
                   pieces=[((k, 'h'), k * (15 + NT), k * (15 + NT) + 15) for k in range(KC)] + grid_pieces(KC, 15 + NT, coloff=15))
        o = al(PH + HH.nb)
        pA = [alloc(f"pA{i}", [128, 15 + NP], F32, off=o + i * al((15 + NP) * 4)) for i in range(2)]
        o += 2 * al((15 + NP) * 4)
        PW = 15 + NP + 1
        hbfB = alloc("hbfB", [128, KC, PW], BF16, off=o, pieces=[((k,), k * PW, (k + 1) * PW) for k in range(KC)]); o = al(o + hbfB.nb)
        Sb = [None] * 6
        for i in range(6):
            Sb[i] = alloc(f"Sb{i}", [128, PW], BF16, off=o); o = al(o + Sb[i].nb)
        WAb = alloc("WAb", [128, KC, 256], BF16, off=o); o = al(o + WAb.nb)
        WBb = alloc("WBb", [128, KC, 256], BF16, off=o); o = al(o + WBb.nb)
        pS = alloc("pS", [128, KC, NS], BF16, off=o, pieces=[((k,), k * NS, (k + 1) * NS) for k in range(KC)]); o = al(o + pS.nb)
        pfix = alloc("pfix", [128, KC, 16], BF16, off=o, pieces=[((k,), k * 16, (k + 1) * 16) for k in range(KC)]); o = al(o + pfix.nb)
        fx = [alloc(f"fx{i}", [128, KC, 32], F32, off=o + i * KC * 128) for i in range(2)]; o += 2 * KC * 128
        SAs = [alloc(f"SAs{i}", [128, KC, 8, 19], F32, off=o + i * al(KC * 152 * 4)) for i in range(2)]; o += 2 * al(KC * 152 * 4)
        ptmp8 = alloc("ptmp8", [128, KC, 16], F32, off=o); o = al(o + ptmp8.nb)
        HS = alloc("HS", [128, KC, 8, 19], F32, off=o, pieces=[((k,), k * 152, (k + 1) * 152) for k in range(KC)]); o = al(o + HS.nb)
        sA = [alloc(f"sA{i}", [128, 8, 19], F32, off=o + i * al(152 * 4)) for i in range(2)]
        o += 2 * al(152 * 4)
        ptmp = alloc("ptmp", [128, 16], F32, off=o); o += 64
        HSo = alloc("HSo", [128, KC, 120], F32, off=o); o = al(o + HSo.nb)
        assert o <= cs_lo, (o - PH, cs_lo - PH)

        xstg = alloc("xstg", [128, 2, D], F32, off=al(PH + 36 * 1024), pieces=[((i,), i * D, (i + 1) * D) for i in range(2)])

        psum = es.enter_context(nc.psum_tensor("psum", [128, 4096], F32))
        bank_r = [Res(reg, "ps", b * 2048, (b + 1) * 2048, f"bank{b}") for b in range(8)]

        def bank(b, c0=0, c1=512):
            return psum[:, b * 512 + c0: b * 512 + c1]

        def mksem(n):
            return es.enter_context(nc.semaphore(n))
        es.enter_context(nc.Block())
        PE = Eng("pe", nc.tensor, mksem("s_pe"), self_safe=True)
        ACT = Eng("act", nc.scalar, mksem("s_act"))
        DVE = Eng("dve", nc.vector, mksem("s_dve"))
        POOL = Eng("pool", nc.gpsimd, mksem("s_pool"))
        SP = Eng("sp", nc.sync, mksem("s_sp"))
        dsems = []

        def newdsem(n):
            d = DmaSem(mksem(n))
            dsems.append(d)
            return d
        ring_sem = [newdsem(f"d_ring{i}") for i in range(NSLOT)]
        r2_sem = [newdsem(f"d_r2{n}") for n in range(4)]
        stg_ld = [newdsem(f"d_stgl{i}") for i in range(2)]
        stg_st = [newdsem(f"d_stgs{i}") for i in range(2)]
        misc_ld = newdsem("d_misc")
        xstg_ld = [newdsem(f"d_xstg{i}") for i in range(2)]
        gvb_sem = newdsem("d_gvb")
        small_sem = newdsem("d_small")
        msm_sem = newdsem("d_msm")
        browM_sem = newdsem("d_browM")

        SP.op(lambda e: e.dma_start(out=ident.t[:], in_=c_ident), writes=ident.all, dsem=misc_ld)
        SP.op(lambda e: e.dma_start(out=maskT.t[:], in_=c_maskT), writes=maskT.all, dsem=newdsem("d_misc2"))
        SP.op(lambda e: e.dma_start(out=icnt.t[:], in_=c_icnt.rearrange("p (a b) -> p a b", b=16)), writes=icnt.all, dsem=newdsem("d_misc3"))
        POOL.op(lambda e: e.memset(onesb.t[:], 1.0), writes=onesb.all)
        POOL.op(lambda e: e.memset(epsb.t[:], EPS), writes=epsb.all)
        POOL.op(lambda e: e.memset(msm.t[:], 0.0), writes=msm.all)

        evac_flip = [0]

        def evac_copy(out_ap, in_ap, reads, writes, eng=None):
            evac_flip[0] ^= 1
            if eng is not None:
                evac_flip[0] = eng
            if evac_flip[0]:
                return ACT.op(lambda e: e.activation(out=out_ap, in_=in_ap, func=AF.Copy), reads=reads, writes=writes)
            return DVE.op(lambda e: e.tensor_copy(out=out_ap, in_=in_ap), reads=reads, writes=writes)

        s0 = stg.r[(0,)]
        SP.op(lambda e: e.dma_start(out=stg.t[0:4, 0, :], in_=norm_mix), writes=[s0], dsem=stg_ld[0])
        SP.op(lambda e: e.dma_start(out=stg.t[4:8, 0, :], in_=norm_ffn), writes=[s0], dsem=stg_ld[0])
        SP.op(lambda e: e.dma_start(out=stg.t[8:9, 0, :], in_=norm_final), writes=[s0], dsem=stg_ld[0])
        SP.op(lambda e: e.dma_start(out=stg.t[9:11, 0, :], in_=scale_b), writes=[s0], dsem=stg_ld[0])

        def tr_pvd(e):
            for kc in range(KC):
                i = e.transpose(bank(7, kc * 16, kc * 16 + 11), stg.t[0:11, 0, kc * 128:(kc + 1) * 128], ident.t[0:11, 0:11])
            return i
        PE.op(tr_pvd, reads=[s0] + ident.all, writes=[bank_r[7]])
        DVE.op(lambda e: e.tensor_copy(out=pvd.t[:, :, 0:11], in_=bank(7, 0, 128).rearrange("p (k c) -> p k c", c=16)[:, :, 0:11]),
               reads=[bank_r[7]], writes=pvd.all)
        stream = []

        def unit_cols(w2d, c0, cw, nk):
            return w2d.rearrange("(k p) f -> p k f", p=128)[:, :, c0:c0 + cw]

        def plan_units(g):
            for l in layers:
                j = l // 2
                if do_mixer:
                    if l % 2 == 0:
                        for u in range(4):
                            stream.append(dict(kind="ring", key=("Uu", g, l, u), src=unit_cols(w_in_a[j], u * 512, 512, KC), shp=(KC, 512)))
                        for q in range(4):
                            stream.append(dict(kind="ring", key=("Wo", g, l, q), src=unit_cols(w_out_a[j], q * 256, 256, EC), shp=(EC, 256)))
                            if q == 0:
                                for u in range(4):
                                    stream.append(dict(kind="r2", key=("R2", g, l, u), src=unit_cols(w_in_a[j], DV + u * 512, 512, KC), u=u))
                    else:
                        stream.append(dict(kind="ring", key=("Wp", g, l, 0), src=unit_cols(w_pool_b[j], 0, 256, KC), shp=(KC, 256)))
                if do_ffn:
                    for u in range(6):
                        cw = 512 if u < 5 else 256
                        stream.append(dict(kind="ring", key=("Wg", g, l, u), src=unit_cols(w_gate[l], u * 512, cw, KC), shp=(KC, cw)))
                        stream.append(dict(kind="ring", key=("Wv", g, l, u), src=unit_cols(w_val[l], u * 512, cw, KC), shp=(KC, cw)))
                    for dc in range(KC):
                        stream.append(dict(kind="ring", key=("Wd", g, l, dc), src=unit_cols(w_down[l], dc * 128, 128, FC), shp=(FC, 128)))
        for g in range(ngroups):
            plan_units(g)
        ring_order = [u for u in stream if u["kind"] == "ring"]
        for i, u in enumerate(ring_order):
            u["ridx"] = i
        unit_by_key = {u["key"]: u for u in stream}
        spos = [0]

        def pump():
            while spos[0] < len(stream):
                nx = stream[spos[0]]
                if nx["kind"] == "ring":
                    k = nx["ridx"]
                    if k >= NSLOT and not ring_order[k - NSLOT].get("released"):
                        break
                    sl = k % NSLOT
                    a, b = nx["shp"]
                    dst = ring[sl].t[:, 0:a * b].rearrange("p (a b) -> p a b", b=b)
                    POOL.op(lambda e: e.dma_start(out=dst, in_=nx["src"]), writes=ring[sl].all, dsem=ring_sem[sl])
                    nx["view"] = dst
                    nx["res"] = ring[sl].all
                else:
                    n = nx["u"]
                    POOL.op(lambda e: e.dma_start(out=R2.t[:, n, :, :], in_=nx["src"]), writes=[R2.r[(n,)]], dsem=r2_sem[n])
                nx["issued"] = True
                spos[0] += 1

        def ensure(key):
            pump()
            u = unit_by_key[key]
            assert u.get("issued"), key
            return u

        def release(key):
            unit_by_key[key]["released"] = True
            pump()

        pump()
        def setup_part2():
            for ci, c0 in enumerate(range(0, DFF, 1024)):
                cw = min(1024, DFF - c0)
                si = (ci + 1) % 2
                sr = stg.r[(si,)]
                SP.op(lambda e: e.dma_start(out=stg.t[0:12, si, 0:cw], in_=conv_w[:, c0:c0 + cw]), writes=[sr], dsem=stg_ld[si])
                SP.op(lambda e: e.dma_start(out=stg.t[12:16, si, 0:cw], in_=conv_b[:, c0:c0 + cw]), writes=[sr], dsem=stg_ld[si])
                nf = cw // 128
                bk = 5 + (ci % 2)

                def tr_pvf(e):
                    for j in range(nf):
                        i = e.transpose(bank(bk, j * 16, j * 16 + 16), stg.t[0:16, si, j * 128:(j + 1) * 128], ident.t[0:16, 0:16])
                    return i
                PE.op(tr_pvf, reads=[sr] + ident.all, writes=[bank_r[bk]])
                f0 = c0 // 128
                DVE.op(lambda e: e.tensor_copy(out=pvf.t[:, f0:f0 + nf, :], in_=bank(bk, 0, nf * 16).rearrange("p (k c) -> p k c", c=16)),
                       reads=[bank_r[bk]], writes=pvf.all)
            for l in range(2):
                si = l % 2
                sr = stg.r[(si,)]
                SP.op(lambda e: e.dma_start(out=stg.t[:, si, :].rearrange("p (h j) -> p h j", j=128),
                                            in_=w_s_a[l].rearrange("h i j -> i h j")), writes=[sr], dsem=stg_ld[si])
                for hb in range(2):
                    bk = 5 + hb

                    def tr_ws(e):
                        for j in range(4):
                            hh = hb * 4 + j
                            i = e.transpose(bank(bk, j * 128, (j + 1) * 128), stg.t[:, si, hh * 128:(hh + 1) * 128], ident.t[:])
                        return i
                    PE.op(tr_ws, reads=[sr] + ident.all, writes=[bank_r[bk]])
                    for j in range(4):
                        hh = hb * 4 + j
                        DVE.op(lambda e: e.tensor_tensor(out=wsT.t[:, l, hh, :], in0=bank(bk, j * 128, (j + 1) * 128), in1=maskT.t[:], op=ALU.mult),
                               reads=[bank_r[bk]] + maskT.all, writes=wsT.all)
                for s in range(8):
                    SP.op(lambda e: e.dma_start(out=msm.t[4 * s:4 * s + 4, l, :, 4 * s:4 * s + 4], in_=wsT.t[0:4, l, :, 0:4]),
                          reads=wsT.all, writes=[msm.all[l * 8 + s]], dsem=msm_sem)

            for l in range(2):
                for s in range(8):
                    POOL.op(lambda e: e.dma_start(out=brow.t[0:1, l, :, 4 * s:4 * s + 4],
                                                  in_=b_s_a[l:l + 1, :].rearrange("o (h i) -> o h i", i=128)[:, :, 0:4]),
                            writes=[brow.all[l * 8 + s]], dsem=small_sem)

        SSQ_BANKS = [0, 1, 2]

        def norm_square(kc, tt=None):
            i = kc % 2
            if tt is None:
                ACT.op(lambda e: e.activation(out=sq.t[:, i, :], in_=xT.t[:, kc, :], func=AF.Square),
                       reads=[xT.r[(kc, t)] for t in range(3)], writes=[sq.r[(i, t)] for t in range(3)])
            else:
                cs = slice(tt * TT, (tt + 1) * TT)
                ACT.op(lambda e: e.activation(out=sq.t[:, i, cs], in_=xT.t[:, kc, cs], func=AF.Square),
                       reads=[xT.r[(kc, tt)]], writes=[sq.r[(i, tt)]])

        def norm_mm(kc, tt=None):
            i = kc % 2
            tts = range(3) if tt is None else [tt]

            def f(e):
                for t in tts:
                    ins = e.matmul(bank(SSQ_BANKS[t], 0, TT), lhsT=onesb.t[:], rhs=sq.t[:, i, t * TT:(t + 1) * TT],
                                   start=(kc == 0), stop=(kc == KC - 1))
                return ins
            PE.op(f, reads=[sq.r[(i, t)] for t in tts] + onesb.all, writes=[bank_r[SSQ_BANKS[t]] for t in tts])

        rstd_ready = [False]

        def norm_finish(gidx, dst, dst_res, dst_f32_off=None, kc_major=False):
            off = 0 if dst_f32_off is None else dst_f32_off
            if not rstd_ready[0]:
                for tt in range(3):
                    norm_rstd(tt)
            rstd_ready[0] = False
            order = [(tt, kc) for tt in range(3) for kc in range(KC)]
            if kc_major:
                order = [(tt, kc) for kc in range(KC) for tt in range(3)]
            for tt, kc in order:
                cs = slice(tt * TT, (tt + 1) * TT)
                ds = slice(off + tt * TT, off + (tt + 1) * TT)
                DVE.op(lambda e: e.scalar_tensor_tensor(out=dst[:, kc, ds], in0=xT.t[:, kc, cs], scalar=pvd.t[:, kc, gidx:gidx + 1],
                                                        in1=rstd.t[:, cs], op0=ALU.mult, op1=ALU.mult),
                       reads=[xT.r[(kc, tt)], rstd.r[(tt,)]] + pvd.all, writes=[dst_res[(kc, tt)]])

        def full_norm_stats():
            for kc in range(KC):
                norm_square(kc)
                norm_mm(kc)

        def norm_rstd(tt):
            cs = slice(tt * TT, (tt + 1) * TT)
            ACT.op(lambda e: e.activation(out=rstd.t[:, cs], in_=bank(SSQ_BANKS[tt], 0, TT), func=AF.Ln, bias=epsb.t[:, 0:1], scale=1.0 / D),
                   reads=[bank_r[SSQ_BANKS[tt]]] + epsb.all, writes=[rstd.r[(tt,)]])
            ACT.op(lambda e: e.activation(out=rstd.t[:, cs], in_=rstd.t[:, cs], func=AF.Exp, scale=-0.5),
                   reads=[rstd.r[(tt,)]], writes=[rstd.r[(tt,)]])

        class FinalPhase:
            def __init__(self):
                self.pend = None
                self.rstd_done = False

            def chunk(self, dc, mm_emit, evac_emit, hooks=None):
                last = (dc == KC - 1)
                for tt in range(3):
                    if hooks and tt in hooks:
                        hooks[tt]()
                    bk = ACC_BANKS[acc_flip[0]]
                    acc_flip[0] ^= 1
                    mm_emit(dc, tt, bk)
                    if self.pend is not None and tt == 0:
                        norm_mm(self.pend)
                        self.pend = None
                    evac_emit(dc, tt, bk)
                    if last:
                        norm_square(dc, tt)
                        if tt >= 1:
                            norm_mm(dc, tt - 1)
                            norm_rstd(tt - 1)
                if last:
                    norm_mm(dc, 2)
                    norm_rstd(2)
                    rstd_ready[0] = True
                    if keep_warm:
                        def warm(e):
                            for _ in range(keep_warm):
                                i = e.matmul(bank(ACC_BANKS[0], 0, TT), lhsT=onesb.t[:], rhs=sq.t[:, 0, 0:TT], start=True, stop=True)
                            return i
                        PE.op(warm, reads=[sq.r[(0, 0)]] + onesb.all, writes=[bank_r[ACC_BANKS[0]]])
                else:
                    norm_square(dc)
                    self.pend = dc

        ACC_BANKS = [3, 4]
        acc_flip = [0]

        def load_x_dma(g, c):
            p0, sr0 = g * NP, g * NS
            si = c % 2
            sres = xstg.r[(si,)]
            if c < 8:
                SP.op(lambda e: e.dma_start(out=xstg.t[:, si, :], in_=x_prompt[p0 + c * 128: p0 + (c + 1) * 128, :]), writes=[sres], dsem=xstg_ld[si])
            else:
                SP.op(lambda e: e.dma_start(out=xstg.t[0:NS, si, :], in_=x_sample[sr0:sr0 + NS, :]), writes=[sres], dsem=xstg_ld[si])

        def load_x_tr(g, c):
            si = c % 2
            sres = xstg.r[(si,)]
            ntok = 128 if c < 8 else NS
            col0 = c * 128
            for hb in range(2):
                bk = [7, 4][hb]

                def trx(e):
                    for j in range(4):
                        kc = hb * 4 + j
                        i = e.transpose(bank(bk, j * 128, j * 128 + ntok), xstg.t[0:ntok, si, kc * 128:(kc + 1) * 128], ident.t[0:ntok, 0:ntok])
                    return i
                PE.op(trx, reads=[sres] + ident.all, writes=[bank_r[bk]])
                wr = [xT.r[(hb * 4 + j, tt)] for j in range(4) for tt in tiles_of(col0, col0 + ntok)]
                evac_copy(xT.t[:, hb * 4:hb * 4 + 4, col0:col0 + ntok],
                          bank(bk).rearrange("p (k c) -> p k c", c=128)[:, :, 0:ntok], [bank_r[bk]], wr)

        def store_y_chunk(g, c):
            p0, sr0 = g * NP, g * NS
            si = c % 2
            sres = stg.r[(si,)]
            ntok = 128 if c < 8 else NS
            col0 = c * 128
            for hb in range(2):
                bk = 5 + hb

                def try_(e):
                    for jj in range(4):
                        kc = hb * 4 + jj
                        i = e.transpose(bank(bk, jj * 128, (jj + 1) * 128)[0:ntok, :], HH.t[:, kc, 15 + col0:15 + col0 + ntok], ident.t[:])
                    return i
                PE.op(try_, reads=[HH.r[(hb * 4 + jj, tt)] for jj in range(4) for tt in tiles_of(col0, col0 + ntok)] + ident.all, writes=[bank_r[bk]])
                evac_copy(stg.t[0:ntok, si, hb * 512:(hb + 1) * 512], bank(bk)[0:ntok, :], [bank_r[bk]], [sres])

        def store_y_dma(g, c):
            p0, sr0 = g * NP, g * NS
            si = c % 2
            sres = stg.r[(si,)]
            if c < 8:
                SP.op(lambda e: e.dma_start(out=y_prompt[p0 + c * 128:p0 + (c + 1) * 128, :], in_=stg.t[:, si, :]), reads=[sres], dsem=stg_st[si])
            else:
                SP.op(lambda e: e.dma_start(out=y_sample[sr0:sr0 + NS, :], in_=stg.t[0:NS, si, :]), reads=[sres], dsem=stg_st[si])

        for g in range(ngroups):
            p0 = g * NP
            sr0 = g * NS
            sq0 = g * 8

            if g == 0:
                load_x_dma(0, 0)
                for c in range(9):
                    if c + 1 < 9:
                        load_x_dma(0, c + 1)
                    load_x_tr(0, c)
            full_norm_stats()
            if g == 0:
                setup_part2()

            pool_state_loaded = [False]

            def load_pool_state_dma(j):
                SP.op(lambda e: e.dma_start(out=stg.t[0:120, 0, :], in_=state_pool[j, sq0 * 15:(sq0 + 8) * 15, :]), writes=[stg.r[(0,)]], dsem=stg_ld[0])
                pool_state_loaded[0] = True

            def load_conv_state(l, eng=None):
                for ci, c0 in enumerate(range(0, DFF, 1024)):
                    cw = min(1024, DFF - c0)
                    nf = cw // 128
                    si = ci % 2
                    SP.op(lambda e: e.dma_start(out=stg.t[0:16, si, 0:cw], in_=state_ffn[l, sq0 * 2:(sq0 + 8) * 2, c0:c0 + cw]),
                          writes=[stg.r[(si,)]], dsem=stg_ld[si])
                    bk = 5 + si

                    def trst(e):
                        for jj in range(nf):
                            i = e.transpose(bank(bk, jj * 16, jj * 16 + 16), stg.t[0:16, si, jj * 128:(jj + 1) * 128], ident.t[0:16, 0:16])
                        return i
                    PE.op(trst, reads=[stg.r[(si,)]] + ident.all, writes=[bank_r[bk]])
                    f0 = c0 // 128
                    evac_copy(ASb.t[:, f0:f0 + nf, :, 0:2], bank(bk, 0, nf * 16).rearrange("p (f s r) -> p f s r", s=8, r=2),
                              [bank_r[bk]], [ASb.r[(f,)] for f in range(f0, f0 + nf)], eng=eng)

            for l in layers:
                j = l // 2
                if do_mixer and l % 2 == 0:
                    gmlp_layer = True
                else:
                    gmlp_layer = False
                if do_mixer and gmlp_layer:
                    norm_finish(l, hB.t, hB.r)
                    SP.op(lambda e: e.dma_start(out=gvb.t[:], in_=g_v_a[j].partition_broadcast(128)), writes=gvb.all, dsem=gvb_sem)
                    for r in range(2):
                        POOL.op(lambda e: e.dma_start(out=browM.t[0:1, :, :].rearrange("o (h r) i -> o h r i", r=2)[:, :, r, :],
                                                      in_=b_s_a[j:j + 1, :].rearrange("o (h i) -> o h i", i=128)),
                                writes=[browM.all[r]], dsem=browM_sem)
                    ubanks = [5, 6, 7, 3]
                    step = 0
                    for ec in range(EC):
                        u = ensure(("Uu", g, l, ec // 4))
                        wv = u["view"]
                        for tt in range(3):
                            bk = ubanks[step % 4]
                            step += 1
                            cs = slice(tt * TT, (tt + 1) * TT)

                            def mmu(e):
                                for kc in range(KC):
                                    i = e.matmul(bank(bk, 0, TT), lhsT=wv[:, kc, (ec % 4) * 128:(ec % 4 + 1) * 128], rhs=hB.t[:, kc, cs],
                                                 start=(kc == 0), stop=(kc == KC - 1))
                                return i
                            PE.op(mmu, reads=u["res"] + [hB.r[(kc, tt)] for kc in range(KC)], writes=[bank_r[bk]])
                            ACT.op(lambda e: e.activation(out=UB.t[:, ec, cs], in_=bank(bk, 0, TT), func=AF.Gelu_apprx_tanh),
                                   reads=[bank_r[bk]], writes=[UB.r[(ec, tt)]])
                        if ec % 4 == 3:
                            release(("Uu", g, l, ec // 4))
                    def v_mm(c):
                        ntok = 128 if c < 8 else NS
                        col0 = c * 128
                        for half in range(2):
                            def f(e):
                                for n in (2 * half, 2 * half + 1):
                                    for kc in range(KC):
                                        i = e.matmul(psum[0:ntok, n * 512:(n + 1) * 512], lhsT=hB.t[:, kc, col0:col0 + ntok],
                                                     rhs=R2.t[:, n, kc, :], start=(kc == 0), stop=(kc == KC - 1))
                                return i
                            PE.op(f, reads=[R2.r[(2 * half,)], R2.r[(2 * half + 1,)]] + [hB.r[(kc, tt)] for kc in range(KC) for tt in tiles_of(col0, col0 + ntok)],
                                  writes=bank_r[2 * half:2 * half + 2])
                            hs_ = slice(half * 1024, (half + 1) * 1024)
                            ACT.op(lambda e: e.activation(out=vsb.t[0:ntok, hs_], in_=psum[0:ntok, hs_], func=AF.Gelu_apprx_tanh),
                                   reads=bank_r[2 * half:2 * half + 2], writes=vsb.all)

                    def v_elem(c):
                        ntok = 128 if c < 8 else NS
                        vb = vnb[c % 2]
                        ACT.op(lambda e: e.activation(out=junk.t[0:ntok, :], in_=vsb.t[0:ntok, :], func=AF.Square, accum_out=vstat.t[0:ntok, 0:1]),
                               reads=vsb.all, writes=junk.all + vstat.all)
                        ACT.op(lambda e: e.activation(out=vstat.t[0:ntok, 1:2], in_=vstat.t[0:ntok, 0:1], func=AF.Sqrt, bias=epsb.t[0:ntok, 0:1], scale=1.0 / DV),
                               reads=vstat.all + epsb.all, writes=vstat.all)
                        DVE.op(lambda e: e.reciprocal(out=vstat.t[0:ntok, 2:3], in_=vstat.t[0:ntok, 1:2]), reads=vstat.all, writes=vstat.all)
                        DVE.op(lambda e: e.scalar_tensor_tensor(out=vb.t[0:ntok, :], in0=vsb.t[0:ntok, :], scalar=vstat.t[0:ntok, 2:3], in1=gvb.t[0:ntok, :],
                                                                op0=ALU.mult, op1=ALU.mult),
                               reads=vsb.all + vstat.all + gvb.all, writes=vb.all)
                        if c == 8:
                            so = stg.t[0:NS, :, :].rearrange("p a b -> p (a b)")
                            DVE.op(lambda e: e.scalar_tensor_tensor(out=so, in0=vsb.t[0:NS, :], scalar=vstat.t[0:NS, 2:3], in1=gvb.t[0:NS, :],
                                                                    op0=ALU.mult, op1=ALU.mult),
                                   reads=vsb.all + vstat.all + gvb.all, writes=stg.all)
                            SP.op(lambda e: e.dma_start(out=gmlp_v[j, sr0:sr0 + NS, :], in_=so), reads=stg.all, dsem=stg_st[0])

                    def s_mm(c):
                        ntok = 128 if c < 8 else NS
                        vb = vnb[c % 2]

                        def f(e):
                            if c < 8:
                                for n in range(4):
                                    e.matmul(psum[:, DV + n * 512: DV + (n + 1) * 512], lhsT=onesb.t[0:1, 0:128],
                                             rhs=browM.t[0:1, 4 * n:4 * n + 4, :].rearrange("o a b -> o (a b)"), start=True, stop=False, skip_group_check=True)
                                for ec in range(EC):
                                    i = e.matmul(psum[:, DV + ec * 128: DV + (ec + 1) * 128], lhsT=vb.t[:, ec * 128:(ec + 1) * 128],
                                                 rhs=wsT.t[:, j, ec // 2, :], start=False, stop=True, skip_group_check=True)
                                return i
                            for ec in range(EC):
                                hh = ec // 2
                                o_ap = bank(7, ec * NS, (ec + 1) * NS)
                                e.matmul(o_ap, lhsT=vb.t[0:ntok, ec * 128:(ec + 1) * 128], rhs=msm.t[0:NS, j, hh, :], start=True, stop=False)
                                i = e.matmul(o_ap, lhsT=onesb.t[0:1, 0:128], rhs=brow.t[0:1, j, hh, :], start=False, stop=True)
                            return i
                        PE.op(f, reads=vb.all + wsT.all + msm.all + brow.all + browM.all + onesb.all, writes=(bank_r[4:8] if c < 8 else [bank_r[7]]))

                    def s_elem(c):
                        ntok = 128 if c < 8 else NS
                        col0 = c * 128
                        urs = [UB.r[(ec, tt)] for ec in range(EC) for tt in tiles_of(col0, col0 + ntok)]
                        if c < 8:
                            s_in, s_rd = psum[:, DV:2 * DV].rearrange("p (a b) -> p a b", b=128), bank_r[4:8]
                        else:
                            s_in, s_rd = bank(7).rearrange("p (a b) -> p a b", b=NS), [bank_r[7]]
                        DVE.op(lambda e: e.tensor_tensor(out=UB.t[:, :, col0:col0 + ntok], in0=s_in, in1=UB.t[:, :, col0:col0 + ntok], op=ALU.mult),
                               reads=s_rd + urs, writes=urs)
                    v_mm(0)
                    v_elem(0)
                    for c in range(1, 9):
                        v_mm(c)
                        s_mm(c - 1)
                        s_elem(c - 1)
                        v_elem(c)
                    fp = FinalPhase()
                    for dc in range(KC):
                        u = ensure(("Wo", g, l, dc // 2))
                        wv = u["view"]

                        def mm_emit(dc, tt, bk):
                            cs = slice(tt * TT, (tt + 1) * TT)

                            def mmo(e):
                                for ec in range(EC):
                                    i = e.matmul(bank(bk, 0, TT), lhsT=wv[:, ec, (dc % 2) * 128:(dc % 2 + 1) * 128], rhs=UB.t[:, ec, cs],
                                                 start=(ec == 0), stop=(ec == EC - 1))
                                return i
                            PE.op(mmo, reads=u["res"] + [UB.r[(ec, tt)] for ec in range(EC)], writes=[bank_r[bk]])

                        def evac_emit(dc, tt, bk):
                            cs = slice(tt * TT, (tt + 1) * TT)
                            DVE.op(lambda e: e.tensor_tensor(out=xT.t[:, dc, cs], in0=bank(bk, 0, TT), in1=xT.t[:, dc, cs], op=ALU.add),
                                   reads=[bank_r[bk], xT.r[(dc, tt)]], writes=[xT.r[(dc, tt)]])
                        hk = None
                        if dc == 0:
                            hk = {2: lambda: (s_mm(8), s_elem(8))}
                        elif dc == 2 and do_ffn:
                            hk = {0: lambda: load_conv_state(l)}
                        fp.chunk(dc, mm_emit, evac_emit, hooks=hk)
                        if dc % 2 == 1:
                            release(("Wo", g, l, dc // 2))
                elif do_mixer:
                    if g == 0:
                        POOL.op(lambda e: e.memset(HH.t[:, :, 0:15], 0.0), writes=[HH.r[(k, 'h')] for k in range(KC)])
                    else:
                        DVE.op(lambda e: e.tensor_copy(out=HH.t[:, :, 0:15], in_=poolhist.t[:, j, :, :]),
                               reads=[poolhist.r[(j,)]], writes=[HH.r[(k, 'h')] for k in range(KC)])
                    norm_finish(l, HH.t, HH.r, dst_f32_off=15, kc_major=True)
                    wp = ensure(("Wp", g, l, 0))
                    wpv = wp["view"]
                    coefA = [-0.5, 0.25, 0.125, 0.0625]
                    coefB = [0.5, -1.0, -1.0, -1.0]
                    for gi in range(4):
                        ACT.op(lambda e: e.activation(out=WAb.t[:, 2 * gi:2 * gi + 2, :], in_=wpv[:, 2 * gi:2 * gi + 2, :], func=AF.Copy, scale=coefA[gi]),
                               reads=wp["res"], writes=WAb.all)
                        ACT.op(lambda e: e.activation(out=WBb.t[:, 2 * gi:2 * gi + 2, :], in_=wpv[:, 2 * gi:2 * gi + 2, :], func=AF.Copy, scale=coefB[gi]),
                               reads=wp["res"], writes=WBb.all)

                    def cast_h(kc):
                        ACT.op(lambda e: e.activation(out=hbfB.t[:, kc, 0:15 + NP], in_=HH.t[:, kc, 0:15 + NP], func=AF.Copy),
                               reads=[HH.r[(kc, 'h')]] + [HH.r[(kc, tt)] for tt in range(3)], writes=[hbfB.r[(kc,)]])
                    cast_h(0)
                    cast_h(1)
                    if not pool_state_loaded[0]:
                        load_pool_state_dma(j)
                    pool_state_loaded[0] = False
                    for hb in range(2):
                        bk = 5 + hb

                        def trs(e):
                            for jj in range(4):
                                kc = hb * 4 + jj
                                i = e.transpose(bank(bk, jj * 128, jj * 128 + 120), stg.t[0:120, 0, kc * 128:(kc + 1) * 128], ident.t[0:120, 0:120])
                            return i
                        PE.op(trs, reads=[stg.r[(0,)]] + ident.all, writes=[bank_r[bk]])
                        for jj in range(4):
                            kc = hb * 4 + jj
                            evac_copy(HS.t[:, kc, :, 0:15], bank(bk, jj * 128, jj * 128 + 120).rearrange("p (s r) -> p s r", r=15),
                                      [bank_r[bk]], [HS.r[(kc,)]], eng=1)
                    for kc in range(2, KC):
                        cast_h(kc)
                    pfp = FinalPhase()
                    fix0 = 16 if g == 0 else 0

                    def pool_terms(gi, k):
                        hsrc = (hbfB.t[:, k, :], [hbfB.r[(k,)]])
                        if gi == 0:
                            return [(hsrc, 0, WAb), (hsrc, 1, WBb)]
                        sb = Sb[k - 2]
                        ssrc = (sb.t[:, :], sb.all)
                        half = 2 ** gi
                        return [(ssrc, 0, WAb), (ssrc, half, WAb), (hsrc, 0, WBb)]

                    def pool_mm(ec):
                        gi = ec // 2
                        eo = (ec % 2) * 128

                        def mm_emit(ec, tt, bk):
                            c0 = tt * TT + (fix0 if tt == 0 else 0)
                            c1 = min((tt + 1) * TT, NP)
                            rds = list(wp["res"]) + WAb.all + WBb.all
                            plan = []
                            grp = []
                            for cc in range(2):
                                k = gi * 2 + cc
                                for (src, srcres), sh, W in pool_terms(gi, k):
                                    grp.append((W.t[:, k, eo:eo + 128], src[:, 15 + c0 - sh:15 + c1 - sh]))
                                    rds += srcres
                            plan.append((bank(bk, c0 - tt * TT, c1 - tt * TT), grp))
                            if tt == 0 and fix0:
                                plan.append((bank(bk, 0, fix0), [(wpv[:, gi * 2 + cc, eo:eo + 128], pfix.t[:, gi * 2 + cc, :]) for cc in range(2)]))
                                rds += [pfix.r[(gi * 2 + cc,)] for cc in range(2)]
                            if tt == 2:
                                plan.append((bank(bk, NP - 2 * TT, TT), [(wpv[:, gi * 2 + cc, eo:eo + 128], pS.t[:, gi * 2 + cc, :]) for cc in range(2)]))
                                rds += [pS.r[(gi * 2 + cc,)] for cc in range(2)]

                            def mmp(e):
                                for o_ap, lst in plan:
                                    for n, (lt, rh) in enumerate(lst):
                                        i = e.matmul(o_ap, lhsT=lt, rhs=rh, start=(n == 0), stop=(n == len(lst) - 1))
                                return i
                            PE.op(mmp, reads=rds, writes=[bank_r[bk]])

                        def evac_emit(ec, tt, bk):
                            cs = slice(tt * TT, (tt + 1) * TT)
                            DVE.op(lambda e: e.scalar_tensor_tensor(out=xT.t[:, ec, cs], in0=bank(bk, 0, TT), scalar=pvd.t[:, ec, 9 + j:10 + j], in1=xT.t[:, ec, cs],
                                                                    op0=ALU.mult, op1=ALU.add),
                                   reads=[bank_r[bk], xT.r[(ec, tt)]] + pvd.all, writes=[xT.r[(ec, tt)]])
                        pfp.chunk(ec, mm_emit, evac_emit)
                    allHH = [HH.r[(k, 'h')] for k in range(KC)] + [HH.r[(k, tt)] for k in range(KC) for tt in range(3)]
                    DVE.op(lambda e: e.tensor_copy(out=HS.t[:, :, :, 15:19], in_=HH.t[:, :, 15 + NP:15 + NT].rearrange("p k (s t) -> p k s t", t=4)),
                           reads=[HH.r[(k, 2)] for k in range(KC)], writes=HS.all)
                    DVE.op(lambda e: e.tensor_copy(out=poolhist.t[:, j, :, :], in_=HH.t[:, :, NP:NP + 15]),
                           reads=[HH.r[(k, 2)] for k in range(KC)], writes=[poolhist.r[(j,)]])
                    for st in range(4):
                        k0 = 2 * st
                        sh = 2 ** st
                        lo = 2 ** (st + 1) - 1
                        w = 2 ** (st + 1)
                        if st == 0:
                            sa, sar = HS.t, HS.all
                            fa, far = HH.t, allHH
                        else:
                            sa, sar = SAs[(st - 1) % 2].t, SAs[(st - 1) % 2].all
                            fa, far = fx[(st - 1) % 2].t, fx[(st - 1) % 2].all
                        sd = SAs[st % 2]
                        DVE.op(lambda e: e.tensor_tensor(out=sd.t[:, k0:KC, :, lo:19], in0=sa[:, k0:KC, :, lo:19], in1=sa[:, k0:KC, :, lo - sh:19 - sh], op=ALU.add),
                               reads=sar, writes=sd.all)
                        DVE.op(lambda e: e.scalar_tensor_tensor(out=pS.t[:, k0:k0 + 2, :].rearrange("p k (s t) -> p k s t", t=4), in0=sd.t[:, k0:k0 + 2, :, 15:19],
                                                                scalar=1.0 / w, in1=HS.t[:, k0:k0 + 2, :, 15:19], op0=ALU.mult, op1=ALU.subtract),
                               reads=sd.all + HS.all, writes=[pS.r[(k0,)], pS.r[(k0 + 1,)]])
                        if fix0:
                            fd = fx[st % 2]
                            DVE.op(lambda e: e.tensor_tensor(out=fd.t[:, k0:KC, lo:31], in0=fa[:, k0:KC, lo:31], in1=fa[:, k0:KC, lo - sh:31 - sh], op=ALU.add),
                                   reads=far, writes=fd.all)
                            for kc in (k0, k0 + 1):
                                DVE.op(lambda e: e.tensor_tensor(out=ptmp8.t[:, kc, :], in0=fd.t[:, kc, 15:31], in1=icnt.t[:, st, :], op=ALU.mult),
                                       reads=fd.all + icnt.all, writes=ptmp8.all)
                                DVE.op(lambda e: e.tensor_tensor(out=pfix.t[:, kc, :], in0=ptmp8.t[:, kc, :], in1=HH.t[:, kc, 15:31], op=ALU.subtract),
                                       reads=ptmp8.all + [HH.r[(kc, 0)]], writes=[pfix.r[(kc,)]])
                    def s_adds(kc):
                        gi = kc // 2
                        src = HH.t[:, kc, 0:15 + NP]
                        srcr = [HH.r[(kc, 'h')]] + [HH.r[(kc, tt)] for tt in range(3)]
                        sb = Sb[kc - 2]
                        cur, curr = src, srcr
                        for st in range(gi):
                            sh = 2 ** st
                            lo = 2 ** (st + 1) - 1
                            dstb = sb if st == gi - 1 else pA[st % 2]
                            a, b = cur, dstb.t
                            DVE.op(lambda e: e.tensor_tensor(out=b[:, lo:15 + NP], in0=a[:, lo:15 + NP], in1=a[:, lo - sh:15 + NP - sh], op=ALU.add),
                                   reads=curr, writes=dstb.all)
                            cur, curr = dstb.t, dstb.all
                    if pool_dbg >= 2:
                        s_adds(2)
                        s_adds(3)
                        pool_mm(0)
                        pool_mm(1)
                        s_adds(4)
                        s_adds(5)
                        pool_mm(2)
                        pool_mm(3)
                        s_adds(6)
                        s_adds(7)
                        pool_mm(4)
                        pool_mm(5)
                        if do_ffn:
                            load_conv_state(l, eng=1)
                        pool_mm(6)
                        pool_mm(7)
                    release(("Wp", g, l, 0))
                    DVE.op(lambda e: e.tensor_copy(out=HSo.t[:, :, :].rearrange("p k (s r) -> p k s r", r=15), in_=HS.t[:, :, :, 4:19]),
                           reads=HS.all, writes=HSo.all)
                    for hb in range(2 if pool_dbg >= 3 else 0):
                        bk = 5 + hb

                        def trps(e):
                            for jj in range(4):
                                kc = hb * 4 + jj
                                i = e.transpose(bank(bk, jj * 128, (jj + 1) * 128)[0:120, :], HSo.t[:, kc, :], ident.t[:])
                            return i
                        PE.op(trps, reads=HSo.all + ident.all, writes=[bank_r[bk]])
                        evac_copy(stg.t[0:120, 1, hb * 512:(hb + 1) * 512], bank(bk)[0:120, :], [bank_r[bk]], [stg.r[(1,)]])
                    if pool_dbg >= 3:
                        SP.op(lambda e: e.dma_start(out=pool_sample[j, sq0 * 15:(sq0 + 8) * 15, :], in_=stg.t[0:120, 1, :]), reads=[stg.r[(1,)]], dsem=stg_st[1])
                    if g == ngroups - 1 and pool_dbg >= 4:
                        for hb in range(2):
                            bk = 5 + hb

                            def trpp(e):
                                for jj in range(4):
                                    kc = hb * 4 + jj
                                    i = e.transpose(bank(bk, jj * 128, (jj + 1) * 128)[0:15, :], poolhist.t[:, j, kc, :], ident.t[:])
                                return i
                            PE.op(trpp, reads=[poolhist.r[(j,)]] + ident.all, writes=[bank_r[bk]])
                            evac_copy(stg.t[0:15, 0, hb * 512:(hb + 1) * 512], bank(bk)[0:15, :], [bank_r[bk]], [stg.r[(0,)]])
                        SP.op(lambda e: e.dma_start(out=pool_prompt[j, :, :], in_=stg.t[0:15, 0, :]), reads=[stg.r[(0,)]], dsem=stg_st[0])
                if do_ffn:
                    if not (do_mixer):
                        load_conv_state(l)
                    norm_finish(4 + l, hB.t, hB.r)
                    gbanks = [0, 1, 2]
                    vbanks = [5, 6, 7]
                    steps = [(fc, tt) for fc in range(FC) for tt in range(3)]

                    def stage1(i):
                        fc, tt = steps[i]
                        ug = ensure(("Wg", g, l, fc // 4))
                        uv = ensure(("Wv", g, l, fc // 4))
                        gbk, vbk = gbanks[i % 3], vbanks[i % 3]
                        cs = slice(tt * TT, (tt + 1) * TT)
                        fo = (fc % 4) * 128
                        hr = [hB.r[(kc, tt)] for kc in range(KC)]

                        def mmg(e):
                            for kc in range(KC):
                                ins = e.matmul(bank(gbk, 0, TT), lhsT=ug["view"][:, kc, fo:fo + 128], rhs=hB.t[:, kc, cs], start=(kc == 0), stop=(kc == KC - 1))
                            return ins
                        PE.op(mmg, reads=ug["res"] + hr, writes=[bank_r[gbk]])

                        def mmv(e):
                            for kc in range(KC):
                                ins = e.matmul(bank(vbk, 0, TT), lhsT=uv["view"][:, kc, fo:fo + 128], rhs=hB.t[:, kc, cs], start=(kc == 0), stop=(kc == KC - 1))
                            return ins
                        PE.op(mmv, reads=uv["res"] + hr, writes=[bank_r[vbk]])
                        ab = asb[fc % 2]
                        abw = [ab.r[(tt,)]]
                        abr = [ab.r[(tt,)]] + ([ab.r[(tt - 1,)]] if tt > 0 else [])
                        npr = TT if tt < 2 else NP - 2 * TT
                        pc0 = tt * TT
                        if tt == 0:
                            if g == 0:
                                DVE.op(lambda e: e.memset(ab.t[:, 0:2], 0.0), writes=abw)
                            else:
                                ACT.op(lambda e: e.activation(out=ab.t[:, 0:2], in_=convhist.t[:, l, fc, :], func=AF.Copy),
                                       reads=[convhist.r[(l, fc)]], writes=abw)
                        ACT.op(lambda e: e.activation(out=ab.t[:, 2 + pc0:2 + pc0 + npr], in_=bank(gbk, 0, npr), func=AF.Copy),
                               reads=[bank_r[gbk]], writes=abw)
                        if tt == 2:
                            ACT.op(lambda e: e.activation(out=ASb.t[:, fc, :, 2:6], in_=bank(gbk, npr, TT).rearrange("p (s t) -> p s t", t=4), func=AF.Copy),
                                   reads=[bank_r[gbk]], writes=[ASb.r[(fc,)]])
                        cb = c0b[i % 3]
                        ACT.op(lambda e: e.activation(out=cb.t[:, :], in_=bank(gbk, 0, TT), func=AF.Identity,
                                                      bias=pvf.t[:, fc, 12 + l:13 + l], scale=pvf.t[:, fc, 3 * l + 2:3 * l + 3]),
                               reads=[bank_r[gbk]] + pvf.all, writes=cb.all)
                        c2 = c2b[i % 3]
                        DVE.op(lambda e: e.scalar_tensor_tensor(out=c2.t[:, 0:npr], in0=ab.t[:, 1 + pc0:1 + pc0 + npr], scalar=pvf.t[:, fc, 3 * l + 1:3 * l + 2],
                                                                in1=cb.t[:, 0:npr], op0=ALU.mult, op1=ALU.add),
                               reads=abr + cb.all + pvf.all, writes=c2.all)
                        DVE.op(lambda e: e.scalar_tensor_tensor(out=c2.t[:, 0:npr], in0=ab.t[:, pc0:pc0 + npr], scalar=pvf.t[:, fc, 3 * l:3 * l + 1],
                                                                in1=c2.t[:, 0:npr], op0=ALU.mult, op1=ALU.add),
                               reads=abr + c2.all + pvf.all, writes=c2.all)
                        if tt == 2:
                            v3 = lambda ap: ap.rearrange("p (s t) -> p s t", t=4)
                            DVE.op(lambda e: e.scalar_tensor_tensor(out=v3(c2.t[:, npr:TT]), in0=ASb.t[:, fc, :, 1:5], scalar=pvf.t[:, fc, 3 * l + 1:3 * l + 2],
                                                                    in1=v3(cb.t[:, npr:TT]), op0=ALU.mult, op1=ALU.add),
                                   reads=[ASb.r[(fc,)]] + cb.all + pvf.all, writes=c2.all)
                            DVE.op(lambda e: e.scalar_tensor_tensor(out=v3(c2.t[:, npr:TT]), in0=ASb.t[:, fc, :, 0:4], scalar=pvf.t[:, fc, 3 * l:3 * l + 1],
                                                                    in1=v3(c2.t[:, npr:TT]), op0=ALU.mult, op1=ALU.add),
                                   reads=[ASb.r[(fc,)]] + c2.all + pvf.all, writes=c2.all)
                            DVE.op(lambda e: e.tensor_copy(out=convhist.t[:, l, fc, :], in_=ab.t[:, NP:NP + 2]),
                                   reads=abw, writes=[convhist.r[(l, fc)]])

                        if tt == 2 and (fc % 4 == 3 or fc == FC - 1):
                            release(("Wg", g, l, fc // 4))
                            release(("Wv", g, l, fc // 4))

                    def stage2(i):
                        fc, tt = steps[i]
                        vbk = vbanks[i % 3]
                        cs = slice(tt * TT, (tt + 1) * TT)
                        c2 = c2b[i % 3]
                        gg = gb[i % 3]
                        ACT.op(lambda e: e.activation(out=gg.t[:, :], in_=c2.t[:, :], func=AF.Silu), reads=c2.all, writes=gg.all)
                        DVE.op(lambda e: e.tensor_tensor(out=mT.t[:, fc, cs], in0=gg.t[:, :], in1=bank(vbk, 0, TT), op=ALU.mult),
                               reads=gg.all + [bank_r[vbk]], writes=[mT.r[(fc, tt)]])
                    for i in range(len(steps)):
                        stage1(i)
                        if i > 0:
                            stage2(i - 1)
                    stage2(len(steps) - 1)
                    def conv_outputs():
                        DVE.op(lambda e: e.tensor_copy(out=cst.t[:, :, :].rearrange("p f (s r) -> p f s r", r=2), in_=ASb.t[:, :, :, 4:6]),
                               reads=ASb.all, writes=cst.all)
                        for ci, f0 in enumerate(range(0, FC, 8)):
                            nf = min(8, FC - f0)
                            bk = 5 + (ci % 2)
                            PE.op(lambda e: e.transpose(bank(bk, 0, 128)[0:nf * 16, :], cst.t[:, f0:f0 + nf, :].rearrange("p f c -> p (f c)"), ident.t[:]),
                                  reads=cst.all + ident.all, writes=[bank_r[bk]])
                            evac_copy(stg.t[0:nf * 16, 0, ci * 128:(ci + 1) * 128], bank(bk, 0, 128)[0:nf * 16, :], [bank_r[bk]], [stg_sub[ci]])
                            for fl in range(nf):
                                fc = f0 + fl
                                SP.op(lambda e: e.dma_start(out=conv_sample[l, sq0 * 2:(sq0 + 8) * 2, fc * 128:(fc + 1) * 128],
                                                            in_=stg.t[fl * 16:(fl + 1) * 16, 0, ci * 128:(ci + 1) * 128]),
                                      reads=[stg_sub[ci]], dsem=stg_st[0])
                        if g == ngroups - 1:
                            PE.op(lambda e: e.transpose(bank(7, 0, 128)[0:2 * FC, :], convhist.t[:, l, :, :].rearrange("p f c -> p (f c)"), ident.t[:]),
                                  reads=[convhist.r[(l, f)] for f in range(FC)] + ident.all, writes=[bank_r[7]])
                            evac_copy(stg.t[0:2 * FC, 1, 0:128], bank(7, 0, 128)[0:2 * FC, :], [bank_r[7]], [stg.r[(1,)]])
                            for fc in range(FC):
                                SP.op(lambda e: e.dma_start(out=conv_prompt[l, :, fc * 128:(fc + 1) * 128], in_=stg.t[2 * fc:2 * fc + 2, 1, 0:128]),
                                      reads=[stg.r[(1,)]], dsem=stg_st[1])
                        if do_mixer and (l + 1) in layers and (l + 1) % 2 == 1:
                            load_pool_state_dma((l + 1) // 2)
                    fp = FinalPhase()
                    for dc in range(KC):
                        u = ensure(("Wd", g, l, dc))
                        wv = u["view"]

                        def mm_emit(dc, tt, bk):
                            cs = slice(tt * TT, (tt + 1) * TT)

                            def mmd(e):
                                for fc in range(FC):
                                    i = e.matmul(bank(bk, 0, TT), lhsT=wv[:, fc, :], rhs=mT.t[:, fc, cs], start=(fc == 0), stop=(fc == FC - 1))
                                return i
                            PE.op(mmd, reads=u["res"] + [mT.r[(fc, tt)] for fc in range(FC)], writes=[bank_r[bk]])

                        def evac_emit(dc, tt, bk):
                            cs = slice(tt * TT, (tt + 1) * TT)
                            DVE.op(lambda e: e.tensor_tensor(out=xT.t[:, dc, cs], in0=bank(bk, 0, TT), in1=xT.t[:, dc, cs], op=ALU.add),
                                   reads=[bank_r[bk], xT.r[(dc, tt)]], writes=[xT.r[(dc, tt)]])
                        fp.chunk(dc, mm_emit, evac_emit, hooks=({0: conv_outputs} if dc == 1 else None))
                        release(("Wd", g, l, dc))

            norm_finish(8, HH.t, HH.r, dst_f32_off=15)
            nxt = g + 1 < ngroups
            if nxt:
                load_x_dma(g + 1, 0)
                load_x_dma(g + 1, 1)
            for c in range(9):
                store_y_chunk(g, c)
                if nxt:
                    load_x_tr(g + 1, c)
                store_y_dma(g, c)
                if nxt and c + 2 < 9:
                    load_x_dma(g + 1, c + 2)

        for d in dsems:
            if d.count > 0:
                SP.h.wait_ge(d.sem, d.count)
        nc._stats = {e.name: (e.nops, e.nwaits) for e in (PE, ACT, DVE, POOL, SP)}
    return nc


def _consts():
    ident = np.eye(128, dtype=np.float32)
    maskT = np.triu(np.ones((128, 128), dtype=np.float32))
    icnt = np.zeros((4, 16), dtype=np.float32)
    for gi, w in enumerate((2, 4, 8, 16)):
        for p in range(16):
            icnt[gi, p] = 1.0 / min(w, p + 1)
    icnt = np.broadcast_to(icnt.reshape(1, 64), (128, 64)).copy()
    return ident, maskT, icnt


def make_in_maps(inputs, n_cores=N_CORES):
    f = lambda a: np.ascontiguousarray(np.asarray(a, dtype=np.float32))
    ident, maskT, icnt = _consts()
    shared = {
        "norm_mix": f(inputs["norm_mix"]), "norm_ffn": f(inputs["norm_ffn"]),
        "norm_final": f(inputs["norm_final"]).reshape(1, D),
        "w_in_a": f(inputs["w_in_a"]), "g_v_a": f(inputs["g_v_a"]), "w_s_a": f(inputs["w_s_a"]),
        "b_s_a": f(inputs["b_s_a"]).reshape(2, 8 * 128), "w_out_a": f(inputs["w_out_a"]),
        "w_pool_b": f(inputs["w_pool_b"]).reshape(2, 1024, 256), "scale_b": f(inputs["scale_b"]),
        "w_gate": f(inputs["w_gate"]), "w_val": f(inputs["w_val"]),
        "conv_w": f(inputs["conv_w"]).reshape(DEPTH * 3, DFF), "conv_b": f(inputs["conv_b"]),
        "w_down": f(inputs["w_down"]),
        "c_ident": ident, "c_maskT": maskT, "c_icnt": icnt,
    }
    xp = f(inputs["x_prompt"]); xs = f(inputs["x_sample"])
    sp = f(inputs["state_pool"]); sf = f(inputs["state_ffn_conv"])
    maps = []
    for c in range(n_cores):
        m = dict(shared)
        m["x_prompt"] = xp[c]
        m["x_sample"] = np.ascontiguousarray(xs[c * NSEQ:(c + 1) * NSEQ].reshape(NSEQ * 4, D))
        m["state_pool"] = np.ascontiguousarray(sp[:, c * NSEQ:(c + 1) * NSEQ].reshape(2, NSEQ * 15, D))
        m["state_ffn_conv"] = np.ascontiguousarray(sf[:, c * NSEQ:(c + 1) * NSEQ].reshape(DEPTH, NSEQ * 2, DFF))
        maps.append(m)
    return maps


def assemble(results, n_cores=N_CORES):
    g = lambda k: [np.asarray(r[k], dtype=np.float32) for r in results]
    y_prompt = np.stack(g("y_prompt"), axis=0)
    y_sample = np.concatenate([a.reshape(NSEQ, 4, D) for a in g("y_sample")], axis=0)
    gv = np.concatenate([a.reshape(2, NSEQ, 4, DV) for a in g("gmlp_v_sample")], axis=1)
    pp = np.stack(g("pool_prompt"), axis=1)
    ps = np.concatenate([a.reshape(2, NSEQ, 15, D) for a in g("pool_sample")], axis=1)
    cp = np.stack(g("ffn_conv_prompt"), axis=1)
    cs = np.concatenate([a.reshape(DEPTH, NSEQ, 2, DFF) for a in g("ffn_conv_sample")], axis=1)
    return (y_prompt, y_sample, gv, pp, ps, cp, cs)


def kernel(**inputs):
    nc = build_program()
    maps = make_in_maps(inputs)
    res = run_bass_kernel_spmd(nc, maps, core_ids=list(range(N_CORES)))
    return assemble(res.results)
```
